# Optimizing a Trainium2 kernel written in Bass

```python
import jax, jax.numpy as jnp
from jax import lax
import numpy as np

D_MODEL = 2048
BATCH = 2
SEQ = 8192
DEPTH = 4

PLE_DIM = 256
EPS = 1e-6
MIX = D_MODEL

CHUNK = 128
A_HEAD = 128
A_WIDTH = MIX // 4
A_HEADS = A_WIDTH // A_HEAD

B_GROUP = 128
B_WIDTH = MIX // 4
B_GROUPS = B_WIDTH // B_GROUP
CONV_W = 3

C_WIDTH = MIX - A_WIDTH - B_WIDTH
C_V = 128
C_HEADS = C_WIDTH // C_V
C_NOPE = 128
C_ROPE = 64
KV_RANK = 512
ROPE_BASE = 10000.0
Q_BLOCK = 128

IN_SPLITS = (A_WIDTH, A_WIDTH, A_WIDTH,
             B_WIDTH, B_WIDTH, B_WIDTH, B_WIDTH,
             C_HEADS * (C_NOPE + C_ROPE), KV_RANK, C_ROPE, C_WIDTH)
IN_WIDTH = 3 * A_WIDTH + 4 * B_WIDTH + C_HEADS * (C_NOPE + C_ROPE) + KV_RANK + C_ROPE + C_WIDTH

kernel_name = 'hybrid_sgu_shortconv_mla_encoder'


def rms_norm(x, g):
    x32 = x.astype(jnp.float32)
    y = x32 * lax.rsqrt(jnp.mean(x32 * x32, axis=-1, keepdims=True) + EPS)
    return y.astype(x.dtype) * g


def rope_tables(positions):
    inv = 1.0 / (ROPE_BASE ** (jnp.arange(0, C_ROPE, 2, dtype=jnp.float32) / C_ROPE))
    ang = positions.astype(jnp.float32)[..., None] * inv
    return jnp.cos(ang), jnp.sin(ang)


def apply_rope(x, cos, sin):
    half = x.shape[-1] // 2
    x1, x2 = x[..., :half], x[..., half:]
    out = jnp.concatenate([x1 * cos - x2 * sin, x2 * cos + x1 * sin], axis=-1)
    return out.astype(x.dtype)


def spatial_gating(u, v, z, v_gain, w_s, b_s):
    bsz, s_len, _ = u.shape
    v = rms_norm(v.reshape(bsz, s_len, A_HEADS, A_HEAD), v_gain)
    vc = v.reshape(bsz, s_len // CHUNK, CHUNK, A_HEADS, A_HEAD)
    s = jnp.einsum('hnm,bkmhc->bknhc', w_s, vc) + b_s.T[None, None, :, :, None]
    return u * s.reshape(bsz, s_len, A_WIDTH) * jax.nn.silu(z)


def short_conv(gate_b, gate_c, h, z, conv_w, conv_b):
    s_len = h.shape[1]
    pad = CONV_W // 2
    xp = jnp.pad(gate_c * h, ((0, 0), (pad, pad), (0, 0)))
    y = conv_b + sum(xp[:, j:j + s_len] * conv_w[j] for j in range(CONV_W))
    return gate_b * y * jax.nn.silu(z)


def latent_attention(q, c_kv, k_rope, z, cos, sin, kv_gain, w_ukv, qn_g, qr_g, kn_g, kr_g):
    bsz, s_len, _ = q.shape
    q = q.reshape(bsz, s_len, C_HEADS, C_NOPE + C_ROPE)
    q_nope = rms_norm(q[..., :C_NOPE], qn_g)
    q_rope = apply_rope(rms_norm(q[..., C_NOPE:], qr_g), cos[:, :, None], sin[:, :, None])
    kv = (rms_norm(c_kv, kv_gain) @ w_ukv).reshape(bsz, s_len, C_HEADS, C_NOPE + C_V)
    k_nope = rms_norm(kv[..., :C_NOPE], kn_g)
    v = kv[..., C_NOPE:]
    k_r = apply_rope(rms_norm(k_rope, kr_g), cos, sin)
    scale = (C_NOPE + C_ROPE) ** -0.5
    n_blk = s_len // Q_BLOCK

    def to_blocks(t):
        return jnp.moveaxis(t.reshape(bsz, n_blk, Q_BLOCK, *t.shape[2:]), 1, 0)

    def attend(blk):
        qn, qr = blk
        s = jnp.einsum('bqhd,bkhd->bhqk', qn, k_nope) + jnp.einsum('bqhr,bkr->bhqk', qr, k_r)
        w = jax.nn.softmax(s.astype(jnp.float32) * scale, axis=-1).astype(v.dtype)
        return jnp.einsum('bhqk,bkhd->bqhd', w, v)

    o = lax.map(attend, (to_blocks(q_nope), to_blocks(q_rope)))
    o = jnp.moveaxis(o, 0, 1).reshape(bsz, s_len, C_WIDTH)
    return o * jax.nn.silu(z)


def setup_inputs(seed: int = 0) -> dict:
    key = jax.random.key(seed)
    ks = jax.random.split(key, 24)
    f32 = jnp.float32

    def nrm(k, shape, scale):
        return jax.random.normal(k, shape, f32) * scale

    def gain(k, shape):
        return 1.0 + 0.01 * jax.random.normal(k, shape, f32)

    return {
        'x': jax.random.normal(ks[0], (BATCH, SEQ, D_MODEL), f32),
        'p': jax.random.normal(ks[1], (DEPTH, BATCH, SEQ, PLE_DIM), f32),
        'positions': jnp.broadcast_to(jnp.arange(SEQ, dtype=jnp.int32), (BATCH, SEQ)),
        'attn_norm': gain(ks[2], (DEPTH, D_MODEL)),
        'w_in': nrm(ks[3], (DEPTH, D_MODEL, IN_WIDTH), D_MODEL ** -0.5),
        'sgu_norm': gain(ks[4], (DEPTH, A_HEADS, A_HEAD)),
        'w_spatial': nrm(ks[5], (DEPTH, A_HEADS, CHUNK, CHUNK), CHUNK ** -0.5),
        'b_spatial': gain(ks[6], (DEPTH, A_HEADS, CHUNK)),
        'conv_w': nrm(ks[7], (DEPTH, CONV_W, B_WIDTH), CONV_W ** -0.5),
        'conv_b': nrm(ks[8], (DEPTH, B_WIDTH), 0.01),
        'kv_norm': gain(ks[9], (DEPTH, KV_RANK)),
        'w_ukv': nrm(ks[10], (DEPTH, KV_RANK, C_HEADS * (C_NOPE + C_V)), KV_RANK ** -0.5),
        'q_nope_norm': gain(ks[11], (DEPTH, C_NOPE)),
        'q_rope_norm': gain(ks[12], (DEPTH, C_ROPE)),
        'k_nope_norm': gain(ks[13], (DEPTH, C_NOPE)),
        'k_rope_norm': gain(ks[14], (DEPTH, C_ROPE)),
        'out_norm': gain(ks[15], (DEPTH, MIX)),
        'w_out': nrm(ks[16], (DEPTH, MIX, D_MODEL), MIX ** -0.5),
        'ple_norm': gain(ks[17], (DEPTH, D_MODEL)),
        'w_ple_gate': nrm(ks[18], (DEPTH, D_MODEL, D_MODEL), D_MODEL ** -0.5),
        'w_ple_proj': nrm(ks[19], (DEPTH, PLE_DIM, D_MODEL), PLE_DIM ** -0.5),
    }


def reference(x, p, positions, attn_norm, w_in, sgu_norm, w_spatial, b_spatial, conv_w, conv_b,
              kv_norm, w_ukv, q_nope_norm, q_rope_norm, k_nope_norm, k_rope_norm,
              out_norm, w_out, ple_norm, w_ple_gate, w_ple_proj):
    cos, sin = rope_tables(positions)
    split_pts = [int(c) for c in np.cumsum(IN_SPLITS)[:-1]]
    out_pts = [A_WIDTH, A_WIDTH + B_WIDTH]
    h = x
    for i in range(DEPTH):
        hn = rms_norm(h, attn_norm[i])
        proj = hn @ w_in[i]
        (a_u, a_v, a_z, b_b, b_c, b_h, b_z, c_q, c_kv, c_kr, c_z) = jnp.split(proj, split_pts, axis=-1)
        y_a = spatial_gating(a_u, a_v, a_z, sgu_norm[i], w_spatial[i], b_spatial[i])
        y_b = short_conv(b_b, b_c, b_h, b_z, conv_w[i], conv_b[i])
        y_c = latent_attention(c_q, c_kv, c_kr, c_z, cos, sin, kv_norm[i], w_ukv[i],
                               q_nope_norm[i], q_rope_norm[i], k_nope_norm[i], k_rope_norm[i])
        g_a, g_b, g_c = jnp.split(out_norm[i], out_pts)
        y = jnp.concatenate([rms_norm(y_a, g_a), rms_norm(y_b, g_b), rms_norm(y_c, g_c)], axis=-1)
        h = h + y @ w_out[i]
        gate = jax.nn.sigmoid(rms_norm(h, ple_norm[i]) @ w_ple_gate[i])
        h = h + gate * (p[i] @ w_ple_proj[i])
    return h
```

```python
import contextlib
import numpy as np
import concourse.bass as bass
import concourse.mybir as mybir
from concourse.bass_utils import run_bass_kernel_spmd

F32 = mybir.dt.float32
BF16 = mybir.dt.bfloat16
I32 = mybir.dt.int32
AF = mybir.ActivationFunctionType
ALU = mybir.AluOpType
AX = mybir.AxisListType

D = 2048
INW = 6720
NPROJ = 5696
EPS = 1e-6
PI = float(np.pi)
SC = float(192 ** -0.5)
V_OA, V_OB, V_SGU, V_CW0, V_CW1, V_CW2, V_CB, V_KV, V_QN, V_QR, V_KN, V_KR, NV = (
    0, 512, 1024, 1536, 2048, 2560, 3072, 3584, 4096, 4224, 4288, 4416, 4480)


class Ins:
    __slots__ = ("idx", "eng", "fn", "waits", "stream", "inc", "signal", "sigval")


class Prog:
    ENGS = ("pe", "act", "dve", "pool", "sp")

    def __init__(self):
        self.ins = []
        self.last_w = {}
        self.readers = {}
        self.stream_last = {}
        self.stream_inc = {}
        self.eng_last = {e: None for e in self.ENGS}
        self.waited = {e: {} for e in self.ENGS}
        self.pending = {e: [] for e in self.ENGS}

    def _key(self, p):
        return ("s", p.stream) if p.stream is not None else ("e", p.eng)

    def op(self, eng, fn, r=(), w=(), stream=None, inc=16, extra=()):
        i = Ins()
        i.idx = len(self.ins); i.eng = eng; i.fn = fn; i.stream = stream; i.inc = inc
        i.signal = False; i.sigval = 0
        deps = set(extra)
        for x in r:
            if x in self.last_w:
                deps.add(self.last_w[x])
        for x in w:
            if x in self.last_w:
                deps.add(self.last_w[x])
            rd = self.readers.get(x)
            if rd:
                deps.update(rd.values())
        deps.update(self.pending[eng]); self.pending[eng] = []
        me_key = ("s", stream) if stream is not None else ("e", eng)
        for x in r:
            self.readers.setdefault(x, {})[me_key] = i.idx
        for x in w:
            self.last_w[x] = i.idx
            self.readers[x] = {}
        waits = []
        wd = self.waited[eng]
        best = {}
        for d in deps:
            k = self._key(self.ins[d])
            if k == ("e", "pe") and eng == "pe" and stream is None:
                continue
            if best.get(k, -1) < d:
                best[k] = d
        for k, d in sorted(best.items(), key=lambda kv: kv[1]):
            if wd.get(k, -1) >= d:
                continue
            wd[k] = d
            self.ins[d].signal = True
            waits.append(d)
        i.waits = waits
        self.ins.append(i)
        if stream is not None:
            self.stream_last[stream] = i.idx
            self.stream_inc[stream] = inc
        self.eng_last[eng] = i.idx
        return i.idx

    def barrier(self, fn):
        extra = [v for v in self.eng_last.values() if v is not None]
        extra += [v for s, v in self.stream_last.items() if not s.startswith("cc")]
        b = self.op("dve", fn, extra=extra)
        for e in self.ENGS:
            if e != "dve":
                self.pending[e].append(b)
        return b

    def emit(self, nc, sem_of_eng, sem_of_stream):
        cnt = {}
        for i in self.ins:
            k = self._key(i)
            if i.stream is not None:
                cnt[k] = cnt.get(k, 0) + i.inc
                i.sigval = cnt[k]
            elif i.signal:
                cnt[k] = cnt.get(k, 0) + 1
                i.sigval = cnt[k]
        self.final_counts = cnt

        def semof(p):
            return sem_of_stream[p.stream] if p.stream is not None else sem_of_eng[p.eng]

        def run(engname, e):
            for i in self.ins:
                if i.eng != engname:
                    continue
                for d in i.waits:
                    p = self.ins[d]
                    e.wait_ge(semof(p), p.sigval)
                inst = i.fn(e)
                if i.stream is not None:
                    inst.then_inc(sem_of_stream[i.stream], i.inc)
                elif i.signal:
                    inst.then_inc(sem_of_eng[i.eng], 1)
            if engname == "sp":
                for s, last in self.stream_last.items():
                    e.wait_ge(sem_of_stream[s], self.ins[last].sigval)

        with nc.Block() as block:
            @block.tensor
            def _(e):
                run("pe", e)

            @block.scalar
            def _(e):
                run("act", e)

            @block.vector
            def _(e):
                run("dve", e)

            @block.gpsimd
            def _(e):
                run("pool", e)

            @block.sync
            def _(e):
                run("sp", e)


class Arena:
    def __init__(self, ap_f32, nf32):
        self.ap = ap_f32; self.n = nf32; self.off = 0

    def reset(self):
        self.off = 0

    def f32(self, n):
        assert self.off + n <= self.n, ("arena overflow", self.off, n, self.n)
        a = self.ap[:, self.off:self.off + n]
        self.off += n
        return a

    def bf16(self, n):
        m = (n + 1) // 2
        return self.f32(m).bitcast(BF16)[:, 0:n]


def build(L, TOK):
    NT = TOK // 128
    NB = TOK // 512
    SEQ = 4 * TOK
    ROWS = 8 * 128 + 8 * 128 + 64 + 2
    K_OFF, V_OFF, KR_OFF, HALO = 0, 1024, 2048, 2112
    CH = max(128, ((1 << 20) // (TOK * 2)) // 128 * 128)
    NCH = (ROWS + CH - 1) // CH
    GW = TOK
    VC = TOK // 128
    nc = bass.Bass("TRN2", target_bir_lowering=False)

    def din(name, shape, dt=F32):
        return nc.dram_tensor(name, list(shape), dt, kind="ExternalInput").ap()

    x_d = din("x", [TOK, D])
    p_d = din("p", [L, TOK, 256])
    pos_d = din("pos", [128, NT], I32)
    win_d = din("w_in", [L, D, INW])
    wukv_d = din("w_ukv", [L, 512, 2048])
    wout_d = din("w_out", [L, D, D])
    wgate_d = din("w_gate", [L, D, D])
    wpp_d = din("w_pp", [L, 256, D])
    wsp_d = din("w_spT", [L, 128, 512])
    vec_d = din("vec", [L, NV])
    gn_d = din("gn", [L, 2, D])
    cols_d = din("cols", [L, 128, 12])
    sel_d = din("sel", [8, 256])
    cmat_d = din("cmat", [128, 640])
    invf_d = din("invf", [128, 32])
    out_d = nc.dram_tensor("out", [TOK, D], F32, kind="ExternalOutput").ap()
    proj_d = nc.dram_tensor("proj", [TOK, NPROJ], F32).ap()
    zct_d = nc.dram_tensor("zct", [1024, TOK], F32).ap()
    hbuf_d = nc.dram_tensor("hbuf", [TOK, D], F32).ap()
    chn = [min(CH, ROWS - k * CH) for k in range(NCH)]
    gin_ts = [nc.dram_tensor("gin%d" % k, [chn[k], GW], BF16) for k in range(NCH)]
    gout_ts = [nc.dram_tensor("gout%d" % k, [4 * chn[k], GW], BF16) for k in range(NCH)]

    def gin_rows(r0, n):
        k = r0 // CH
        assert (r0 + n - 1) // CH == k
        return gin_ts[k].ap()[r0 - k * CH:r0 - k * CH + n, :]

    def gout_rows(rk, r0, n):
        k = r0 // CH
        assert (r0 + n - 1) // CH == k
        o = rk * chn[k] + r0 - k * CH
        return gout_ts[k].ap()[o:o + n, :]

    def gres(r0):
        return "gout.%d" % (r0 // CH)

    P = Prog()
    es = contextlib.ExitStack()
    with es:
        def sb(name, shape, dt):
            return es.enter_context(nc.sbuf_tensor("s_" + name, list(shape), dt))

        A_t = sb("A", [128, 16, TOK], BF16)
        PHN = 27136
        PH_t = sb("PH", [128, PHN], F32)
        vec_t = sb("vec", [128, NV], F32)
        cols_t = sb("cols", [128, 12], F32)
        ident = sb("ident", [128, 128], BF16)
        ones_b = sb("ones_b", [128, 128], BF16)
        ones_f = sb("ones_f", [128, 128], F32)
        shm = sb("shm", [128, 4, 128], BF16)
        sel_b = sb("sel_b", [8, 256], BF16)
        cosT = sb("cosT", [128, NT, 32], F32)
        sinT = sb("sinT", [128, NT, 32], F32)
        invf = sb("invf", [128, 32], F32)
        posi = sb("posi", [128, NT], I32)
        neghalf = sb("neghalf", [128, 1], F32)
        dummy = sb("dummy", [128, 8], F32)
        wsp_b = sb("wsp_b", [128, 4, 128], BF16)
        G_b = sb("G_b", [8, 512], BF16)
        ps = [es.enter_context(nc.psum_tensor("ps%d" % b, [128, 512], F32)) for b in range(8)]
        A = A_t[:]
        ph = Arena(PH_t[:], PHN)

        def psb(b):
            return ps[b][:].bitcast(BF16)

        def dma(q, out, in_, r, w, stream):
            P.op(q, lambda e, out=out, in_=in_: e.dma_start(out=out, in_=in_), r=r, w=w, stream=stream)

        def act(out, in_, func, r, w, scale=None, accum=None, bias=None):
            kw = {}
            if scale is not None:
                kw["scale"] = scale
            if accum is not None:
                kw["accum_out"] = accum
            if bias is not None:
                kw["bias"] = bias
            P.op("act", lambda e, out=out, in_=in_, func=func, kw=kw: e.activation(out=out, in_=in_, func=func, **kw), r=r, w=w)

        def tt(eng, out, in0, in1, op, r, w):
            P.op(eng, lambda e, out=out, in0=in0, in1=in1, op=op: e.tensor_tensor(out=out, in0=in0, in1=in1, op=op), r=r, w=w)

        def ts(eng, out, in0, s1, s2, op0, op1, r, w):
            if op1 is None:
                P.op(eng, lambda e, out=out, in0=in0, s1=s1, op0=op0: e.tensor_scalar(out=out, in0=in0, scalar1=s1, scalar2=None, op0=op0), r=r, w=w)
            else:
                P.op(eng, lambda e, out=out, in0=in0, s1=s1, s2=s2, op0=op0, op1=op1: e.tensor_scalar(out=out, in0=in0, scalar1=s1, scalar2=s2, op0=op0, op1=op1), r=r, w=w)

        def stt(out, in0, scalar, in1, op0, op1, r, w):
            P.op("dve", lambda e, out=out, in0=in0, scalar=scalar, in1=in1, op0=op0, op1=op1:
                 e.scalar_tensor_tensor(out=out, in0=in0, scalar=scalar, in1=in1, op0=op0, op1=op1), r=r, w=w)

        def cp(eng, out, in_, r, w):
            if eng == "act":
                P.op("act", lambda e, out=out, in_=in_: e.copy(out=out, in_=in_), r=r, w=w)
            else:
                P.op(eng, lambda e, out=out, in_=in_: e.tensor_copy(out=out, in_=in_), r=r, w=w)

        def mm(out, lhsT, rhs, start, stop, r, w):
            P.op("pe", lambda e, out=out, lhsT=lhsT, rhs=rhs, start=start, stop=stop:
                 e.matmul(out, lhsT=lhsT, rhs=rhs, start=start, stop=stop), r=r, w=w)

        def tr(out, in_, r, w, idn=None):
            idn = ident[:] if idn is None else idn
            P.op("pe", lambda e, out=out, in_=in_, idn=idn: e.transpose(out=out, in_=in_, identity=idn), r=r, w=w)

        def red(out, in_, r, w):
            P.op("dve", lambda e, out=out, in_=in_: e.tensor_reduce(out=out, in_=in_, axis=AX.X, op=ALU.add), r=r, w=w)

        def rstd_of(ssq, n, rname, width, tmp, out):
            ts("dve", tmp[0], ssq[0], 1.0 / n, EPS, ALU.mult, ALU.add, r=[ssq[1]], w=[tmp[1]])
            tt("pool", out[0], tmp[0], neghalf[:, 0:1].to_broadcast([128, width]), ALU.pow, r=[tmp[1]], w=[out[1]])

        def barrier():
            P.barrier(lambda e: e.memset(dummy[:, 0:1], 0.0))
            ph.reset()

        evac_flip = [0]

        def evac(out, in_, r, w):
            evac_flip[0] ^= 1
            cp("act" if evac_flip[0] else "dve", out, in_, r, w)

        cm_f = ph.f32(640)
        dma("sp", cm_f, cmat_d[:, :], [], ["cm_f"], "i0")
        cp("dve", ident[:], cm_f[:, 0:128], ["cm_f"], ["ident"])
        cp("dve", shm[:].rearrange("p a b -> p (a b)"), cm_f[:, 128:640], ["cm_f"], ["shm"])
        P.op("pool", lambda e: e.memset(ones_b[:], 1.0), w=["ones_b"])
        P.op("pool", lambda e: e.memset(ones_f[:], 1.0), w=["ones_f"])
        P.op("pool", lambda e: e.memset(neghalf[:], -0.5), w=["neghalf"])
        sel_f = ph.f32(256)
        dma("sp", sel_f[0:8, :], sel_d[:, :], [], ["sel_f"], "i1")
        cp("dve", sel_b[:], sel_f[0:8, :], ["sel_f"], ["sel_b"])
        dma("sp", invf[:], invf_d[:, :], [], ["invf"], "i2")
        dma("sp", posi[:], pos_d[:, :], [], ["posi"], "i3")
        posf = ph.f32(NT)
        cp("dve", posf, posi[:], ["posi"], ["posf"])
        ang = ph.f32(NT * 32).rearrange("p (a b) -> p a b", b=32)
        tt("dve", ang, posf.unsqueeze(2).to_broadcast([128, NT, 32]),
           invf[:].unsqueeze(1).to_broadcast([128, NT, 32]), ALU.mult, ["posf", "invf"], ["ang"])
        uu = ph.f32(NT * 32).rearrange("p (a b) -> p a b", b=32)
        ki = ph.f32(NT * 32).bitcast(I32).rearrange("p (a b) -> p a b", b=32)
        kf = ph.f32(NT * 32).rearrange("p (a b) -> p a b", b=32)
        rr = ph.f32(NT * 32).rearrange("p (a b) -> p a b", b=32)
        mk = ph.f32(NT * 32).rearrange("p (a b) -> p a b", b=32)
        C1 = 6.28125
        C2 = float(2 * np.pi - 6.28125)
        for which, dst in ((0, sinT), (1, cosT)):
            if which == 1:
                ts("dve", ang, ang, PI / 2, None, ALU.add, None, ["ang"], ["ang"])
            ts("dve", uu, ang, 1.0 / (2 * PI), None, ALU.mult, None, ["ang"], ["uu"])
            cp("dve", ki, uu, ["uu"], ["ki"])
            cp("dve", kf, ki, ["ki"], ["kf"])
            stt(rr, kf, -C1, ang, ALU.mult, ALU.add, ["kf", "ang"], ["rr"])
            stt(rr, kf, -C2, rr, ALU.mult, ALU.add, ["kf", "rr"], ["rr"])
            ts("dve", mk, rr, -PI, None, ALU.is_lt, None, ["rr"], ["mk"])
            stt(rr, mk, 2 * PI, rr, ALU.mult, ALU.add, ["mk", "rr"], ["rr"])
            ts("dve", mk, rr, PI, None, ALU.is_gt, None, ["rr"], ["mk"])
            stt(rr, mk, -2 * PI, rr, ALU.mult, ALU.add, ["mk", "rr"], ["rr"])
            ts("dve", rr, rr, PI, -PI, ALU.min, ALU.max, ["rr"], ["rr"])
            act(dst[:], rr, AF.Sin, ["rr"], ["tab%d" % which])
        barrier()

        def norm_transpose(l, src_d, gidx):
            gnorm = ph.f32(D)
            dma("sp", gnorm, gn_d[l, gidx, :].partition_broadcast(128), [], ["gnorm"], "lgn")
            hts = [ph.f32(D) for _ in range(2)]
            hns = [ph.bf16(D) for _ in range(2)]
            junk = ph.bf16(D)
            st = ph.f32(8)
            for t in range(NT):
                b = t % 2
                dma("sp", hts[b], src_d[t * 128:(t + 1) * 128, :], [], ["ht%d" % b], "ldh%d" % b)
                act(junk, hts[b], AF.Square, ["ht%d" % b], ["junk", "ssq%d" % b], accum=st[:, b:b + 1])
                rstd_of((st[:, b:b + 1], "ssq%d" % b), D, None, 1, (st[:, 2 + b:3 + b], "tq%d" % b), (st[:, 4 + b:5 + b], "rs%d" % b))
                stt(hns[b], hts[b], st[:, 4 + b:5 + b], gnorm, ALU.mult, ALU.mult, ["ht%d" % b, "rs%d" % b, "gnorm"], ["hn%d" % b])
                for kg in range(4):
                    bank = 4 + (kg % 2)
                    for j in range(4):
                        kc = kg * 4 + j
                        tr(psb(bank)[:, j * 128:(j + 1) * 128], hns[b][:, kc * 128:(kc + 1) * 128], ["hn%d" % b, "ident"], ["ps%d" % bank])
                    evac(A[:, kg * 4:kg * 4 + 4, t * 128:(t + 1) * 128],
                         psb(bank)[:, 0:512].rearrange("p (a b) -> p a b", b=128), ["ps%d" % bank], ["A.%d.%d" % (t, kg)])

        gemm_bank = [0]

        def gemm(w_src, KC, ncols, wslot, wres, actT, act_res, epilogue, yform=False):
            P.op("pool", lambda e: e.dma_start(out=wslot[:, 0:KC, 0:ncols], in_=w_src.rearrange("(kc p) n -> p kc n", p=128)),
                 r=[], w=[wres], stream="w." + wres)
            if not yform:
                for t in range(NT):
                    bank = gemm_bank[0]; gemm_bank[0] = (gemm_bank[0] + 1) % 4
                    for kc in range(KC):
                        mm(ps[bank][:, 0:ncols], actT[:, kc, t * 128:(t + 1) * 128], wslot[:, kc, 0:ncols],
                           kc == 0, kc == KC - 1, [wres] + act_res(t), ["ps%d" % bank])
                    epilogue(t, ps[bank][:, 0:ncols], "ps%d" % bank)
            else:
                for fc in range(ncols // 128):
                    for qb in range(NB):
                        bank = gemm_bank[0]; gemm_bank[0] = (gemm_bank[0] + 1) % 4
                        rs = []
                        for t in range(4 * qb, 4 * qb + 4):
                            rs += act_res(t)
                        for kc in range(KC):
                            mm(ps[bank][:, :], wslot[:, kc, fc * 128:(fc + 1) * 128], actT[:, kc, qb * 512:(qb + 1) * 512],
                               kc == 0, kc == KC - 1, [wres] + rs, ["ps%d" % bank])
                        epilogue((fc, qb), ps[bank][:, :], "ps%d" % bank)

        def a_res(t):
            return ["A.%d.%d" % (t, kg) for kg in range(4)]

        KSTOP = 99
        for l in range(L):
            if KSTOP <= 0:
                break
            hsrc = x_d if l == 0 else hbuf_d
            hdst = out_d if l == L - 1 else hbuf_d
            dma("sp", vec_t[:], vec_d[l, :].partition_broadcast(128), [], ["vec"], "lvec")
            dma("sp", cols_t[:], cols_d[l, :, :], [], ["cols"], "lcols")
            P.op("pool", lambda e, l=l: e.dma_start(out=wsp_b[:].rearrange("p a b -> p (a b)"), in_=wsp_d[l, :, :]), r=[], w=["wsp_b"], stream="w.wsp")

            norm_transpose(l, hsrc, 0)
            barrier()

            if KSTOP <= 1:
                break
            wslots = [ph.bf16(16 * 512).rearrange("p (a b) -> p a b", b=512) for _ in range(3)]
            stg = [ph.f32(512) for _ in range(4)]
            stg_i = [0]
            groups = [(5120, 512, "plain"), (5632, 64, "plain")]
            groups += [(c0, 512, "plain") for c0 in (2048, 2560)]
            groups += [(0, 512, "plain"), (512, 512, "plain"), (1024, 512, "silu"), (1536, 512, "plain"),
                       (3072, 512, "silu"), (3584, 512, "plain"), (4096, 512, "plain"), (4608, 512, "plain"),
                       (5696, 512, "zc"), (6208, 512, "zc")]
            for gi, (c0, ncols, kind) in enumerate(groups):
                slot = gi % 3
                wres = "wslot%d" % slot

                def epi(t, pap, pres, c0=c0, ncols=ncols, kind=kind):
                    s = stg_i[0]; stg_i[0] = (s + 1) % 4
                    sres = "stg%d" % s
                    if kind == "plain":
                        evac(stg[s][:, 0:ncols], pap, [pres], [sres])
                        dma("sp", proj_d[t * 128:(t + 1) * 128, c0:c0 + ncols], stg[s][:, 0:ncols], [sres], [], "st%d" % s)
                    elif kind == "silu":
                        act(stg[s][:, 0:ncols], pap, AF.Silu, [pres], [sres])
                        dma("sp", proj_d[t * 128:(t + 1) * 128, c0:c0 + ncols], stg[s][:, 0:ncols], [sres], [], "st%d" % s)
                    else:
                        fc, qb = t
                        act(stg[s][:, :], pap, AF.Silu, [pres], [sres])
                        r0 = (c0 - 5696) + fc * 128
                        dma("sp", zct_d[r0:r0 + 128, qb * 512:(qb + 1) * 512], stg[s][:, :], [sres], [], "st%d" % s)

                gemm(win_d[l, :, c0:c0 + ncols], 16, ncols, wslots[slot], wres, A, a_res, epi, yform=(kind == "zc"))
            barrier()

            if KSTOP <= 2:
                break
            rowres = {}
            for h in range(8):
                rowres.setdefault((K_OFF + h * 128) // CH, []).append("gin.k%d" % h)
                rowres.setdefault((V_OFF + h * 128) // CH, []).append("gin.v%d" % h)
            rowres.setdefault(KR_OFF // CH, []).append("gin.kr")
            rowres.setdefault(HALO // CH, []).append("gin.h0")
            rowres.setdefault((HALO + 1) // CH, []).append("gin.h1")
            cc_state = {"written": set(), "issued": set(), "pending": [], "tick": 0, "last": -100}

            def cc_written(res):
                cc_state["written"].add(res)
                for k in range(NCH):
                    if k not in cc_state["issued"] and k not in cc_state["pending"] and all(x in cc_state["written"] for x in rowres[k]):
                        cc_state["pending"].append(k)

            def cc_issue(k):
                cc_state["issued"].add(k)
                P.op("pool", lambda e, k=k: e.collective_compute("AllGather", ALU.bypass, replica_groups=[[0, 1, 2, 3], [4, 5, 6, 7]],
                                                                ins=[gin_ts[k].ap()], outs=[gout_ts[k].ap()]),
                     r=rowres[k], w=["gout.%d" % k, "ccserial"], stream="cc%d" % k, inc=1)

            def cc_tick(spacing):
                cc_state["tick"] += 1
                if cc_state["pending"] and cc_state["tick"] - cc_state["last"] >= spacing:
                    cc_state["last"] = cc_state["tick"]
                    cc_issue(cc_state["pending"].pop(0))

            def cc_flush_ready():
                pass

            def cc_flush_all():
                while cc_state["pending"]:
                    cc_issue(cc_state["pending"].pop(0))
                assert len(cc_state["issued"]) == NCH

            ckvT = ph.bf16(4 * TOK).rearrange("p (a b) -> p a b", b=TOK)
            krTs = ph.bf16(TOK)
            ck = [ph.f32(576) for _ in range(2)]
            ckn = [ph.bf16(512) for _ in range(2)]
            junk = ph.bf16(512)
            st = ph.f32(16)
            krn = [ph.f32(64) for _ in range(2)]
            rt = [ph.f32(32) for _ in range(4)]
            krr = [ph.bf16(64) for _ in range(2)]
            hb = [ph.f32(1024) for _ in range(2)]
            hx = [ph.bf16(512) for _ in range(2)]
            for i, t in enumerate((0, NT - 1)):
                dma("sp", hb[i], proj_d[t * 128:(t + 1) * 128, 2048:3072], [], ["hb%d" % i], "lhb%d" % i)
                tt("pool", hx[i], hb[i][:, 0:512], hb[i][:, 512:1024], ALU.mult, ["hb%d" % i], ["hx%d" % i])
            dma("sp", gin_rows(HALO, 1)[:, 0:512], hx[0][0:1, :], ["hx0"], ["gin.h0"], "st0")
            dma("sp", gin_rows(HALO + 1, 1)[:, 0:512], hx[1][127:128, :], ["hx1"], ["gin.h1"], "st1")
            cc_written("gin.h0"); cc_written("gin.h1")
            for t in range(NT):
                b = t % 2
                dma("sp", ck[b], proj_d[t * 128:(t + 1) * 128, 5120:5696], [], ["ck%d" % b], "ldh%d" % b)
                act(junk, ck[b][:, 0:512], AF.Square, ["ck%d" % b], ["junk", "sq%d" % b], accum=st[:, b:b + 1])
                act(junk[:, 0:64], ck[b][:, 512:576], AF.Square, ["ck%d" % b], ["junk", "sr%d" % b], accum=st[:, 2 + b:3 + b])
                rstd_of((st[:, b:b + 1], "sq%d" % b), 512, None, 1, (st[:, 4 + b:5 + b], "t1%d" % b), (st[:, 6 + b:7 + b], "r1%d" % b))
                rstd_of((st[:, 2 + b:3 + b], "sr%d" % b), 64, None, 1, (st[:, 8 + b:9 + b], "t2%d" % b), (st[:, 10 + b:11 + b], "r2%d" % b))
                stt(ckn[b], ck[b][:, 0:512], st[:, 6 + b:7 + b], vec_t[:, V_KV:V_KV + 512], ALU.mult, ALU.mult,
                    ["ck%d" % b, "r1%d" % b, "vec"], ["ckn%d" % b])
                stt(krn[b], ck[b][:, 512:576], st[:, 10 + b:11 + b], vec_t[:, V_KR:V_KR + 64], ALU.mult, ALU.mult,
                    ["ck%d" % b, "r2%d" % b, "vec"], ["krn%d" % b])
                x1 = krn[b][:, 0:32]; x2 = krn[b][:, 32:64]
                Ct = cosT[:, t, :]; St = sinT[:, t, :]
                tt("pool", rt[0], x1, Ct, ALU.mult, ["krn%d" % b, "tab1"], ["rt0"])
                tt("pool", rt[1], x2, St, ALU.mult, ["krn%d" % b, "tab0"], ["rt1"])
                tt("pool", krr[b][:, 0:32], rt[0], rt[1], ALU.subtract, ["rt0", "rt1"], ["krr%d" % b])
                tt("pool", rt[2], x2, Ct, ALU.mult, ["krn%d" % b, "tab1"], ["rt2"])
                tt("pool", rt[3], x1, St, ALU.mult, ["krn%d" % b, "tab0"], ["rt3"])
                tt("pool", krr[b][:, 32:64], rt[2], rt[3], ALU.add, ["rt2", "rt3", "krr%d" % b], ["krr%d" % b])
                for j in range(4):
                    tr(psb(4)[:, j * 128:(j + 1) * 128], ckn[b][:, j * 128:(j + 1) * 128], ["ckn%d" % b, "ident"], ["ps4"])
                evac(ckvT[:, :, t * 128:(t + 1) * 128], psb(4)[:, 0:512].rearrange("p (a b) -> p a b", b=128), ["ps4"], ["ckvT.%d" % t])
                tr(psb(5)[0:64, 0:128], krr[b][:, 0:64], ["krr%d" % b, "ident"], ["ps5"])
                evac(krTs[0:64, t * 128:(t + 1) * 128], psb(5)[0:64, 0:128], ["ps5"], ["krTs"])
            dma("sp", gin_rows(KR_OFF, 64), krTs[0:64, :], ["krTs"], ["gin.kr"], "st2")
            cc_written("gin.kr")
            wuk = [ph.bf16(4 * 512).rearrange("p (a b) -> p a b", b=512) for _ in range(2)]
            kst = [ph.bf16(2 * TOK).rearrange("p (a b) -> p a b", b=TOK) for _ in range(2)]
            vst = [ph.bf16(2 * TOK).rearrange("p (a b) -> p a b", b=TOK) for _ in range(2)]
            knb = [ph.bf16(128) for _ in range(4)]
            kq = ph.f32(12)
            kn_i = [0]
            for g in range(4):
                sl = g % 2

                def epi(t, pap, pres, g=g, sl=sl):
                    cc_tick(10)
                    for hh in range(2):
                        i = kn_i[0]; kn_i[0] = (i + 1) % 4
                        kps = pap[:, hh * 256:hh * 256 + 128]
                        vps = pap[:, hh * 256 + 128:hh * 256 + 256]
                        act(junk[:, 0:128], kps, AF.Square, [pres], ["junk", "ks%d" % i], accum=kq[:, i:i + 1])
                        rstd_of((kq[:, i:i + 1], "ks%d" % i), 128, None, 1, (kq[:, 4 + i:5 + i], "kt%d" % i), (kq[:, 8 + i:9 + i], "kr%d" % i))
                        stt(knb[i], kps, kq[:, 8 + i:9 + i], vec_t[:, V_KN:V_KN + 128], ALU.mult, ALU.mult,
                            [pres, "kr%d" % i, "vec"], ["knb%d" % i])
                        cp("act", vst[sl][:, hh, t * 128:(t + 1) * 128], vps, [pres], ["vst%d" % sl])
                        bank = 4 + i
                        tr(psb(bank)[:, 0:128], knb[i], ["knb%d" % i, "ident"], ["ps%d" % bank])
                        evac(kst[sl][:, hh, t * 128:(t + 1) * 128], psb(bank)[:, 0:128], ["ps%d" % bank], ["kst%d" % sl])

                gemm(wukv_d[l, :, g * 512:(g + 1) * 512], 4, 512, wuk[sl], "wuk%d" % sl, ckvT, lambda t: ["ckvT.%d" % t], epi)
                for hh in range(2):
                    h = 2 * g + hh
                    dma("sp", gin_rows(K_OFF + h * 128, 128), kst[sl][:, hh, :], ["kst%d" % sl], ["gin.k%d" % h], "sk%d" % sl)
                    dma("sp", gin_rows(V_OFF + h * 128, 128), vst[sl][:, hh, :], ["vst%d" % sl], ["gin.v%d" % h], "sv%d" % sl)
                    cc_written("gin.k%d" % h); cc_written("gin.v%d" % h)
            if KSTOP <= 3:
                break
            cc_flush_ready()
            barrier()

            if KSTOP <= 4:
                break
            uvz = [ph.f32(1536) for _ in range(2)]
            vsq = [ph.f32(512) for _ in range(2)]
            sa_ = [ph.f32(16) for _ in range(2)]
            vn = [ph.bf16(512) for _ in range(2)]
            t1_ = [ph.f32(512) for _ in range(2)]
            ya_ = [ph.f32(512) for _ in range(2)]
            junk_ = [ph.bf16(512) for _ in range(2)]
            yan = [ph.bf16(512) for _ in range(2)]
            for t in range(NT):
                b = t % 2
                cc_tick(4)
                sa = sa_[b]; t1 = t1_[b]; ya = ya_[b]; junk = junk_[b]; B_ = str(b)
                U = uvz[b][:, 0:512]; Vv = uvz[b][:, 512:1024]; Z = uvz[b][:, 1024:1536]
                dma("sp", uvz[b], proj_d[t * 128:(t + 1) * 128, 0:1536], [], ["uvz%d" % b], "ldh%d" % b)
                tt("pool", vsq[b], Vv, Vv, ALU.mult, ["uvz%d" % b], ["vsq" + B_])
                red(sa[:, 0:4], vsq[b].rearrange("p (a b) -> p a b", b=128), ["vsq" + B_], ["sa0" + B_])
                rstd_of((sa[:, 0:4], "sa0" + B_), 128, None, 4, (sa[:, 4:8], "sa1" + B_), (sa[:, 8:12], "sa2" + B_))
                for h in range(4):
                    stt(vn[b][:, h * 128:(h + 1) * 128], Vv[:, h * 128:(h + 1) * 128], sa[:, 8 + h:9 + h],
                        vec_t[:, V_SGU + h * 128:V_SGU + (h + 1) * 128], ALU.mult, ALU.mult, ["uvz%d" % b, "sa2" + B_, "vec"], ["vn%d" % b])
                for h in range(4):
                    mm(ps[b][:, h * 128:(h + 1) * 128], wsp_b[:, h, :], vn[b][:, h * 128:(h + 1) * 128], True, True, ["wsp_b", "vn%d" % b], ["ps" + B_])
                for h in range(4):
                    stt(t1[:, h * 128:(h + 1) * 128], ps[b][:, h * 128:(h + 1) * 128], cols_t[:, 8 + h:9 + h], U[:, h * 128:(h + 1) * 128],
                        ALU.add, ALU.mult, ["ps" + B_, "cols", "uvz%d" % b], ["t1" + B_])
                tt("pool", ya, t1, Z, ALU.mult, ["t1" + B_, "uvz%d" % b], ["ya" + B_])
                act(junk, ya, AF.Square, ["ya" + B_], ["junk" + B_, "sa3" + B_], accum=sa[:, 12:13])
                rstd_of((sa[:, 12:13], "sa3" + B_), 512, None, 1, (sa[:, 13:14], "sa4" + B_), (sa[:, 14:15], "sa5" + B_))
                stt(yan[b], ya, sa[:, 14:15], vec_t[:, V_OA:V_OA + 512], ALU.mult, ALU.mult, ["ya" + B_, "sa5" + B_, "vec"], ["yan%d" % b])
                for j in range(4):
                    tr(psb(4 + b)[:, j * 128:(j + 1) * 128], yan[b][:, j * 128:(j + 1) * 128], ["yan%d" % b, "ident"], ["ps%d" % (4 + b)])
                evac(A[:, 0:4, t * 128:(t + 1) * 128], psb(4 + b)[:, 0:512].rearrange("p (a b) -> p a b", b=128), ["ps%d" % (4 + b)], ["A.%d.0" % t])
            barrier()

            if KSTOP <= 5:
                break
            cc_flush_all()
            for rk in range(4):
                dma("sp", G_b[2 * rk:2 * rk + 2, :], gout_rows(rk, HALO, 2)[:, 0:512],
                    [gres(HALO)], ["G_b.%d" % rk], "lgb%d" % rk)
            b4 = [ph.f32(2048) for _ in range(3)]
            xf = [ph.f32(512) for _ in range(3)]
            xb = [ph.bf16(512) for _ in range(3)]
            cvs = [[ph.f32(512) for _ in range(4)] for _ in range(2)]
            junk_ = [ph.bf16(512) for _ in range(2)]
            sbs = [ph.f32(4) for _ in range(2)]
            ybn = [ph.bf16(512) for _ in range(2)]
            for t in range(NT + 1):
                if t < NT:
                    b = t % 3
                    dma("sp", b4[b], proj_d[t * 128:(t + 1) * 128, 1536:3584], [], ["b4%d" % b], "ldb%d" % b)
                    tt("pool", xf[b], b4[b][:, 512:1024], b4[b][:, 1024:1536], ALU.mult, ["b4%d" % b], ["xf%d" % b])
                    cp("act", xb[b], xf[b], ["xf%d" % b], ["xb%d" % b])
                j = t - 1
                if j < 0:
                    continue
                jb = j % 3; pb_ = (j - 1) % 3; nb_ = (j + 1) % 3
                o = j % 2; O_ = str(o)
                cv, ca, cb2, yb = cvs[o]; junk = junk_[o]; sb_ = sbs[o]
                b1, b2 = (1, 2) if o == 0 else (6, 7)
                p1 = "ps%d" % b1; p2 = "ps%d" % b2
                mm(ps[b1][:, :], shm[:, 0, :], xb[jb], True, False, ["shm", "xb%d" % jb], [p1])
                if j > 0:
                    mm(ps[b1][:, :], shm[:, 2, :], xb[pb_], False, True, ["shm", "xb%d" % pb_], [p1])
                else:
                    mm(ps[b1][:, :], sel_b[0:8, 0:128], G_b[0:8, :], False, True, ["sel_b", "G_b.0", "G_b.1", "G_b.2", "G_b.3"], [p1])
                mm(ps[b2][:, :], shm[:, 1, :], xb[jb], True, False, ["shm", "xb%d" % jb], [p2])
                if j < NT - 1:
                    mm(ps[b2][:, :], shm[:, 3, :], xb[nb_], False, True, ["shm", "xb%d" % nb_], [p2])
                else:
                    mm(ps[b2][:, :], sel_b[0:8, 128:256], G_b[0:8, :], False, True, ["sel_b", "G_b.0", "G_b.1", "G_b.2", "G_b.3"], [p2])
                tt("pool", cv, xf[jb], vec_t[:, V_CW1:V_CW1 + 512], ALU.mult, ["xf%d" % jb, "vec"], ["cv" + O_])
                tt("pool", cv, cv, vec_t[:, V_CB:V_CB + 512], ALU.add, ["cv" + O_, "vec"], ["cv" + O_])
                tt("dve", ca, ps[b1][:, :], vec_t[:, V_CW0:V_CW0 + 512], ALU.mult, [p1, "vec"], ["ca" + O_])
                tt("dve", cb2, ps[b2][:, :], vec_t[:, V_CW2:V_CW2 + 512], ALU.mult, [p2, "vec"], ["cb2" + O_])
                tt("dve", cv, cv, ca, ALU.add, ["cv" + O_, "ca" + O_], ["cv" + O_])
                tt("dve", cv, cv, cb2, ALU.add, ["cv" + O_, "cb2" + O_], ["cv" + O_])
                tt("dve", yb, cv, b4[jb][:, 0:512], ALU.mult, ["cv" + O_, "b4%d" % jb], ["yb" + O_])
                tt("pool", yb, yb, b4[jb][:, 1536:2048], ALU.mult, ["yb" + O_, "b4%d" % jb], ["yb" + O_])
                act(junk, yb, AF.Square, ["yb" + O_], ["junk" + O_, "sb0" + O_], accum=sb_[:, 0:1])
                rstd_of((sb_[:, 0:1], "sb0" + O_), 512, None, 1, (sb_[:, 1:2], "sb1" + O_), (sb_[:, 2:3], "sb2" + O_))
                stt(ybn[o], yb, sb_[:, 2:3], vec_t[:, V_OB:V_OB + 512], ALU.mult, ALU.mult, ["yb" + O_, "sb2" + O_, "vec"], ["ybn%d" % o])
                for q in range(4):
                    tr(psb(4 + o)[:, q * 128:(q + 1) * 128], ybn[o][:, q * 128:(q + 1) * 128], ["ybn%d" % o, "ident"], ["ps%d" % (4 + o)])
                evac(A[:, 4:8, j * 128:(j + 1) * 128], psb(4 + o)[:, 0:512].rearrange("p (a b) -> p a b", b=128), ["ps%d" % (4 + o)], ["A.%d.1" % j])
            barrier()

            if KSTOP <= 6:
                break
            krT_all = ph.bf16(SEQ)
            for rk in range(4):
                dma("sp", krT_all[0:64, rk * TOK:(rk + 1) * TOK], gout_rows(rk, KR_OFF, 64), [gres(KR_OFF)], ["krT.%d" % rk], "lkr%d" % rk)
            P.op("pool", lambda e: e.memset(krT_all[64:128, :], 0.0), w=["krT.pad"])
            kTr = [ph.bf16(TOK) for _ in range(3)]
            Vr = [ph.bf16(TOK).rearrange("p (a b) -> p a b", b=128) for _ in range(3)]
            qt = ph.f32(1536)
            qsq = ph.f32(1536)
            qs = ph.f32(48)
            qnb = ph.bf16(1024)
            qrf = ph.f32(512)
            qrb = ph.bf16(512)
            r4 = [qsq[:, i * 256:(i + 1) * 256] for i in range(4)]
            QnT = ph.bf16(8 * 512).rearrange("p (a b) -> p a b", b=512)
            QrT = ph.bf16(8 * 512).rearrange("p (a b) -> p a b", b=512)
            P.op("pool", lambda e: e.memset(QrT[64:128, :, :], 0.0), w=["QrT.pad"])
            zc = [ph.f32(512) for _ in range(2)]
            yc = ph.f32(8 * 512).rearrange("p (a b) -> p a b", b=512)
            pT = [ph.bf16(512) for _ in range(8)]
            d4 = ph.f32(512)
            rden = ph.f32(512)
            ysq = rden
            rc = qsq[:, 0:512]; rc2 = qsq[:, 512:1024]
            for qb in range(NB):
                for ti in range(4):
                    t = qb * 4 + ti
                    dma("sp", qt, proj_d[t * 128:(t + 1) * 128, 3584:5120], [], ["qt"], "ldq")
                    q3 = qt.rearrange("p (h d) -> p h d", d=192)
                    s3 = qsq.rearrange("p (h d) -> p h d", d=192)
                    tt("dve", qsq, qt, qt, ALU.mult, ["qt"], ["qsq", "r40", "r41", "r42", "r43"])
                    red(qs[:, 0:8], s3[:, :, 0:128], ["qsq"], ["qs0"])
                    red(qs[:, 8:16], s3[:, :, 128:192], ["qsq"], ["qs1"])
                    rstd_of((qs[:, 0:8], "qs0"), 128, None, 8, (qs[:, 16:24], "qs2"), (qs[:, 32:40], "qs4"))
                    rstd_of((qs[:, 8:16], "qs1"), 64, None, 8, (qs[:, 24:32], "qs3"), (qs[:, 40:48], "qs5"))
                    for h in range(8):
                        stt(qnb[:, h * 128:(h + 1) * 128], q3[:, h, 0:128], qs[:, 32 + h:33 + h], vec_t[:, V_QN:V_QN + 128],
                            ALU.mult, ALU.mult, ["qt", "qs4", "vec"], ["qnb"])
                        stt(qrf[:, h * 64:(h + 1) * 64], q3[:, h, 128:192], qs[:, 40 + h:41 + h], vec_t[:, V_QR:V_QR + 64],
                            ALU.mult, ALU.mult, ["qt", "qs5", "vec"], ["qrf"])
                    f3 = qrf.rearrange("p (h d) -> p h d", d=64)
                    o3 = qrb.rearrange("p (h d) -> p h d", d=64)
                    x1 = f3[:, :, 0:32]; x2 = f3[:, :, 32:64]
                    Ct = cosT[:, t, :].unsqueeze(1).to_broadcast([128, 8, 32])
                    St = sinT[:, t, :].unsqueeze(1).to_broadcast([128, 8, 32])
                    rv = [r.rearrange("p (h d) -> p h d", d=32) for r in r4]
                    tt("pool", rv[0], x1, Ct, ALU.mult, ["qrf", "tab1", "qs0", "qs1"], ["r40"])
                    tt("pool", rv[1], x2, St, ALU.mult, ["qrf", "tab0", "qs0", "qs1"], ["r41"])
                    tt("pool", o3[:, :, 0:32], rv[0], rv[1], ALU.subtract, ["r40", "r41"], ["qrb.a"])
                    tt("dve", rv[2], x2, Ct, ALU.mult, ["qrf", "tab1", "qs0", "qs1"], ["r42"])
                    tt("dve", rv[3], x1, St, ALU.mult, ["qrf", "tab0", "qs0", "qs1"], ["r43"])
                    tt("dve", o3[:, :, 32:64], rv[2], rv[3], ALU.add, ["r42", "r43"], ["qrb.b"])
                    for hg in range(2):
                        for j in range(4):
                            h = hg * 4 + j
                            tr(psb(4)[:, j * 128:(j + 1) * 128], qnb[:, h * 128:(h + 1) * 128], ["qnb", "ident"], ["ps4"])
                        evac(QnT[:, hg * 4:hg * 4 + 4, ti * 128:(ti + 1) * 128], psb(4)[:, 0:512].rearrange("p (a b) -> p a b", b=128),
                             ["ps4"], ["QnT"])
                    for h in range(8):
                        tr(psb(5)[0:64, h * 128:(h + 1) * 128], qrb[:, h * 64:(h + 1) * 64], ["qrb.a", "qrb.b", "ident"], ["ps5"])
                    evac(QrT[0:64, :, ti * 128:(ti + 1) * 128], psb(5)[0:64, 0:1024].rearrange("p (a b) -> p a b", b=128), ["ps5"], ["QrT"])
                its = [(h, rk, kc) for h in range(8) for rk in range(4) for kc in range(VC)]
                segs = [(h, rk) for h in range(8) for rk in range(4)]

                def load_seg(si):
                    h, rk = segs[si]
                    s = si % 3
                    dma("sp", kTr[s], gout_rows(rk, K_OFF + h * 128, 128), [gres(K_OFF + h * 128)], ["kTr%d" % s], "lk%d" % s)
                    dma("sp", Vr[s], gout_rows(rk, V_OFF + h * 128, 128).rearrange("p (a b) -> p a b", b=128),
                        [gres(V_OFF + h * 128)], ["Vr%d" % s], "lv%d" % s)

                SB = (0, 1, 2, 5)

                def s_mm(ii):
                    h, rk, kc = its[ii]
                    si = h * 4 + rk
                    s = si % 3
                    bank = SB[ii % 4]
                    mm(ps[bank][:, :], kTr[s][:, kc * 128:(kc + 1) * 128], QnT[:, h, :], True, False, ["kTr%d" % s, "QnT"], ["ps%d" % bank])
                    mm(ps[bank][:, :], krT_all[:, rk * TOK + kc * 128:rk * TOK + (kc + 1) * 128], QrT[:, h, :], False, True,
                       ["krT.%d" % rk, "krT.pad", "QrT", "QrT.pad"], ["ps%d" % bank])

                load_seg(0); load_seg(1)
                nit = len(its)
                s_mm(0); s_mm(1)
                for ii in range(nit):
                    h, rk, kc = its[ii]
                    si = h * 4 + rk
                    s = si % 3
                    if kc == 0:
                        if si + 2 < len(segs):
                            load_seg(si + 2)
                        if rk == 0:
                            z = h % 2
                            dma("sp", zc[z], zct_d[h * 128:(h + 1) * 128, qb * 512:(qb + 1) * 512], [], ["zc%d" % z], "lz%d" % z)
                    if ii + 2 < nit:
                        s_mm(ii + 2)
                    bank = SB[ii % 4]
                    pi_ = ii % 8
                    act(pT[pi_], ps[bank][:, :], AF.Exp, ["ps%d" % bank], ["pT%d" % pi_], scale=SC)
                    first = (rk == 0 and kc == 0); last = (rk == 3 and kc == VC - 1)
                    mm(ps[3][:, :], Vr[s][:, kc, :], pT[pi_], first, last, ["Vr%d" % s, "pT%d" % pi_], ["ps3"])
                    if ii % 4 == 3:
                        gfirst = (rk == 0 and kc == 3); glast = last
                        for j in range(4):
                            pj = (ii - 3 + j) % 8
                            P.op("pe", lambda e, j=j, pj=pj, gfirst=gfirst, glast=glast:
                                 e.matmul(ps[6][32 * j:32 * j + 32, :], lhsT=ones_b[:, 0:32], rhs=pT[pj], start=gfirst, stop=glast,
                                          tile_position=(0, 32 * j)),
                                 r=["ones_b", "pT%d" % pj], w=["ps6"])
                    if last:
                        z = h % 2
                        cp("dve", d4, ps[6][:, :], ["ps6"], ["d4"])
                        mm(ps[4][:, :], ones_f[:], d4, True, True, ["ones_f", "d4"], ["ps4"])
                        P.op("dve", lambda e: e.reciprocal(out=rden, in_=ps[4][:, :]), r=["ps4"], w=["rden"])
                        stt(yc[:, h, :], ps[3][:, :], 32.0, rden, ALU.mult, ALU.mult, ["ps3", "rden"], ["yc%d" % h])
                        tt("pool", yc[:, h, :], yc[:, h, :], zc[z], ALU.mult, ["yc%d" % h, "zc%d" % z], ["yc%d" % h])
                        tt("pool", ysq, yc[:, h, :], yc[:, h, :], ALU.mult, ["yc%d" % h], ["rden"])
                        mm(ps[7][:, :], ones_f[:], ysq, h == 0, h == 7, ["ones_f", "rden"], ["ps7"])
                ts("dve", rc, ps[7][:, :], 1.0 / 1024, EPS, ALU.mult, ALU.add, ["ps7"], ["qsq", "r40", "r41", "r42", "r43"])
                act(rc2, rc, AF.Sqrt, ["qsq"], ["qsq"])
                P.op("dve", lambda e: e.reciprocal(out=rc2, in_=rc2), r=["qsq"], w=["qsq"])
                for h in range(8):
                    stt(A[:, 8 + h, qb * 512:(qb + 1) * 512], yc[:, h, :], cols_t[:, h:h + 1], rc2, ALU.mult, ALU.mult,
                        ["yc%d" % h, "cols", "qsq"], ["A.%d.%d" % (4 * qb + i, 2 + h // 4) for i in range(4)])
            barrier()

            if KSTOP <= 7:
                break
            wslots = [ph.bf16(16 * 512).rearrange("p (a b) -> p a b", b=512) for _ in range(2)]
            hsl = [ph.f32(512) for _ in range(4)]
            hi = [0]
            for g in range(4):
                sl = g % 2

                def epi(t, pap, pres, g=g):
                    s = hi[0]; hi[0] = (s + 1) % 4
                    dma("act", hsl[s], hsrc[t * 128:(t + 1) * 128, g * 512:(g + 1) * 512], [], ["hsl%d" % s], "lh%d" % s)
                    tt("dve", hsl[s], pap, hsl[s], ALU.add, [pres, "hsl%d" % s], ["hsl%d" % s])
                    dma("sp", hbuf_d[t * 128:(t + 1) * 128, g * 512:(g + 1) * 512], hsl[s], ["hsl%d" % s], [], "sh%d" % s)

                gemm(wout_d[l, :, g * 512:(g + 1) * 512], 16, 512, wslots[sl], "wslot%d" % sl, A, a_res, epi)
            barrier()

            if KSTOP <= 8:
                break
            norm_transpose(l, hbuf_d, 1)
            pTb = ph.bf16(2 * TOK).rearrange("p (a b) -> p a b", b=TOK)
            pf = [ph.f32(256) for _ in range(2)]
            pbf = [ph.bf16(256) for _ in range(2)]
            for t in range(NT):
                b = t % 2
                dma("sp", pf[b], p_d[l, t * 128:(t + 1) * 128, :], [], ["pf%d" % b], "lp%d" % b)
                cp("pool", pbf[b], pf[b], ["pf%d" % b], ["pbf%d" % b])
                for j in range(2):
                    tr(psb(6)[:, j * 128:(j + 1) * 128], pbf[b][:, j * 128:(j + 1) * 128], ["pbf%d" % b, "ident"], ["ps6"])
                evac(pTb[:, :, t * 128:(t + 1) * 128], psb(6)[:, 0:256].rearrange("p (a b) -> p a b", b=128), ["ps6"], ["pTb.%d" % t])
            wslots = [ph.bf16(16 * 512).rearrange("p (a b) -> p a b", b=512) for _ in range(2)]
            wpp = ph.bf16(2 * D).rearrange("p (a b) -> p a b", b=D)
            P.op("pool", lambda e, l=l: e.dma_start(out=wpp, in_=wpp_d[l, :, :].rearrange("(kc p) n -> p kc n", p=128)), r=[], w=["wpp"], stream="w.wpp")
            hsl = [ph.f32(512) for _ in range(4)]
            gt = [ph.f32(512) for _ in range(2)]
            gi_ = [0]
            for g in range(4):
                sl = g % 2

                def epi(t, pap, pres, g=g):
                    s = hi[0]; hi[0] = (s + 1) % 4
                    k = gi_[0]; gi_[0] ^= 1
                    pbank = 6 + k
                    for kc in range(2):
                        mm(ps[pbank][:, :], pTb[:, kc, t * 128:(t + 1) * 128], wpp[:, kc, g * 512:(g + 1) * 512], kc == 0, kc == 1,
                           ["pTb.%d" % t, "wpp"], ["ps%d" % pbank])
                    dma("act", hsl[s], hbuf_d[t * 128:(t + 1) * 128, g * 512:(g + 1) * 512], [], ["hsl%d" % s], "lh%d" % s)
                    act(gt[k], pap, AF.Sigmoid, [pres], ["gt%d" % k])
                    tt("dve", gt[k], gt[k], ps[pbank][:, :], ALU.mult, ["gt%d" % k, "ps%d" % pbank], ["gt%d" % k])
                    tt("pool", hsl[s], hsl[s], gt[k], ALU.add, ["hsl%d" % s, "gt%d" % k], ["hsl%d" % s])
                    dma("sp", hdst[t * 128:(t + 1) * 128, g * 512:(g + 1) * 512], hsl[s], ["hsl%d" % s], [], "sh%d" % s)

                gemm(wgate_d[l, :, g * 512:(g + 1) * 512], 16, 512, wslots[sl], "wslot%d" % sl, A, a_res, epi)
            barrier()

        streams = sorted(P.stream_last.keys())
        sems = {}
        for e in Prog.ENGS:
            sems[e] = es.enter_context(nc.semaphore("sem_" + e))
        ssem = {}
        for i, s in enumerate(streams):
            ssem[s] = es.enter_context(nc.semaphore("ds%d" % i))
        P.emit(nc, sems, ssem)
    return nc


_CACHE = {}


def _host_inputs(L, TOK, x, p, positions, attn_norm, w_in, sgu_norm, w_spatial, b_spatial, conv_w, conv_b,
                 kv_norm, w_ukv, q_nope_norm, q_rope_norm, k_nope_norm, k_rope_norm,
                 out_norm, w_out, ple_norm, w_ple_gate, w_ple_proj):
    f = np.float32
    NT = TOK // 128
    a = lambda v: np.ascontiguousarray(np.asarray(v))
    x = a(x); p = a(p); positions = a(positions)
    w_in = a(w_in)[:L]; w_ukv = a(w_ukv)[:L]; w_out = a(w_out)[:L]; w_gate = a(w_ple_gate)[:L]; w_pp = a(w_ple_proj)[:L]
    w_spT = a(np.transpose(np.asarray(w_spatial)[:L], (0, 3, 1, 2)).reshape(L, 128, 512))
    out_norm = np.asarray(out_norm)[:L]
    vec = np.concatenate([out_norm[:, 0:1024], np.asarray(sgu_norm)[:L].reshape(L, 512), np.asarray(conv_w)[:L].reshape(L, 1536),
                          np.asarray(conv_b)[:L], np.asarray(kv_norm)[:L], np.asarray(q_nope_norm)[:L], np.asarray(q_rope_norm)[:L],
                          np.asarray(k_nope_norm)[:L], np.asarray(k_rope_norm)[:L]], axis=1).astype(f)
    assert vec.shape[1] == NV
    gn = a(np.stack([np.asarray(attn_norm)[:L], np.asarray(ple_norm)[:L]], axis=1).astype(f))
    gc = np.transpose(out_norm[:, 1024:2048].reshape(L, 8, 128), (0, 2, 1))
    bs = np.transpose(np.asarray(b_spatial)[:L], (0, 2, 1))
    cols = a(np.concatenate([gc, bs], axis=2).astype(f))
    cmat = np.zeros((128, 640), f)
    cmat[:, 0:128] = np.eye(128, dtype=f)
    for m in range(127):
        cmat[m, 128 + m + 1] = 1.0
        cmat[m + 1, 256 + m] = 1.0
    cmat[127, 384 + 0] = 1.0
    cmat[0, 512 + 127] = 1.0
    invf = (1.0 / (10000.0 ** (np.arange(0, 64, 2, dtype=f) / f(64)))).astype(f)
    invf = a(np.broadcast_to(invf[None, :], (128, 32)))
    shared = dict(w_in=w_in, w_ukv=w_ukv, w_out=w_out, w_gate=w_gate, w_pp=w_pp, w_spT=w_spT, vec=vec, gn=gn, cols=cols,
                  cmat=cmat, invf=invf)
    maps = []
    for c in range(8):
        b, r = c // 4, c % 4
        sel = np.zeros((8, 256), f)
        if r > 0:
            sel[2 * (r - 1) + 1, 0] = 1.0
        if r < 3:
            sel[2 * (r + 1), 128 + 127] = 1.0
        m = dict(shared)
        m["x"] = a(x[b, r * TOK:(r + 1) * TOK, :])
        m["p"] = a(p[:L, b, r * TOK:(r + 1) * TOK, :])
        m["pos"] = a(positions[b, r * TOK:(r + 1) * TOK].reshape(NT, 128).T.astype(np.int32))
        m["sel"] = sel
        maps.append(m)
    return maps


def run(L, TOK, inputs, trace=False):
    key = (L, TOK)
    if key not in _CACHE:
        _CACHE[key] = build(L, TOK)
    nc = _CACHE[key]
    maps = _host_inputs(L, TOK, **inputs)
    res = run_bass_kernel_spmd(nc, maps, core_ids=list(range(8)), trace=trace)
    S = 4 * TOK
    out = np.empty((2, S, D), np.float32)
    for c in range(8):
        b, r = c // 4, c % 4
        out[b, r * TOK:(r + 1) * TOK, :] = res.results[c]["out"]
    return out, res


def kernel(**inputs):
    out, _ = run(4, 2048, inputs)
    return out
```

```python
import contextlib
import numpy as np
import concourse.bass as bass
import concourse.mybir as mybir
from concourse.bass_utils import run_bass_kernel_spmd

F32 = mybir.dt.float32
BF16 = mybir.dt.bfloat16
I32 = mybir.dt.int32
AF = mybir.ActivationFunctionType
ALU = mybir.AluOpType
AX = mybir.AxisListType

D = 2048
INW = 6720
NPROJ = 5696
EPS = 1e-6
PI = float(np.pi)
SC = float(192 ** -0.5)
V_OA, V_OB, V_SGU, V_CW0, V_CW1, V_CW2, V_CB, V_KV, V_QN, V_QR, V_KN, V_KR, NV = (
    0, 512, 1024, 1536, 2048, 2560, 3072, 3584, 4096, 4224, 4288, 4416, 4480)


class Ins:
    __slots__ = ("idx", "eng", "fn", "waits", "stream", "inc", "signal", "sigval")


class Prog:
    ENGS = ("pe", "act", "dve", "pool", "sp")

    def __init__(self):
        self.ins = []
        self.last_w = {}
        self.readers = {}
        self.stream_last = {}
        self.stream_inc = {}
        self.eng_last = {e: None for e in self.ENGS}
        self.waited = {e: {} for e in self.ENGS}
        self.pending = {e: [] for e in self.ENGS}

    def _key(self, p):
        return ("s", p.stream) if p.stream is not None else ("e", p.eng)

    def op(self, eng, fn, r=(), w=(), stream=None, inc=16, extra=()):
        i = Ins()
        i.idx = len(self.ins); i.eng = eng; i.fn = fn; i.stream = stream; i.inc = inc
        i.signal = False; i.sigval = 0
        deps = set(extra)
        for x in r:
            if x in self.last_w:
                deps.add(self.last_w[x])
        for x in w:
            if x in self.last_w:
                deps.add(self.last_w[x])
            rd = self.readers.get(x)
            if rd:
                deps.update(rd.values())
        deps.update(self.pending[eng]); self.pending[eng] = []
        me_key = ("s", stream) if stream is not None else ("e", eng)
        for x in r:
            self.readers.setdefault(x, {})[me_key] = i.idx
        for x in w:
            self.last_w[x] = i.idx
            self.readers[x] = {}
        waits = []
        wd = self.waited[eng]
        best = {}
        for d in deps:
            k = self._key(self.ins[d])
            if k == ("e", "pe") and eng == "pe" and stream is None:
                continue
            if best.get(k, -1) < d:
                best[k] = d
        for k, d in sorted(best.items(), key=lambda kv: kv[1]):
            if wd.get(k, -1) >= d:
                continue
            wd[k] = d
            self.ins[d].signal = True
            waits.append(d)
        i.waits = waits
        self.ins.append(i)
        if stream is not None:
            self.stream_last[stream] = i.idx
            self.stream_inc[stream] = inc
        self.eng_last[eng] = i.idx
        return i.idx

    def barrier(self, fn):
        extra = [v for v in self.eng_last.values() if v is not None]
        extra += [v for s, v in self.stream_last.items() if not s.startswith("cc")]
        b = self.op("dve", fn, extra=extra)
        for e in self.ENGS:
            if e != "dve":
                self.pending[e].append(b)
        return b

    def emit(self, nc, sem_of_eng, sem_of_stream):
        cnt = {}
        for i in self.ins:
            k = self._key(i)
            if i.stream is not None:
                cnt[k] = cnt.get(k, 0) + i.inc
                i.sigval = cnt[k]
            elif i.signal:
                cnt[k] = cnt.get(k, 0) + 1
                i.sigval = cnt[k]
        self.final_counts = cnt

        def semof(p):
            return sem_of_stream[p.stream] if p.stream is not None else sem_of_eng[p.eng]

        def run(engname, e):
            for i in self.ins:
                if i.eng != engname:
                    continue
                for d in i.waits:
                    p = self.ins[d]
                    e.wait_ge(semof(p), p.sigval)
                inst = i.fn(e)
                if i.stream is not None:
                    inst.then_inc(sem_of_stream[i.stream], i.inc)
                elif i.signal:
                    inst.then_inc(sem_of_eng[i.eng], 1)
            if engname == "sp":
                for s, last in self.stream_last.items():
                    e.wait_ge(sem_of_stream[s], self.ins[last].sigval)

        with nc.Block() as block:
            @block.tensor
            def _(e):
                run("pe", e)

            @block.scalar
            def _(e):
                run("act", e)

            @block.vector
            def _(e):
                run("dve", e)

            @block.gpsimd
            def _(e):
                run("pool", e)

            @block.sync
            def _(e):
                run("sp", e)


class Arena:
    def __init__(self, ap_f32, nf32):
        self.ap = ap_f32; self.n = nf32; self.off = 0

    def reset(self):
        self.off = 0

    def f32(self, n):
        assert self.off + n <= self.n, ("arena overflow", self.off, n, self.n)
        a = self.ap[:, self.off:self.off + n]
        self.off += n
        return a

    def bf16(self, n):
        m = (n + 1) // 2
        return self.f32(m).bitcast(BF16)[:, 0:n]


def build(L, TOK):
    NT = TOK // 128
    NB = TOK // 512
    SEQ = 4 * TOK
    ROWS = 8 * 128 + 8 * 128 + 64 + 2
    K_OFF, V_OFF, KR_OFF, HALO = 0, 1024, 2048, 2112
    CH = max(128, ((1 << 20) // (TOK * 2)) // 128 * 128)
    NCH = (ROWS + CH - 1) // CH
    GW = TOK
    VC = TOK // 128
    nc = bass.Bass("TRN2", target_bir_lowering=False)

    def din(name, shape, dt=F32):
        return nc.dram_tensor(name, list(shape), dt, kind="ExternalInput").ap()

    x_d = din("x", [TOK, D])
    p_d = din("p", [L, TOK, 256])
    pos_d = din("pos", [128, NT], I32)
    win_d = din("w_in", [L, D, INW])
    wukv_d = din("w_ukv", [L, 512, 2048])
    wout_d = din("w_out", [L, D, D])
    wgate_d = din("w_gate", [L, D, D])
    wpp_d = din("w_pp", [L, 256, D])
    wsp_d = din("w_spT", [L, 128, 512])
    vec_d = din("vec", [L, NV])
    gn_d = din("gn", [L, 2, D])
    cols_d = din("cols", [L, 128, 12])
    sel_d = din("sel", [8, 256])
    cmat_d = din("cmat", [128, 640])
    invf_d = din("invf", [128, 32])
    out_d = nc.dram_tensor("out", [TOK, D], F32, kind="ExternalOutput").ap()
    proj_d = nc.dram_tensor("proj", [TOK, NPROJ], F32).ap()
    zct_d = nc.dram_tensor("zct", [1024, TOK], F32).ap()
    hbuf_d = nc.dram_tensor("hbuf", [TOK, D], F32).ap()
    chn = [min(CH, ROWS - k * CH) for k in range(NCH)]
    gin_ts = [nc.dram_tensor("gin%d" % k, [chn[k], GW], BF16) for k in range(NCH)]
    gout_ts = [nc.dram_tensor("gout%d" % k, [4 * chn[k], GW], BF16) for k in range(NCH)]

    def gin_rows(r0, n):
        k = r0 // CH
        assert (r0 + n - 1) // CH == k
        return gin_ts[k].ap()[r0 - k * CH:r0 - k * CH + n, :]

    def gout_rows(rk, r0, n):
        k = r0 // CH
        assert (r0 + n - 1) // CH == k
        o = rk * chn[k] + r0 - k * CH
        return gout_ts[k].ap()[o:o + n, :]

    def gres(r0):
        return "gout.%d" % (r0 // CH)

    P = Prog()
    es = contextlib.ExitStack()
    with es:
        def sb(name, shape, dt):
            return es.enter_context(nc.sbuf_tensor("s_" + name, list(shape), dt))

        A_t = sb("A", [128, 16, TOK], BF16)
        PHN = 27136
        PH_t = sb("PH", [128, PHN], F32)
        vec_t = sb("vec", [128, NV], F32)
        cols_t = sb("cols", [128, 12], F32)
        ident = sb("ident", [128, 128], BF16)
        ones_b = sb("ones_b", [128, 128], BF16)
        ones_f = sb("ones_f", [128, 128], F32)
        shm = sb("shm", [128, 4, 128], BF16)
        sel_b = sb("sel_b", [8, 256], BF16)
        cosT = sb("cosT", [128, NT, 32], F32)
        sinT = sb("sinT", [128, NT, 32], F32)
        invf = sb("invf", [128, 32], F32)
        posi = sb("posi", [128, NT], I32)
        neghalf = sb("neghalf", [128, 1], F32)
        dummy = sb("dummy", [128, 8], F32)
        wsp_b = sb("wsp_b", [128, 4, 128], BF16)
        G_b = sb("G_b", [8, 512], BF16)
        ps = [es.enter_context(nc.psum_tensor("ps%d" % b, [128, 512], F32)) for b in range(8)]
        A = A_t[:]
        ph = Arena(PH_t[:], PHN)

        def psb(b):
            return ps[b][:].bitcast(BF16)

        def dma(q, out, in_, r, w, stream):
            P.op(q, lambda e, out=out, in_=in_: e.dma_start(out=out, in_=in_), r=r, w=w, stream=stream)

        def act(out, in_, func, r, w, scale=None, accum=None, bias=None):
            kw = {}
            if scale is not None:
                kw["scale"] = scale
            if accum is not None:
                kw["accum_out"] = accum
            if bias is not None:
                kw["bias"] = bias
            P.op("act", lambda e, out=out, in_=in_, func=func, kw=kw: e.activation(out=out, in_=in_, func=func, **kw), r=r, w=w)

        def tt(eng, out, in0, in1, op, r, w):
            P.op(eng, lambda e, out=out, in0=in0, in1=in1, op=op: e.tensor_tensor(out=out, in0=in0, in1=in1, op=op), r=r, w=w)

        def ts(eng, out, in0, s1, s2, op0, op1, r, w):
            if op1 is None:
                P.op(eng, lambda e, out=out, in0=in0, s1=s1, op0=op0: e.tensor_scalar(out=out, in0=in0, scalar1=s1, scalar2=None, op0=op0), r=r, w=w)
            else:
                P.op(eng, lambda e, out=out, in0=in0, s1=s1, s2=s2, op0=op0, op1=op1: e.tensor_scalar(out=out, in0=in0, scalar1=s1, scalar2=s2, op0=op0, op1=op1), r=r, w=w)

        def stt(out, in0, scalar, in1, op0, op1, r, w):
            P.op("dve", lambda e, out=out, in0=in0, scalar=scalar, in1=in1, op0=op0, op1=op1:
                 e.scalar_tensor_tensor(out=out, in0=in0, scalar=scalar, in1=in1, op0=op0, op1=op1), r=r, w=w)

        def cp(eng, out, in_, r, w):
            if eng == "act":
                P.op("act", lambda e, out=out, in_=in_: e.copy(out=out, in_=in_), r=r, w=w)
            else:
                P.op(eng, lambda e, out=out, in_=in_: e.tensor_copy(out=out, in_=in_), r=r, w=w)

        def mm(out, lhsT, rhs, start, stop, r, w):
            P.op("pe", lambda e, out=out, lhsT=lhsT, rhs=rhs, start=start, stop=stop:
                 e.matmul(out, lhsT=lhsT, rhs=rhs, start=start, stop=stop), r=r, w=w)

        def tr(out, in_, r, w, idn=None):
            idn = ident[:] if idn is None else idn
            P.op("pe", lambda e, out=out, in_=in_, idn=idn: e.transpose(out=out, in_=in_, identity=idn), r=r, w=w)

        def red(out, in_, r, w):
            P.op("dve", lambda e, out=out, in_=in_: e.tensor_reduce(out=out, in_=in_, axis=AX.X, op=ALU.add), r=r, w=w)

        def rstd_of(ssq, n, rname, width, tmp, out):
            ts("dve", tmp[0], ssq[0], 1.0 / n, EPS, ALU.mult, ALU.add, r=[ssq[1]], w=[tmp[1]])
            tt("pool", out[0], tmp[0], neghalf[:, 0:1].to_broadcast([128, width]), ALU.pow, r=[tmp[1]], w=[out[1]])

        def barrier():
            P.barrier(lambda e: e.memset(dummy[:, 0:1], 0.0))
            ph.reset()

        evac_flip = [0]

        def evac(out, in_, r, w):
            evac_flip[0] ^= 1
            cp("act" if evac_flip[0] else "dve", out, in_, r, w)

        cm_f = ph.f32(640)
        dma("sp", cm_f, cmat_d[:, :], [], ["cm_f"], "i0")
        cp("dve", ident[:], cm_f[:, 0:128], ["cm_f"], ["ident"])
        cp("dve", shm[:].rearrange("p a b -> p (a b)"), cm_f[:, 128:640], ["cm_f"], ["shm"])
        P.op("pool", lambda e: e.memset(ones_b[:], 1.0), w=["ones_b"])
        P.op("pool", lambda e: e.memset(ones_f[:], 1.0), w=["ones_f"])
        P.op("pool", lambda e: e.memset(neghalf[:], -0.5), w=["neghalf"])
        sel_f = ph.f32(256)
        dma("sp", sel_f[0:8, :], sel_d[:, :], [], ["sel_f"], "i1")
        cp("dve", sel_b[:], sel_f[0:8, :], ["sel_f"], ["sel_b"])
        dma("sp", invf[:], invf_d[:, :], [], ["invf"], "i2")
        dma("sp", posi[:], pos_d[:, :], [], ["posi"], "i3")
        posf = ph.f32(NT)
        cp("dve", posf, posi[:], ["posi"], ["posf"])
        ang = ph.f32(NT * 32).rearrange("p (a b) -> p a b", b=32)
        tt("dve", ang, posf.unsqueeze(2).to_broadcast([128, NT, 32]),
           invf[:].unsqueeze(1).to_broadcast([128, NT, 32]), ALU.mult, ["posf", "invf"], ["ang"])
        uu = ph.f32(NT * 32).rearrange("p (a b) -> p a b", b=32)
        ki = ph.f32(NT * 32).bitcast(I32).rearrange("p (a b) -> p a b", b=32)
        kf = ph.f32(NT * 32).rearrange("p (a b) -> p a b", b=32)
        rr = ph.f32(NT * 32).rearrange("p (a b) -> p a b", b=32)
        mk = ph.f32(NT * 32).rearrange("p (a b) -> p a b", b=32)
        C1 = 6.28125
        C2 = float(2 * np.pi - 6.28125)
        for which, dst in ((0, sinT), (1, cosT)):
            if which == 1:
                ts("dve", ang, ang, PI / 2, None, ALU.add, None, ["ang"], ["ang"])
            ts("dve", uu, ang, 1.0 / (2 * PI), None, ALU.mult, None, ["ang"], ["uu"])
            cp("dve", ki, uu, ["uu"], ["ki"])
            cp("dve", kf, ki, ["ki"], ["kf"])
            stt(rr, kf, -C1, ang, ALU.mult, ALU.add, ["kf", "ang"], ["rr"])
            stt(rr, kf, -C2, rr, ALU.mult, ALU.add, ["kf", "rr"], ["rr"])
            ts("dve", mk, rr, -PI, None, ALU.is_lt, None, ["rr"], ["mk"])
            stt(rr, mk, 2 * PI, rr, ALU.mult, ALU.add, ["mk", "rr"], ["rr"])
            ts("dve", mk, rr, PI, None, ALU.is_gt, None, ["rr"], ["mk"])
            stt(rr, mk, -2 * PI, rr, ALU.mult, ALU.add, ["mk", "rr"], ["rr"])
            ts("dve", rr, rr, PI, -PI, ALU.min, ALU.max, ["rr"], ["rr"])
            act(dst[:], rr, AF.Sin, ["rr"], ["tab%d" % which])
        barrier()

        def norm_transpose(l, src_d, gidx):
            gnorm = ph.f32(D)
            dma("sp", gnorm, gn_d[l, gidx, :].partition_broadcast(128), [], ["gnorm"], "lgn")
            hts = [ph.f32(D) for _ in range(2)]
            hns = [ph.bf16(D) for _ in range(2)]
            junk = ph.bf16(D)
            st = ph.f32(8)
            def nt_s1(t):
                b = t % 2
                dma("sp", hts[b], src_d[t * 128:(t + 1) * 128, :], [], ["ht%d" % b], "ldh%d" % b)
                act(junk, hts[b], AF.Square, ["ht%d" % b], ["junk", "ssq%d" % b], accum=st[:, b:b + 1])
                rstd_of((st[:, b:b + 1], "ssq%d" % b), D, None, 1, (st[:, 2 + b:3 + b], "tq%d" % b), (st[:, 4 + b:5 + b], "rs%d" % b))
                stt(hns[b], hts[b], st[:, 4 + b:5 + b], gnorm, ALU.mult, ALU.mult, ["ht%d" % b, "rs%d" % b, "gnorm"], ["hn%d" % b])

            def nt_s2(t):
                b = t % 2
                for kg in range(4):
                    bank = 4 + (kg % 2)
                    for j in range(4):
                        kc = kg * 4 + j
                        tr(psb(bank)[:, j * 128:(j + 1) * 128], hns[b][:, kc * 128:(kc + 1) * 128], ["hn%d" % b, "ident"], ["ps%d" % bank])
                    evac(A[:, kg * 4:kg * 4 + 4, t * 128:(t + 1) * 128],
                         psb(bank)[:, 0:512].rearrange("p (a b) -> p a b", b=128), ["ps%d" % bank], ["A.%d.%d" % (t, kg)])

            for t in range(NT + 1):
                if t < NT:
                    nt_s1(t)
                if t >= 1:
                    nt_s2(t - 1)

        gemm_bank = [0]

        def gemm(w_src, KC, ncols, wslot, wres, actT, act_res, epilogue, yform=False):
            P.op("pool", lambda e: e.dma_start(out=wslot[:, 0:KC, 0:ncols], in_=w_src.rearrange("(kc p) n -> p kc n", p=128)),
                 r=[], w=[wres], stream="w." + wres)
            if not yform:
                for t in range(NT):
                    bank = gemm_bank[0]; gemm_bank[0] = (gemm_bank[0] + 1) % 4
                    for kc in range(KC):
                        mm(ps[bank][:, 0:ncols], actT[:, kc, t * 128:(t + 1) * 128], wslot[:, kc, 0:ncols],
                           kc == 0, kc == KC - 1, [wres] + act_res(t), ["ps%d" % bank])
                    epilogue(t, ps[bank][:, 0:ncols], "ps%d" % bank)
            else:
                for fc in range(ncols // 128):
                    for qb in range(NB):
                        bank = gemm_bank[0]; gemm_bank[0] = (gemm_bank[0] + 1) % 4
                        rs = []
                        for t in range(4 * qb, 4 * qb + 4):
                            rs += act_res(t)
                        for kc in range(KC):
                            mm(ps[bank][:, :], wslot[:, kc, fc * 128:(fc + 1) * 128], actT[:, kc, qb * 512:(qb + 1) * 512],
                               kc == 0, kc == KC - 1, [wres] + rs, ["ps%d" % bank])
                        epilogue((fc, qb), ps[bank][:, :], "ps%d" % bank)

        def a_res(t):
            return ["A.%d.%d" % (t, kg) for kg in range(4)]

        KSTOP = 99
        for l in range(L):
            if KSTOP <= 0:
                break
            hsrc = x_d if l == 0 else hbuf_d
            hdst = out_d if l == L - 1 else hbuf_d
            dma("sp", vec_t[:], vec_d[l, :].partition_broadcast(128), [], ["vec"], "lvec")
            dma("sp", cols_t[:], cols_d[l, :, :], [], ["cols"], "lcols")
            P.op("pool", lambda e, l=l: e.dma_start(out=wsp_b[:].rearrange("p a b -> p (a b)"), in_=wsp_d[l, :, :]), r=[], w=["wsp_b"], stream="w.wsp")

            norm_transpose(l, hsrc, 0)
            barrier()

            if KSTOP <= 1:
                break
            wslots = [ph.bf16(16 * 512).rearrange("p (a b) -> p a b", b=512) for _ in range(3)]
            stg = [ph.f32(512) for _ in range(4)]
            stg_i = [0]
            groups = [(5120, 512, "plain"), (5632, 64, "plain")]
            groups += [(c0, 512, "plain") for c0 in (2048, 2560)]
            groups += [(0, 512, "plain"), (512, 512, "plain"), (1024, 512, "silu"), (1536, 512, "plain"),
                       (3072, 512, "silu"), (3584, 512, "plain"), (4096, 512, "plain"), (4608, 512, "plain"),
                       (5696, 512, "zc"), (6208, 512, "zc")]
            for gi, (c0, ncols, kind) in enumerate(groups):
                slot = gi % 3
                wres = "wslot%d" % slot

                def epi(t, pap, pres, c0=c0, ncols=ncols, kind=kind):
                    s = stg_i[0]; stg_i[0] = (s + 1) % 4
                    sres = "stg%d" % s
                    if kind == "plain":
                        evac(stg[s][:, 0:ncols], pap, [pres], [sres])
                        dma("sp", proj_d[t * 128:(t + 1) * 128, c0:c0 + ncols], stg[s][:, 0:ncols], [sres], [], "st%d" % s)
                    elif kind == "silu":
                        act(stg[s][:, 0:ncols], pap, AF.Silu, [pres], [sres])
                        dma("sp", proj_d[t * 128:(t + 1) * 128, c0:c0 + ncols], stg[s][:, 0:ncols], [sres], [], "st%d" % s)
                    else:
                        fc, qb = t
                        act(stg[s][:, :], pap, AF.Silu, [pres], [sres])
                        r0 = (c0 - 5696) + fc * 128
                        dma("sp", zct_d[r0:r0 + 128, qb * 512:(qb + 1) * 512], stg[s][:, :], [sres], [], "st%d" % s)

                gemm(win_d[l, :, c0:c0 + ncols], 16, ncols, wslots[slot], wres, A, a_res, epi, yform=(kind == "zc"))
            barrier()

            if KSTOP <= 2:
                break
            rowres = {}
            for h in range(8):
                rowres.setdefault((K_OFF + h * 128) // CH, []).append("gin.k%d" % h)
                rowres.setdefault((V_OFF + h * 128) // CH, []).append("gin.v%d" % h)
            rowres.setdefault(KR_OFF // CH, []).append("gin.kr")
            rowres.setdefault(HALO // CH, []).append("gin.h0")
            rowres.setdefault((HALO + 1) // CH, []).append("gin.h1")
            cc_state = {"written": set(), "issued": set(), "pending": [], "tick": 0, "last": -100}

            def cc_written(res):
                cc_state["written"].add(res)
                for k in range(NCH):
                    if k not in cc_state["issued"] and k not in cc_state["pending"] and all(x in cc_state["written"] for x in rowres[k]):
                        cc_state["pending"].append(k)

            def cc_issue(k):
                cc_state["issued"].add(k)
                P.op("pool", lambda e, k=k: e.collective_compute("AllGather", ALU.bypass, replica_groups=[[0, 1, 2, 3], [4, 5, 6, 7]],
                                                                ins=[gin_ts[k].ap()], outs=[gout_ts[k].ap()]),
                     r=rowres[k], w=["gout.%d" % k, "ccserial"], stream="cc%d" % k, inc=1)

            def cc_tick(spacing):
                cc_state["tick"] += 1
                if cc_state["pending"] and cc_state["tick"] - cc_state["last"] >= spacing:
                    cc_state["last"] = cc_state["tick"]
                    cc_issue(cc_state["pending"].pop(0))

            def cc_flush_ready():
                pass

            def cc_flush_all():
                while cc_state["pending"]:
                    cc_issue(cc_state["pending"].pop(0))
                assert len(cc_state["issued"]) == NCH

            ckvT = ph.bf16(4 * TOK).rearrange("p (a b) -> p a b", b=TOK)
            krTs = ph.bf16(TOK)
            ck = [ph.f32(576) for _ in range(2)]
            ckn = [ph.bf16(512) for _ in range(2)]
            junk = ph.bf16(512)
            st = ph.f32(16)
            krn = [ph.f32(64) for _ in range(2)]
            rt = [ph.f32(32) for _ in range(4)]
            krr = [ph.bf16(64) for _ in range(2)]
            hb = [ph.f32(1024) for _ in range(2)]
            hx = [ph.bf16(512) for _ in range(2)]
            for i, t in enumerate((0, NT - 1)):
                dma("sp", hb[i], proj_d[t * 128:(t + 1) * 128, 2048:3072], [], ["hb%d" % i], "lhb%d" % i)
                tt("pool", hx[i], hb[i][:, 0:512], hb[i][:, 512:1024], ALU.mult, ["hb%d" % i], ["hx%d" % i])
            dma("sp", gin_rows(HALO, 1)[:, 0:512], hx[0][0:1, :], ["hx0"], ["gin.h0"], "st0")
            dma("sp", gin_rows(HALO + 1, 1)[:, 0:512], hx[1][127:128, :], ["hx1"], ["gin.h1"], "st1")
            cc_written("gin.h0"); cc_written("gin.h1")
            for t in range(NT):
                b = t % 2
                dma("sp", ck[b], proj_d[t * 128:(t + 1) * 128, 5120:5696], [], ["ck%d" % b], "ldh%d" % b)
                act(junk, ck[b][:, 0:512], AF.Square, ["ck%d" % b], ["junk", "sq%d" % b], accum=st[:, b:b + 1])
                act(junk[:, 0:64], ck[b][:, 512:576], AF.Square, ["ck%d" % b], ["junk", "sr%d" % b], accum=st[:, 2 + b:3 + b])
                rstd_of((st[:, b:b + 1], "sq%d" % b), 512, None, 1, (st[:, 4 + b:5 + b], "t1%d" % b), (st[:, 6 + b:7 + b], "r1%d" % b))
                rstd_of((st[:, 2 + b:3 + b], "sr%d" % b), 64, None, 1, (st[:, 8 + b:9 + b], "t2%d" % b), (st[:, 10 + b:11 + b], "r2%d" % b))
                stt(ckn[b], ck[b][:, 0:512], st[:, 6 + b:7 + b], vec_t[:, V_KV:V_KV + 512], ALU.mult, ALU.mult,
                    ["ck%d" % b, "r1%d" % b, "vec"], ["ckn%d" % b])
                stt(krn[b], ck[b][:, 512:576], st[:, 10 + b:11 + b], vec_t[:, V_KR:V_KR + 64], ALU.mult, ALU.mult,
                    ["ck%d" % b, "r2%d" % b, "vec"], ["krn%d" % b])
                x1 = krn[b][:, 0:32]; x2 = krn[b][:, 32:64]
                Ct = cosT[:, t, :]; St = sinT[:, t, :]
                tt("pool", rt[0], x1, Ct, ALU.mult, ["krn%d" % b, "tab1"], ["rt0"])
                tt("pool", rt[1], x2, St, ALU.mult, ["krn%d" % b, "tab0"], ["rt1"])
                tt("pool", krr[b][:, 0:32], rt[0], rt[1], ALU.subtract, ["rt0", "rt1"], ["krr%d" % b])
                tt("pool", rt[2], x2, Ct, ALU.mult, ["krn%d" % b, "tab1"], ["rt2"])
                tt("pool", rt[3], x1, St, ALU.mult, ["krn%d" % b, "tab0"], ["rt3"])
                tt("pool", krr[b][:, 32:64], rt[2], rt[3], ALU.add, ["rt2", "rt3", "krr%d" % b], ["krr%d" % b])
                for j in range(4):
                    tr(psb(4)[:, j * 128:(j + 1) * 128], ckn[b][:, j * 128:(j + 1) * 128], ["ckn%d" % b, "ident"], ["ps4"])
                evac(ckvT[:, :, t * 128:(t + 1) * 128], psb(4)[:, 0:512].rearrange("p (a b) -> p a b", b=128), ["ps4"], ["ckvT.%d" % t])
                tr(psb(5)[0:64, 0:128], krr[b][:, 0:64], ["krr%d" % b, "ident"], ["ps5"])
                evac(krTs[0:64, t * 128:(t + 1) * 128], psb(5)[0:64, 0:128], ["ps5"], ["krTs"])
            dma("sp", gin_rows(KR_OFF, 64), krTs[0:64, :], ["krTs"], ["gin.kr"], "st2")
            cc_written("gin.kr")
            wuk = [ph.bf16(4 * 512).rearrange("p (a b) -> p a b", b=512) for _ in range(2)]
            kst = [ph.bf16(2 * TOK).rearrange("p (a b) -> p a b", b=TOK) for _ in range(2)]
            vst = [ph.bf16(2 * TOK).rearrange("p (a b) -> p a b", b=TOK) for _ in range(2)]
            knb = [ph.bf16(128) for _ in range(4)]
            kq = ph.f32(12)
            kn_i = [0]
            for g in range(4):
                sl = g % 2

                def epi(t, pap, pres, g=g, sl=sl):
                    cc_tick(10)
                    for hh in range(2):
                        i = kn_i[0]; kn_i[0] = (i + 1) % 4
                        kps = pap[:, hh * 256:hh * 256 + 128]
                        vps = pap[:, hh * 256 + 128:hh * 256 + 256]
                        act(junk[:, 0:128], kps, AF.Square, [pres], ["junk", "ks%d" % i], accum=kq[:, i:i + 1])
                        rstd_of((kq[:, i:i + 1], "ks%d" % i), 128, None, 1, (kq[:, 4 + i:5 + i], "kt%d" % i), (kq[:, 8 + i:9 + i], "kr%d" % i))
                        stt(knb[i], kps, kq[:, 8 + i:9 + i], vec_t[:, V_KN:V_KN + 128], ALU.mult, ALU.mult,
                            [pres, "kr%d" % i, "vec"], ["knb%d" % i])
                        cp("act", vst[sl][:, hh, t * 128:(t + 1) * 128], vps, [pres], ["vst%d" % sl])
                        bank = 4 + i
                        tr(psb(bank)[:, 0:128], knb[i], ["knb%d" % i, "ident"], ["ps%d" % bank])
                        evac(kst[sl][:, hh, t * 128:(t + 1) * 128], psb(bank)[:, 0:128], ["ps%d" % bank], ["kst%d" % sl])

                gemm(wukv_d[l, :, g * 512:(g + 1) * 512], 4, 512, wuk[sl], "wuk%d" % sl, ckvT, lambda t: ["ckvT.%d" % t], epi)
                for hh in range(2):
                    h = 2 * g + hh
                    dma("sp", gin_rows(K_OFF + h * 128, 128), kst[sl][:, hh, :], ["kst%d" % sl], ["gin.k%d" % h], "sk%d" % sl)
                    dma("sp", gin_rows(V_OFF + h * 128, 128), vst[sl][:, hh, :], ["vst%d" % sl], ["gin.v%d" % h], "sv%d" % sl)
                    cc_written("gin.k%d" % h); cc_written("gin.v%d" % h)
            if KSTOP <= 3:
                break
            cc_flush_ready()
            barrier()

            if KSTOP <= 4:
                break
            uvz = [ph.f32(1536) for _ in range(2)]
            vsq = [ph.f32(512) for _ in range(2)]
            sa_ = [ph.f32(16) for _ in range(2)]
            vn = [ph.bf16(512) for _ in range(2)]
            t1_ = [ph.f32(512) for _ in range(2)]
            ya_ = [ph.f32(512) for _ in range(2)]
            junk_ = [ph.bf16(512) for _ in range(2)]
            yan = [ph.bf16(512) for _ in range(2)]
            def a_s1(t):
                b = t % 2
                cc_tick(4)
                sa = sa_[b]; t1 = t1_[b]; ya = ya_[b]; junk = junk_[b]; B_ = str(b)
                U = uvz[b][:, 0:512]; Vv = uvz[b][:, 512:1024]; Z = uvz[b][:, 1024:1536]
                dma("sp", uvz[b], proj_d[t * 128:(t + 1) * 128, 0:1536], [], ["uvz%d" % b], "ldh%d" % b)
                tt("pool", vsq[b], Vv, Vv, ALU.mult, ["uvz%d" % b], ["vsq" + B_])
                red(sa[:, 0:4], vsq[b].rearrange("p (a b) -> p a b", b=128), ["vsq" + B_], ["sa0" + B_])
                rstd_of((sa[:, 0:4], "sa0" + B_), 128, None, 4, (sa[:, 4:8], "sa1" + B_), (sa[:, 8:12], "sa2" + B_))
                for h in range(4):
                    stt(vn[b][:, h * 128:(h + 1) * 128], Vv[:, h * 128:(h + 1) * 128], sa[:, 8 + h:9 + h],
                        vec_t[:, V_SGU + h * 128:V_SGU + (h + 1) * 128], ALU.mult, ALU.mult, ["uvz%d" % b, "sa2" + B_, "vec"], ["vn%d" % b])
                for h in range(4):
                    mm(ps[b][:, h * 128:(h + 1) * 128], wsp_b[:, h, :], vn[b][:, h * 128:(h + 1) * 128], True, True, ["wsp_b", "vn%d" % b], ["ps" + B_])

            def a_s2(t):
                b = t % 2
                sa = sa_[b]; t1 = t1_[b]; ya = ya_[b]; junk = junk_[b]; B_ = str(b)
                U = uvz[b][:, 0:512]; Vv = uvz[b][:, 512:1024]; Z = uvz[b][:, 1024:1536]
                for h in range(4):
                    stt(t1[:, h * 128:(h + 1) * 128], ps[b][:, h * 128:(h + 1) * 128], cols_t[:, 8 + h:9 + h], U[:, h * 128:(h + 1) * 128],
                        ALU.add, ALU.mult, ["ps" + B_, "cols", "uvz%d" % b], ["t1" + B_])
                tt("pool", ya, t1, Z, ALU.mult, ["t1" + B_, "uvz%d" % b], ["ya" + B_])
                act(junk, ya, AF.Square, ["ya" + B_], ["junk" + B_, "sa3" + B_], accum=sa[:, 12:13])
                rstd_of((sa[:, 12:13], "sa3" + B_), 512, None, 1, (sa[:, 13:14], "sa4" + B_), (sa[:, 14:15], "sa5" + B_))
                stt(yan[b], ya, sa[:, 14:15], vec_t[:, V_OA:V_OA + 512], ALU.mult, ALU.mult, ["ya" + B_, "sa5" + B_, "vec"], ["yan%d" % b])
                for j in range(4):
                    tr(psb(4 + b)[:, j * 128:(j + 1) * 128], yan[b][:, j * 128:(j + 1) * 128], ["yan%d" % b, "ident"], ["ps%d" % (4 + b)])
                evac(A[:, 0:4, t * 128:(t + 1) * 128], psb(4 + b)[:, 0:512].rearrange("p (a b) -> p a b", b=128), ["ps%d" % (4 + b)], ["A.%d.0" % t])

            for t in range(NT + 1):
                if t < NT:
                    a_s1(t)
                if t >= 1:
                    a_s2(t - 1)
            barrier()

            if KSTOP <= 5:
                break
            cc_flush_all()
            for rk in range(4):
                dma("sp", G_b[2 * rk:2 * rk + 2, :], gout_rows(rk, HALO, 2)[:, 0:512],
                    [gres(HALO)], ["G_b.%d" % rk], "lgb%d" % rk)
            b4 = [ph.f32(2048) for _ in range(3)]
            xf = [ph.f32(512) for _ in range(3)]
            xb = [ph.bf16(512) for _ in range(3)]
            cvs = [[ph.f32(512) for _ in range(4)] for _ in range(2)]
            junk_ = [ph.bf16(512) for _ in range(2)]
            sbs = [ph.f32(4) for _ in range(2)]
            ybn = [ph.bf16(512) for _ in range(2)]
            for t in range(NT + 1):
                if t < NT:
                    b = t % 3
                    dma("sp", b4[b], proj_d[t * 128:(t + 1) * 128, 1536:3584], [], ["b4%d" % b], "ldb%d" % b)
                    tt("pool", xf[b], b4[b][:, 512:1024], b4[b][:, 1024:1536], ALU.mult, ["b4%d" % b], ["xf%d" % b])
                    cp("act", xb[b], xf[b], ["xf%d" % b], ["xb%d" % b])
                j = t - 1
                if j < 0:
                    continue
                jb = j % 3; pb_ = (j - 1) % 3; nb_ = (j + 1) % 3
                o = j % 2; O_ = str(o)
                cv, ca, cb2, yb = cvs[o]; junk = junk_[o]; sb_ = sbs[o]
                b1, b2 = (1, 2) if o == 0 else (6, 7)
                p1 = "ps%d" % b1; p2 = "ps%d" % b2
                mm(ps[b1][:, :], shm[:, 0, :], xb[jb], True, False, ["shm", "xb%d" % jb], [p1])
                if j > 0:
                    mm(ps[b1][:, :], shm[:, 2, :], xb[pb_], False, True, ["shm", "xb%d" % pb_], [p1])
                else:
                    mm(ps[b1][:, :], sel_b[0:8, 0:128], G_b[0:8, :], False, True, ["sel_b", "G_b.0", "G_b.1", "G_b.2", "G_b.3"], [p1])
                mm(ps[b2][:, :], shm[:, 1, :], xb[jb], True, False, ["shm", "xb%d" % jb], [p2])
                if j < NT - 1:
                    mm(ps[b2][:, :], shm[:, 3, :], xb[nb_], False, True, ["shm", "xb%d" % nb_], [p2])
                else:
                    mm(ps[b2][:, :], sel_b[0:8, 128:256], G_b[0:8, :], False, True, ["sel_b", "G_b.0", "G_b.1", "G_b.2", "G_b.3"], [p2])
                tt("pool", cv, xf[jb], vec_t[:, V_CW1:V_CW1 + 512], ALU.mult, ["xf%d" % jb, "vec"], ["cv" + O_])
                tt("pool", cv, cv, vec_t[:, V_CB:V_CB + 512], ALU.add, ["cv" + O_, "vec"], ["cv" + O_])
                tt("dve", ca, ps[b1][:, :], vec_t[:, V_CW0:V_CW0 + 512], ALU.mult, [p1, "vec"], ["ca" + O_])
                tt("dve", cb2, ps[b2][:, :], vec_t[:, V_CW2:V_CW2 + 512], ALU.mult, [p2, "vec"], ["cb2" + O_])
                tt("dve", cv, cv, ca, ALU.add, ["cv" + O_, "ca" + O_], ["cv" + O_])
                tt("dve", cv, cv, cb2, ALU.add, ["cv" + O_, "cb2" + O_], ["cv" + O_])
                tt("dve", yb, cv, b4[jb][:, 0:512], ALU.mult, ["cv" + O_, "b4%d" % jb], ["yb" + O_])
                tt("pool", yb, yb, b4[jb][:, 1536:2048], ALU.mult, ["yb" + O_, "b4%d" % jb], ["yb" + O_])
                act(junk, yb, AF.Square, ["yb" + O_], ["junk" + O_, "sb0" + O_], accum=sb_[:, 0:1])
                rstd_of((sb_[:, 0:1], "sb0" + O_), 512, None, 1, (sb_[:, 1:2], "sb1" + O_), (sb_[:, 2:3], "sb2" + O_))
                stt(ybn[o], yb, sb_[:, 2:3], vec_t[:, V_OB:V_OB + 512], ALU.mult, ALU.mult, ["yb" + O_, "sb2" + O_, "vec"], ["ybn%d" % o])
                for q in range(4):
                    tr(psb(4 + o)[:, q * 128:(q + 1) * 128], ybn[o][:, q * 128:(q + 1) * 128], ["ybn%d" % o, "ident"], ["ps%d" % (4 + o)])
                evac(A[:, 4:8, j * 128:(j + 1) * 128], psb(4 + o)[:, 0:512].rearrange("p (a b) -> p a b", b=128), ["ps%d" % (4 + o)], ["A.%d.1" % j])
            barrier()

            if KSTOP <= 6:
                break
            krT_all = ph.bf16(SEQ)
            for rk in range(4):
                dma("sp", krT_all[0:64, rk * TOK:(rk + 1) * TOK], gout_rows(rk, KR_OFF, 64), [gres(KR_OFF)], ["krT.%d" % rk], "lkr%d" % rk)
            P.op("pool", lambda e: e.memset(krT_all[64:128, :], 0.0), w=["krT.pad"])
            kTr = [ph.bf16(TOK) for _ in range(3)]
            Vr = [ph.bf16(TOK).rearrange("p (a b) -> p a b", b=128) for _ in range(3)]
            qt = ph.f32(1536)
            qsq = ph.f32(1536)
            qs = ph.f32(48)
            qnb = ph.bf16(1024)
            qrf = ph.f32(512)
            qrb = ph.bf16(512)
            r4 = [qsq[:, i * 256:(i + 1) * 256] for i in range(4)]
            QnT = ph.bf16(8 * 512).rearrange("p (a b) -> p a b", b=512)
            QrT = ph.bf16(8 * 512).rearrange("p (a b) -> p a b", b=512)
            P.op("pool", lambda e: e.memset(QrT[64:128, :, :], 0.0), w=["QrT.pad"])
            zc = [ph.f32(512) for _ in range(2)]
            yc = ph.f32(8 * 512).rearrange("p (a b) -> p a b", b=512)
            pT = [ph.bf16(512) for _ in range(8)]
            d4 = ph.f32(512)
            rden = ph.f32(512)
            ysq = rden
            rc = qsq[:, 0:512]; rc2 = qsq[:, 512:1024]
            for qb in range(NB):
                for ti in range(4):
                    t = qb * 4 + ti
                    dma("sp", qt, proj_d[t * 128:(t + 1) * 128, 3584:5120], [], ["qt"], "ldq")
                    q3 = qt.rearrange("p (h d) -> p h d", d=192)
                    s3 = qsq.rearrange("p (h d) -> p h d", d=192)
                    tt("dve", qsq, qt, qt, ALU.mult, ["qt"], ["qsq", "r40", "r41", "r42", "r43"])
                    red(qs[:, 0:8], s3[:, :, 0:128], ["qsq"], ["qs0"])
                    red(qs[:, 8:16], s3[:, :, 128:192], ["qsq"], ["qs1"])
                    rstd_of((qs[:, 0:8], "qs0"), 128, None, 8, (qs[:, 16:24], "qs2"), (qs[:, 32:40], "qs4"))
                    rstd_of((qs[:, 8:16], "qs1"), 64, None, 8, (qs[:, 24:32], "qs3"), (qs[:, 40:48], "qs5"))
                    for h in range(8):
                        stt(qnb[:, h * 128:(h + 1) * 128], q3[:, h, 0:128], qs[:, 32 + h:33 + h], vec_t[:, V_QN:V_QN + 128],
                            ALU.mult, ALU.mult, ["qt", "qs4", "vec"], ["qnb"])
                        stt(qrf[:, h * 64:(h + 1) * 64], q3[:, h, 128:192], qs[:, 40 + h:41 + h], vec_t[:, V_QR:V_QR + 64],
                            ALU.mult, ALU.mult, ["qt", "qs5", "vec"], ["qrf"])
                    f3 = qrf.rearrange("p (h d) -> p h d", d=64)
                    o3 = qrb.rearrange("p (h d) -> p h d", d=64)
                    x1 = f3[:, :, 0:32]; x2 = f3[:, :, 32:64]
                    Ct = cosT[:, t, :].unsqueeze(1).to_broadcast([128, 8, 32])
                    St = sinT[:, t, :].unsqueeze(1).to_broadcast([128, 8, 32])
                    rv = [r.rearrange("p (h d) -> p h d", d=32) for r in r4]
                    tt("pool", rv[0], x1, Ct, ALU.mult, ["qrf", "tab1", "qs0", "qs1"], ["r40"])
                    tt("pool", rv[1], x2, St, ALU.mult, ["qrf", "tab0", "qs0", "qs1"], ["r41"])
                    tt("pool", o3[:, :, 0:32], rv[0], rv[1], ALU.subtract, ["r40", "r41"], ["qrb.a"])
                    tt("dve", rv[2], x2, Ct, ALU.mult, ["qrf", "tab1", "qs0", "qs1"], ["r42"])
                    tt("dve", rv[3], x1, St, ALU.mult, ["qrf", "tab0", "qs0", "qs1"], ["r43"])
                    tt("dve", o3[:, :, 32:64], rv[2], rv[3], ALU.add, ["r42", "r43"], ["qrb.b"])
                    for hg in range(2):
                        for j in range(4):
                            h = hg * 4 + j
                            tr(psb(4)[:, j * 128:(j + 1) * 128], qnb[:, h * 128:(h + 1) * 128], ["qnb", "ident"], ["ps4"])
                        evac(QnT[:, hg * 4:hg * 4 + 4, ti * 128:(ti + 1) * 128], psb(4)[:, 0:512].rearrange("p (a b) -> p a b", b=128),
                             ["ps4"], ["QnT"])
                    for h in range(8):
                        tr(psb(5)[0:64, h * 128:(h + 1) * 128], qrb[:, h * 64:(h + 1) * 64], ["qrb.a", "qrb.b", "ident"], ["ps5"])
                    evac(QrT[0:64, :, ti * 128:(ti + 1) * 128], psb(5)[0:64, 0:1024].rearrange("p (a b) -> p a b", b=128), ["ps5"], ["QrT"])
                its = [(h, rk, kc) for h in range(8) for rk in range(4) for kc in range(VC)]
                segs = [(h, rk) for h in range(8) for rk in range(4)]

                def load_seg(si):
                    h, rk = segs[si]
                    s = si % 3
                    dma("sp", kTr[s], gout_rows(rk, K_OFF + h * 128, 128), [gres(K_OFF + h * 128)], ["kTr%d" % s], "lk%d" % s)
                    dma("sp", Vr[s], gout_rows(rk, V_OFF + h * 128, 128).rearrange("p (a b) -> p a b", b=128),
                        [gres(V_OFF + h * 128)], ["Vr%d" % s], "lv%d" % s)

                SB = (0, 1, 2, 5)

                def s_mm(ii):
                    h, rk, kc = its[ii]
                    si = h * 4 + rk
                    s = si % 3
                    bank = SB[ii % 4]
                    mm(ps[bank][:, :], kTr[s][:, kc * 128:(kc + 1) * 128], QnT[:, h, :], True, False, ["kTr%d" % s, "QnT"], ["ps%d" % bank])
                    mm(ps[bank][:, :], krT_all[:, rk * TOK + kc * 128:rk * TOK + (kc + 1) * 128], QrT[:, h, :], False, True,
                       ["krT.%d" % rk, "krT.pad", "QrT", "QrT.pad"], ["ps%d" % bank])

                load_seg(0); load_seg(1)
                nit = len(its)
                s_mm(0); s_mm(1)
                for ii in range(nit):
                    h, rk, kc = its[ii]
                    si = h * 4 + rk
                    s = si % 3
                    if kc == 0:
                        if si + 2 < len(segs):
                            load_seg(si + 2)
                        if rk == 0:
                            z = h % 2
                            dma("sp", zc[z], zct_d[h * 128:(h + 1) * 128, qb * 512:(qb + 1) * 512], [], ["zc%d" % z], "lz%d" % z)
                    if ii + 2 < nit:
                        s_mm(ii + 2)
                    bank = SB[ii % 4]
                    pi_ = ii % 8
                    act(pT[pi_], ps[bank][:, :], AF.Exp, ["ps%d" % bank], ["pT%d" % pi_], scale=SC)
                    first = (rk == 0 and kc == 0); last = (rk == 3 and kc == VC - 1)
                    mm(ps[3][:, :], Vr[s][:, kc, :], pT[pi_], first, last, ["Vr%d" % s, "pT%d" % pi_], ["ps3"])
                    if ii % 4 == 3:
                        gfirst = (rk == 0 and kc == 3); glast = last
                        for j in range(4):
                            pj = (ii - 3 + j) % 8
                            P.op("pe", lambda e, j=j, pj=pj, gfirst=gfirst, glast=glast:
                                 e.matmul(ps[6][32 * j:32 * j + 32, :], lhsT=ones_b[:, 0:32], rhs=pT[pj], start=gfirst, stop=glast,
                                          tile_position=(0, 32 * j)),
                                 r=["ones_b", "pT%d" % pj], w=["ps6"])
                    if last:
                        z = h % 2
                        cp("dve", d4, ps[6][:, :], ["ps6"], ["d4"])
                        mm(ps[4][:, :], ones_f[:], d4, True, True, ["ones_f", "d4"], ["ps4"])
                        P.op("dve", lambda e: e.reciprocal(out=rden, in_=ps[4][:, :]), r=["ps4"], w=["rden"])
                        stt(yc[:, h, :], ps[3][:, :], 32.0, rden, ALU.mult, ALU.mult, ["ps3", "rden"], ["yc%d" % h])
                        tt("pool", yc[:, h, :], yc[:, h, :], zc[z], ALU.mult, ["yc%d" % h, "zc%d" % z], ["yc%d" % h])
                        tt("pool", ysq, yc[:, h, :], yc[:, h, :], ALU.mult, ["yc%d" % h], ["rden"])
                        mm(ps[7][:, :], ones_f[:], ysq, h == 0, h == 7, ["ones_f", "rden"], ["ps7"])
                ts("dve", rc, ps[7][:, :], 1.0 / 1024, EPS, ALU.mult, ALU.add, ["ps7"], ["qsq", "r40", "r41", "r42", "r43"])
                act(rc2, rc, AF.Sqrt, ["qsq"], ["qsq"])
                P.op("dve", lambda e: e.reciprocal(out=rc2, in_=rc2), r=["qsq"], w=["qsq"])
                for h in range(8):
                    stt(A[:, 8 + h, qb * 512:(qb + 1) * 512], yc[:, h, :], cols_t[:, h:h + 1], rc2, ALU.mult, ALU.mult,
                        ["yc%d" % h, "cols", "qsq"], ["A.%d.%d" % (4 * qb + i, 2 + h // 4) for i in range(4)])
            barrier()

            if KSTOP <= 7:
                break
            wslots = [ph.bf16(16 * 512).rearrange("p (a b) -> p a b", b=512) for _ in range(2)]
            hsl = [ph.f32(512) for _ in range(4)]
            hi = [0]
            for g in range(4):
                sl = g % 2

                def epi(t, pap, pres, g=g):
                    s = hi[0]; hi[0] = (s + 1) % 4
                    dma("act", hsl[s], hsrc[t * 128:(t + 1) * 128, g * 512:(g + 1) * 512], [], ["hsl%d" % s], "lh%d" % s)
                    tt("dve", hsl[s], pap, hsl[s], ALU.add, [pres, "hsl%d" % s], ["hsl%d" % s])
                    dma("sp", hbuf_d[t * 128:(t + 1) * 128, g * 512:(g + 1) * 512], hsl[s], ["hsl%d" % s], [], "sh%d" % s)

                gemm(wout_d[l, :, g * 512:(g + 1) * 512], 16, 512, wslots[sl], "wslot%d" % sl, A, a_res, epi)
            barrier()

            if KSTOP <= 8:
                break
            norm_transpose(l, hbuf_d, 1)
            pTb = ph.bf16(2 * TOK).rearrange("p (a b) -> p a b", b=TOK)
            pf = [ph.f32(256) for _ in range(2)]
            pbf = [ph.bf16(256) for _ in range(2)]
            for t in range(NT):
                b = t % 2
                dma("sp", pf[b], p_d[l, t * 128:(t + 1) * 128, :], [], ["pf%d" % b], "lp%d" % b)
                cp("pool", pbf[b], pf[b], ["pf%d" % b], ["pbf%d" % b])
                for j in range(2):
                    tr(psb(6)[:, j * 128:(j + 1) * 128], pbf[b][:, j * 128:(j + 1) * 128], ["pbf%d" % b, "ident"], ["ps6"])
                evac(pTb[:, :, t * 128:(t + 1) * 128], psb(6)[:, 0:256].rearrange("p (a b) -> p a b", b=128), ["ps6"], ["pTb.%d" % t])
            wslots = [ph.bf16(16 * 512).rearrange("p (a b) -> p a b", b=512) for _ in range(2)]
            wpp = ph.bf16(2 * D).rearrange("p (a b) -> p a b", b=D)
            P.op("pool", lambda e, l=l: e.dma_start(out=wpp, in_=wpp_d[l, :, :].rearrange("(kc p) n -> p kc n", p=128)), r=[], w=["wpp"], stream="w.wpp")
            hsl = [ph.f32(512) for _ in range(4)]
            gt = [ph.f32(512) for _ in range(2)]
            gi_ = [0]
            for g in range(4):
                sl = g % 2

                def epi(t, pap, pres, g=g):
                    s = hi[0]; hi[0] = (s + 1) % 4
                    k = gi_[0]; gi_[0] ^= 1
                    pbank = 6 + k
                    for kc in range(2):
                        mm(ps[pbank][:, :], pTb[:, kc, t * 128:(t + 1) * 128], wpp[:, kc, g * 512:(g + 1) * 512], kc == 0, kc == 1,
                           ["pTb.%d" % t, "wpp"], ["ps%d" % pbank])
                    dma("act", hsl[s], hbuf_d[t * 128:(t + 1) * 128, g * 512:(g + 1) * 512], [], ["hsl%d" % s], "lh%d" % s)
                    act(gt[k], pap, AF.Sigmoid, [pres], ["gt%d" % k])
                    tt("dve", gt[k], gt[k], ps[pbank][:, :], ALU.mult, ["gt%d" % k, "ps%d" % pbank], ["gt%d" % k])
                    tt("pool", hsl[s], hsl[s], gt[k], ALU.add, ["hsl%d" % s, "gt%d" % k], ["hsl%d" % s])
                    dma("sp", hdst[t * 128:(t + 1) * 128, g * 512:(g + 1) * 512], hsl[s], ["hsl%d" % s], [], "sh%d" % s)

                gemm(wgate_d[l, :, g * 512:(g + 1) * 512], 16, 512, wslots[sl], "wslot%d" % sl, A, a_res, epi)
            barrier()

        streams = sorted(P.stream_last.keys())
        sems = {}
        for e in Prog.ENGS:
            sems[e] = es.enter_context(nc.semaphore("sem_" + e))
        ssem = {}
        for i, s in enumerate(streams):
            ssem[s] = es.enter_context(nc.semaphore("ds%d" % i))
        P.emit(nc, sems, ssem)
    return nc


_CACHE = {}


def _host_inputs(L, TOK, x, p, positions, attn_norm, w_in, sgu_norm, w_spatial, b_spatial, conv_w, conv_b,
                 kv_norm, w_ukv, q_nope_norm, q_rope_norm, k_nope_norm, k_rope_norm,
                 out_norm, w_out, ple_norm, w_ple_gate, w_ple_proj):
    f = np.float32
    NT = TOK // 128
    a = lambda v: np.ascontiguousarray(np.asarray(v))
    x = a(x); p = a(p); positions = a(positions)
    w_in = a(w_in)[:L]; w_ukv = a(w_ukv)[:L]; w_out = a(w_out)[:L]; w_gate = a(w_ple_gate)[:L]; w_pp = a(w_ple_proj)[:L]
    w_spT = a(np.transpose(np.asarray(w_spatial)[:L], (0, 3, 1, 2)).reshape(L, 128, 512))
    out_norm = np.asarray(out_norm)[:L]
    vec = np.concatenate([out_norm[:, 0:1024], np.asarray(sgu_norm)[:L].reshape(L, 512), np.asarray(conv_w)[:L].reshape(L, 1536),
                          np.asarray(conv_b)[:L], np.asarray(kv_norm)[:L], np.asarray(q_nope_norm)[:L], np.asarray(q_rope_norm)[:L],
                          np.asarray(k_nope_norm)[:L], np.asarray(k_rope_norm)[:L]], axis=1).astype(f)
    assert vec.shape[1] == NV
    gn = a(np.stack([np.asarray(attn_norm)[:L], np.asarray(ple_norm)[:L]], axis=1).astype(f))
    gc = np.transpose(out_norm[:, 1024:2048].reshape(L, 8, 128), (0, 2, 1))
    bs = np.transpose(np.asarray(b_spatial)[:L], (0, 2, 1))
    cols = a(np.concatenate([gc, bs], axis=2).astype(f))
    cmat = np.zeros((128, 640), f)
    cmat[:, 0:128] = np.eye(128, dtype=f)
    for m in range(127):
        cmat[m, 128 + m + 1] = 1.0
        cmat[m + 1, 256 + m] = 1.0
    cmat[127, 384 + 0] = 1.0
    cmat[0, 512 + 127] = 1.0
    invf = (1.0 / (10000.0 ** (np.arange(0, 64, 2, dtype=f) / f(64)))).astype(f)
    invf = a(np.broadcast_to(invf[None, :], (128, 32)))
    shared = dict(w_in=w_in, w_ukv=w_ukv, w_out=w_out, w_gate=w_gate, w_pp=w_pp, w_spT=w_spT, vec=vec, gn=gn, cols=cols,
                  cmat=cmat, invf=invf)
    maps = []
    for c in range(8):
        b, r = c // 4, c % 4
        sel = np.zeros((8, 256), f)
        if r > 0:
            sel[2 * (r - 1) + 1, 0] = 1.0
        if r < 3:
            sel[2 * (r + 1), 128 + 127] = 1.0
        m = dict(shared)
        m["x"] = a(x[b, r * TOK:(r + 1) * TOK, :])
        m["p"] = a(p[:L, b, r * TOK:(r + 1) * TOK, :])
        m["pos"] = a(positions[b, r * TOK:(r + 1) * TOK].reshape(NT, 128).T.astype(np.int32))
        m["sel"] = sel
        maps.append(m)
    return maps


def run(L, TOK, inputs, trace=False):
    key = (L, TOK)
    if key not in _CACHE:
        _CACHE[key] = build(L, TOK)
    nc = _CACHE[key]
    maps = _host_inputs(L, TOK, **inputs)
    res = run_bass_kernel_spmd(nc, maps, core_ids=list(range(8)), trace=trace)
    S = 4 * TOK
    out = np.empty((2, S, D), np.float32)
    for c in range(8):
        b, r = c // 4, c % 4
        out[b, r * TOK:(r + 1) * TOK, :] = res.results[c]["out"]
    return out, res


def kernel(**inputs):
    out, _ = run(4, 2048, inputs)
    return out
```

```python
import contextlib
import numpy as np
import concourse.bass as bass
import concourse.mybir as mybir
from concourse.bass_utils import run_bass_kernel_spmd

F32 = mybir.dt.float32
BF16 = mybir.dt.bfloat16
I32 = mybir.dt.int32
AF = mybir.ActivationFunctionType
ALU = mybir.AluOpType
AX = mybir.AxisListType

D = 2048
INW = 6720
NPROJ = 5696
EPS = 1e-6
PI = float(np.pi)
SC = float(192 ** -0.5)
V_OA, V_OB, V_SGU, V_CW0, V_CW1, V_CW2, V_CB, V_KV, V_QN, V_QR, V_KN, V_KR, NV = (
    0, 512, 1024, 1536, 2048, 2560, 3072, 3584, 4096, 4224, 4288, 4416, 4480)


class Ins:
    __slots__ = ("idx", "eng", "fn", "waits", "stream", "inc", "signal", "sigval")


class Prog:
    ENGS = ("pe", "act", "dve", "pool", "sp")

    def __init__(self):
        self.ins = []
        self.last_w = {}
        self.readers = {}
        self.stream_last = {}
        self.stream_inc = {}
        self.eng_last = {e: None for e in self.ENGS}
        self.waited = {e: {} for e in self.ENGS}
        self.pending = {e: [] for e in self.ENGS}

    def _key(self, p):
        return ("s", p.stream) if p.stream is not None else ("e", p.eng)

    def op(self, eng, fn, r=(), w=(), stream=None, inc=16, extra=()):
        i = Ins()
        i.idx = len(self.ins); i.eng = eng; i.fn = fn; i.stream = stream; i.inc = inc
        i.signal = False; i.sigval = 0
        deps = set(extra)
        for x in r:
            if x in self.last_w:
                deps.add(self.last_w[x])
        for x in w:
            if x in self.last_w:
                deps.add(self.last_w[x])
            rd = self.readers.get(x)
            if rd:
                deps.update(rd.values())
        deps.update(self.pending[eng]); self.pending[eng] = []
        me_key = ("s", stream) if stream is not None else ("e", eng)
        for x in r:
            self.readers.setdefault(x, {})[me_key] = i.idx
        for x in w:
            self.last_w[x] = i.idx
            self.readers[x] = {}
        waits = []
        wd = self.waited[eng]
        best = {}
        for d in deps:
            k = self._key(self.ins[d])
            if k == ("e", "pe") and eng == "pe" and stream is None:
                continue
            if best.get(k, -1) < d:
                best[k] = d
        for k, d in sorted(best.items(), key=lambda kv: kv[1]):
            if wd.get(k, -1) >= d:
                continue
            wd[k] = d
            self.ins[d].signal = True
            waits.append(d)
        i.waits = waits
        self.ins.append(i)
        if stream is not None:
            self.stream_last[stream] = i.idx
            self.stream_inc[stream] = inc
        self.eng_last[eng] = i.idx
        return i.idx

    def barrier(self, fn):
        extra = [v for v in self.eng_last.values() if v is not None]
        extra += [v for s, v in self.stream_last.items() if not s.startswith("cc")]
        b = self.op("dve", fn, extra=extra)
        for e in self.ENGS:
            if e != "dve":
                self.pending[e].append(b)
        return b

    def emit(self, nc, sem_of_eng, sem_of_stream):
        cnt = {}
        for i in self.ins:
            k = self._key(i)
            if i.stream is not None:
                cnt[k] = cnt.get(k, 0) + i.inc
                i.sigval = cnt[k]
            elif i.signal:
                cnt[k] = cnt.get(k, 0) + 1
                i.sigval = cnt[k]
        self.final_counts = cnt

        def semof(p):
            return sem_of_stream[p.stream] if p.stream is not None else sem_of_eng[p.eng]

        def run(engname, e):
            for i in self.ins:
                if i.eng != engname:
                    continue
                for d in i.waits:
                    p = self.ins[d]
                    e.wait_ge(semof(p), p.sigval)
                inst = i.fn(e)
                if i.stream is not None:
                    inst.then_inc(sem_of_stream[i.stream], i.inc)
                elif i.signal:
                    inst.then_inc(sem_of_eng[i.eng], 1)
            if engname == "sp":
                for s, last in self.stream_last.items():
                    e.wait_ge(sem_of_stream[s], self.ins[last].sigval)

        with nc.Block() as block:
            @block.tensor
            def _(e):
                run("pe", e)

            @block.scalar
            def _(e):
                run("act", e)

            @block.vector
            def _(e):
                run("dve", e)

            @block.gpsimd
            def _(e):
                run("pool", e)

            @block.sync
            def _(e):
                run("sp", e)


class Arena:
    def __init__(self, ap_f32, nf32):
        self.ap = ap_f32; self.n = nf32; self.off = 0

    def reset(self):
        self.off = 0

    def f32(self, n):
        assert self.off + n <= self.n, ("arena overflow", self.off, n, self.n)
        a = self.ap[:, self.off:self.off + n]
        self.off += n
        return a

    def bf16(self, n):
        m = (n + 1) // 2
        return self.f32(m).bitcast(BF16)[:, 0:n]


def build(L, TOK):
    NT = TOK // 128
    NB = TOK // 512
    SEQ = 4 * TOK
    ROWS = 8 * 128 + 8 * 128 + 64 + 2
    K_OFF, V_OFF, KR_OFF, HALO = 0, 1024, 2048, 2112
    CH = max(128, ((1 << 20) // (TOK * 2)) // 128 * 128)
    NCH = (ROWS + CH - 1) // CH
    GW = TOK
    VC = TOK // 128
    nc = bass.Bass("TRN2", target_bir_lowering=False)

    def din(name, shape, dt=F32):
        return nc.dram_tensor(name, list(shape), dt, kind="ExternalInput").ap()

    x_d = din("x", [TOK, D])
    p_d = din("p", [L, TOK, 256])
    pos_d = din("pos", [128, NT], I32)
    win_d = din("w_in", [L, D, INW])
    wukv_d = din("w_ukv", [L, 512, 2048])
    wout_d = din("w_out", [L, D, D])
    wgate_d = din("w_gate", [L, D, D])
    wpp_d = din("w_pp", [L, 256, D])
    wsp_d = din("w_spT", [L, 128, 512])
    vec_d = din("vec", [L, NV])
    gn_d = din("gn", [L, 2, D])
    cols_d = din("cols", [L, 128, 12])
    sel_d = din("sel", [8, 256])
    cmat_d = din("cmat", [128, 640])
    invf_d = din("invf", [128, 32])
    out_d = nc.dram_tensor("out", [TOK, D], F32, kind="ExternalOutput").ap()
    proj_d = nc.dram_tensor("proj", [TOK, NPROJ], F32).ap()
    zct_d = nc.dram_tensor("zct", [1024, TOK], F32).ap()
    hbuf_d = nc.dram_tensor("hbuf", [TOK, D], F32).ap()
    chn = [min(CH, ROWS - k * CH) for k in range(NCH)]
    gin_ts = [nc.dram_tensor("gin%d" % k, [chn[k], GW], BF16) for k in range(NCH)]
    gout_ts = [nc.dram_tensor("gout%d" % k, [4 * chn[k], GW], BF16) for k in range(NCH)]

    def gin_rows(r0, n):
        k = r0 // CH
        assert (r0 + n - 1) // CH == k
        return gin_ts[k].ap()[r0 - k * CH:r0 - k * CH + n, :]

    def gout_rows(rk, r0, n):
        k = r0 // CH
        assert (r0 + n - 1) // CH == k
        o = rk * chn[k] + r0 - k * CH
        return gout_ts[k].ap()[o:o + n, :]

    def gres(r0):
        return "gout.%d" % (r0 // CH)

    P = Prog()
    es = contextlib.ExitStack()
    with es:
        def sb(name, shape, dt):
            return es.enter_context(nc.sbuf_tensor("s_" + name, list(shape), dt))

        A_t = sb("A", [128, 16, TOK], BF16)
        PHN = 27136
        PH_t = sb("PH", [128, PHN], F32)
        vec_t = sb("vec", [128, NV], F32)
        cols_t = sb("cols", [128, 12], F32)
        ident = sb("ident", [128, 128], BF16)
        ones_b = sb("ones_b", [128, 128], BF16)
        ones_f = sb("ones_f", [128, 128], F32)
        shm = sb("shm", [128, 4, 128], BF16)
        sel_b = sb("sel_b", [8, 256], BF16)
        cosT = sb("cosT", [128, NT, 32], F32)
        sinT = sb("sinT", [128, NT, 32], F32)
        invf = sb("invf", [128, 32], F32)
        posi = sb("posi", [128, NT], I32)
        neghalf = sb("neghalf", [128, 1], F32)
        dummy = sb("dummy", [128, 8], F32)
        wsp_b = sb("wsp_b", [128, 4, 128], BF16)
        G_b = sb("G_b", [8, 512], BF16)
        ps = [es.enter_context(nc.psum_tensor("ps%d" % b, [128, 512], F32)) for b in range(8)]
        A = A_t[:]
        ph = Arena(PH_t[:], PHN)

        def psb(b):
            return ps[b][:].bitcast(BF16)

        def dma(q, out, in_, r, w, stream):
            P.op(q, lambda e, out=out, in_=in_: e.dma_start(out=out, in_=in_), r=r, w=w, stream=stream)

        def act(out, in_, func, r, w, scale=None, accum=None, bias=None):
            kw = {}
            if scale is not None:
                kw["scale"] = scale
            if accum is not None:
                kw["accum_out"] = accum
            if bias is not None:
                kw["bias"] = bias
            P.op("act", lambda e, out=out, in_=in_, func=func, kw=kw: e.activation(out=out, in_=in_, func=func, **kw), r=r, w=w)

        def tt(eng, out, in0, in1, op, r, w):
            P.op(eng, lambda e, out=out, in0=in0, in1=in1, op=op: e.tensor_tensor(out=out, in0=in0, in1=in1, op=op), r=r, w=w)

        def ts(eng, out, in0, s1, s2, op0, op1, r, w):
            if op1 is None:
                P.op(eng, lambda e, out=out, in0=in0, s1=s1, op0=op0: e.tensor_scalar(out=out, in0=in0, scalar1=s1, scalar2=None, op0=op0), r=r, w=w)
            else:
                P.op(eng, lambda e, out=out, in0=in0, s1=s1, s2=s2, op0=op0, op1=op1: e.tensor_scalar(out=out, in0=in0, scalar1=s1, scalar2=s2, op0=op0, op1=op1), r=r, w=w)

        def stt(out, in0, scalar, in1, op0, op1, r, w):
            P.op("dve", lambda e, out=out, in0=in0, scalar=scalar, in1=in1, op0=op0, op1=op1:
                 e.scalar_tensor_tensor(out=out, in0=in0, scalar=scalar, in1=in1, op0=op0, op1=op1), r=r, w=w)

        def cp(eng, out, in_, r, w):
            if eng == "act":
                P.op("act", lambda e, out=out, in_=in_: e.copy(out=out, in_=in_), r=r, w=w)
            else:
                P.op(eng, lambda e, out=out, in_=in_: e.tensor_copy(out=out, in_=in_), r=r, w=w)

        def mm(out, lhsT, rhs, start, stop, r, w):
            P.op("pe", lambda e, out=out, lhsT=lhsT, rhs=rhs, start=start, stop=stop:
                 e.matmul(out, lhsT=lhsT, rhs=rhs, start=start, stop=stop), r=r, w=w)

        def tr(out, in_, r, w, idn=None):
            idn = ident[:] if idn is None else idn
            P.op("pe", lambda e, out=out, in_=in_, idn=idn: e.transpose(out=out, in_=in_, identity=idn), r=r, w=w)

        def red(out, in_, r, w):
            P.op("dve", lambda e, out=out, in_=in_: e.tensor_reduce(out=out, in_=in_, axis=AX.X, op=ALU.add), r=r, w=w)

        def rstd_of(ssq, n, rname, width, tmp, out):
            ts("dve", tmp[0], ssq[0], 1.0 / n, EPS, ALU.mult, ALU.add, r=[ssq[1]], w=[tmp[1]])
            tt("pool", out[0], tmp[0], neghalf[:, 0:1].to_broadcast([128, width]), ALU.pow, r=[tmp[1]], w=[out[1]])

        def barrier():
            P.barrier(lambda e: e.memset(dummy[:, 0:1], 0.0))
            ph.reset()

        evac_flip = [0]

        def evac(out, in_, r, w):
            evac_flip[0] ^= 1
            cp("act" if evac_flip[0] else "dve", out, in_, r, w)

        cm_f = ph.f32(640)
        dma("sp", cm_f, cmat_d[:, :], [], ["cm_f"], "i0")
        cp("dve", ident[:], cm_f[:, 0:128], ["cm_f"], ["ident"])
        cp("dve", shm[:].rearrange("p a b -> p (a b)"), cm_f[:, 128:640], ["cm_f"], ["shm"])
        P.op("pool", lambda e: e.memset(ones_b[:], 1.0), w=["ones_b"])
        P.op("pool", lambda e: e.memset(ones_f[:], 1.0), w=["ones_f"])
        P.op("pool", lambda e: e.memset(neghalf[:], -0.5), w=["neghalf"])
        sel_f = ph.f32(256)
        dma("sp", sel_f[0:8, :], sel_d[:, :], [], ["sel_f"], "i1")
        cp("dve", sel_b[:], sel_f[0:8, :], ["sel_f"], ["sel_b"])
        dma("sp", invf[:], invf_d[:, :], [], ["invf"], "i2")
        dma("sp", posi[:], pos_d[:, :], [], ["posi"], "i3")
        posf = ph.f32(NT)
        cp("dve", posf, posi[:], ["posi"], ["posf"])
        ang = ph.f32(NT * 32).rearrange("p (a b) -> p a b", b=32)
        tt("dve", ang, posf.unsqueeze(2).to_broadcast([128, NT, 32]),
           invf[:].unsqueeze(1).to_broadcast([128, NT, 32]), ALU.mult, ["posf", "invf"], ["ang"])
        uu = ph.f32(NT * 32).rearrange("p (a b) -> p a b", b=32)
        ki = ph.f32(NT * 32).bitcast(I32).rearrange("p (a b) -> p a b", b=32)
        kf = ph.f32(NT * 32).rearrange("p (a b) -> p a b", b=32)
        rr = ph.f32(NT * 32).rearrange("p (a b) -> p a b", b=32)
        mk = ph.f32(NT * 32).rearrange("p (a b) -> p a b", b=32)
        C1 = 6.28125
        C2 = float(2 * np.pi - 6.28125)
        for which, dst in ((0, sinT), (1, cosT)):
            if which == 1:
                ts("dve", ang, ang, PI / 2, None, ALU.add, None, ["ang"], ["ang"])
            ts("dve", uu, ang, 1.0 / (2 * PI), None, ALU.mult, None, ["ang"], ["uu"])
            cp("dve", ki, uu, ["uu"], ["ki"])
            cp("dve", kf, ki, ["ki"], ["kf"])
            stt(rr, kf, -C1, ang, ALU.mult, ALU.add, ["kf", "ang"], ["rr"])
            stt(rr, kf, -C2, rr, ALU.mult, ALU.add, ["kf", "rr"], ["rr"])
            ts("dve", mk, rr, -PI, None, ALU.is_lt, None, ["rr"], ["mk"])
            stt(rr, mk, 2 * PI, rr, ALU.mult, ALU.add, ["mk", "rr"], ["rr"])
            ts("dve", mk, rr, PI, None, ALU.is_gt, None, ["rr"], ["mk"])
            stt(rr, mk, -2 * PI, rr, ALU.mult, ALU.add, ["mk", "rr"], ["rr"])
            ts("dve", rr, rr, PI, -PI, ALU.min, ALU.max, ["rr"], ["rr"])
            act(dst[:], rr, AF.Sin, ["rr"], ["tab%d" % which])
        barrier()

        def norm_transpose(l, src_d, gidx):
            gnorm = ph.f32(D)
            dma("sp", gnorm, gn_d[l, gidx, :].partition_broadcast(128), [], ["gnorm"], "lgn")
            hts = [ph.f32(D) for _ in range(2)]
            hns = [ph.bf16(D) for _ in range(2)]
            junk = ph.bf16(D)
            st = ph.f32(8)
            def nt_s1(t):
                b = t % 2
                dma("sp", hts[b], src_d[t * 128:(t + 1) * 128, :], [], ["ht%d" % b], "ldh%d" % b)
                act(junk, hts[b], AF.Square, ["ht%d" % b], ["junk", "ssq%d" % b], accum=st[:, b:b + 1])
                rstd_of((st[:, b:b + 1], "ssq%d" % b), D, None, 1, (st[:, 2 + b:3 + b], "tq%d" % b), (st[:, 4 + b:5 + b], "rs%d" % b))
                stt(hns[b], hts[b], st[:, 4 + b:5 + b], gnorm, ALU.mult, ALU.mult, ["ht%d" % b, "rs%d" % b, "gnorm"], ["hn%d" % b])

            def nt_s2(t):
                b = t % 2
                for kg in range(4):
                    bank = 4 + (kg % 2)
                    for j in range(4):
                        kc = kg * 4 + j
                        tr(psb(bank)[:, j * 128:(j + 1) * 128], hns[b][:, kc * 128:(kc + 1) * 128], ["hn%d" % b, "ident"], ["ps%d" % bank])
                    evac(A[:, kg * 4:kg * 4 + 4, t * 128:(t + 1) * 128],
                         psb(bank)[:, 0:512].rearrange("p (a b) -> p a b", b=128), ["ps%d" % bank], ["A.%d.%d" % (t, kg)])

            for t in range(NT + 1):
                if t < NT:
                    nt_s1(t)
                if t >= 1:
                    nt_s2(t - 1)

        gemm_bank = [0]

        def gemm(w_src, KC, ncols, wslot, wres, actT, act_res, epilogue, yform=False):
            P.op("pool", lambda e: e.dma_start(out=wslot[:, 0:KC, 0:ncols], in_=w_src.rearrange("(kc p) n -> p kc n", p=128)),
                 r=[], w=[wres], stream="w." + wres)
            if not yform:
                for t in range(NT):
                    bank = gemm_bank[0]; gemm_bank[0] = (gemm_bank[0] + 1) % 4
                    for kc in range(KC):
                        mm(ps[bank][:, 0:ncols], actT[:, kc, t * 128:(t + 1) * 128], wslot[:, kc, 0:ncols],
                           kc == 0, kc == KC - 1, [wres] + act_res(t), ["ps%d" % bank])
                    epilogue(t, ps[bank][:, 0:ncols], "ps%d" % bank)
            else:
                for fc in range(ncols // 128):
                    for qb in range(NB):
                        bank = gemm_bank[0]; gemm_bank[0] = (gemm_bank[0] + 1) % 4
                        rs = []
                        for t in range(4 * qb, 4 * qb + 4):
                            rs += act_res(t)
                        for kc in range(KC):
                            mm(ps[bank][:, :], wslot[:, kc, fc * 128:(fc + 1) * 128], actT[:, kc, qb * 512:(qb + 1) * 512],
                               kc == 0, kc == KC - 1, [wres] + rs, ["ps%d" % bank])
                        epilogue((fc, qb), ps[bank][:, :], "ps%d" % bank)

        def a_res(t):
            return ["A.%d.%d" % (t, kg) for kg in range(4)]

        KSTOP = 99
        for l in range(L):
            if KSTOP <= 0:
                break
            hsrc = x_d if l == 0 else hbuf_d
            hdst = out_d if l == L - 1 else hbuf_d
            dma("sp", vec_t[:], vec_d[l, :].partition_broadcast(128), [], ["vec"], "lvec")
            dma("sp", cols_t[:], cols_d[l, :, :], [], ["cols"], "lcols")
            P.op("pool", lambda e, l=l: e.dma_start(out=wsp_b[:].rearrange("p a b -> p (a b)"), in_=wsp_d[l, :, :]), r=[], w=["wsp_b"], stream="w.wsp")

            norm_transpose(l, hsrc, 0)
            barrier()

            if KSTOP <= 1:
                break
            wslots = [ph.bf16(16 * 512).rearrange("p (a b) -> p a b", b=512) for _ in range(3)]
            stg = [ph.f32(512) for _ in range(4)]
            stg_i = [0]
            groups = [(5120, 512, "plain"), (5632, 64, "plain")]
            groups += [(c0, 512, "plain") for c0 in (2048, 2560)]
            groups += [(0, 512, "plain"), (512, 512, "plain"), (1024, 512, "silu"), (1536, 512, "plain"),
                       (3072, 512, "silu"), (3584, 512, "plain"), (4096, 512, "plain"), (4608, 512, "plain"),
                       (5696, 512, "zc"), (6208, 512, "zc")]
            for gi, (c0, ncols, kind) in enumerate(groups):
                slot = gi % 3
                wres = "wslot%d" % slot

                def epi(t, pap, pres, c0=c0, ncols=ncols, kind=kind):
                    s = stg_i[0]; stg_i[0] = (s + 1) % 4
                    sres = "stg%d" % s
                    if kind == "plain":
                        evac(stg[s][:, 0:ncols], pap, [pres], [sres])
                        dma("sp", proj_d[t * 128:(t + 1) * 128, c0:c0 + ncols], stg[s][:, 0:ncols], [sres], [], "st%d" % s)
                    elif kind == "silu":
                        act(stg[s][:, 0:ncols], pap, AF.Silu, [pres], [sres])
                        dma("sp", proj_d[t * 128:(t + 1) * 128, c0:c0 + ncols], stg[s][:, 0:ncols], [sres], [], "st%d" % s)
                    else:
                        fc, qb = t
                        act(stg[s][:, :], pap, AF.Silu, [pres], [sres])
                        r0 = (c0 - 5696) + fc * 128
                        dma("sp", zct_d[r0:r0 + 128, qb * 512:(qb + 1) * 512], stg[s][:, :], [sres], [], "st%d" % s)

                gemm(win_d[l, :, c0:c0 + ncols], 16, ncols, wslots[slot], wres, A, a_res, epi, yform=(kind == "zc"))
            barrier()

            if KSTOP <= 2:
                break
            rowres = {}
            for h in range(8):
                rowres.setdefault((K_OFF + h * 128) // CH, []).append("gin.k%d" % h)
                rowres.setdefault((V_OFF + h * 128) // CH, []).append("gin.v%d" % h)
            rowres.setdefault(KR_OFF // CH, []).append("gin.kr")
            rowres.setdefault(HALO // CH, []).append("gin.h0")
            rowres.setdefault((HALO + 1) // CH, []).append("gin.h1")
            cc_state = {"written": set(), "issued": set(), "pending": [], "tick": 0, "last": -100}

            def cc_written(res):
                cc_state["written"].add(res)
                for k in range(NCH):
                    if k not in cc_state["issued"] and k not in cc_state["pending"] and all(x in cc_state["written"] for x in rowres[k]):
                        cc_state["pending"].append(k)

            def cc_issue(k):
                cc_state["issued"].add(k)
                P.op("pool", lambda e, k=k: e.collective_compute("AllGather", ALU.bypass, replica_groups=[[0, 1, 2, 3], [4, 5, 6, 7]],
                                                                ins=[gin_ts[k].ap()], outs=[gout_ts[k].ap()]),
                     r=rowres[k], w=["gout.%d" % k, "ccserial"], stream="cc%d" % k, inc=1)

            def cc_tick(spacing):
                cc_state["tick"] += 1
                if cc_state["pending"] and cc_state["tick"] - cc_state["last"] >= spacing:
                    cc_state["last"] = cc_state["tick"]
                    cc_issue(cc_state["pending"].pop(0))

            def cc_flush_ready():
                pass

            def cc_flush_all():
                while cc_state["pending"]:
                    cc_issue(cc_state["pending"].pop(0))
                assert len(cc_state["issued"]) == NCH

            ckvT = ph.bf16(4 * TOK).rearrange("p (a b) -> p a b", b=TOK)
            krTs = ph.bf16(TOK)
            ck = [ph.f32(576) for _ in range(2)]
            ckn = [ph.bf16(512) for _ in range(2)]
            junk = ph.bf16(512)
            st = ph.f32(16)
            krn = [ph.f32(64) for _ in range(2)]
            rt = [ph.f32(32) for _ in range(4)]
            krr = [ph.bf16(64) for _ in range(2)]
            hb = [ph.f32(1024) for _ in range(2)]
            hx = [ph.bf16(512) for _ in range(2)]
            for i, t in enumerate((0, NT - 1)):
                dma("sp", hb[i], proj_d[t * 128:(t + 1) * 128, 2048:3072], [], ["hb%d" % i], "lhb%d" % i)
                tt("pool", hx[i], hb[i][:, 0:512], hb[i][:, 512:1024], ALU.mult, ["hb%d" % i], ["hx%d" % i])
            dma("sp", gin_rows(HALO, 1)[:, 0:512], hx[0][0:1, :], ["hx0"], ["gin.h0"], "st0")
            dma("sp", gin_rows(HALO + 1, 1)[:, 0:512], hx[1][127:128, :], ["hx1"], ["gin.h1"], "st1")
            if GW > 512:
                zrow = ph.bf16(GW)
                P.op("pool", lambda e: e.memset(zrow[0:2, 0:GW - 512], 0.0), w=["zrow"])
                dma("sp", gin_rows(HALO, 2)[:, 512:GW], zrow[0:2, 0:GW - 512], ["zrow"], ["gin.hz"], "st3")
                rowres[HALO // CH].append("gin.hz")
            cc_written("gin.h0"); cc_written("gin.h1"); cc_written("gin.hz")
            for t in range(NT):
                b = t % 2
                dma("sp", ck[b], proj_d[t * 128:(t + 1) * 128, 5120:5696], [], ["ck%d" % b], "ldh%d" % b)
                act(junk, ck[b][:, 0:512], AF.Square, ["ck%d" % b], ["junk", "sq%d" % b], accum=st[:, b:b + 1])
                act(junk[:, 0:64], ck[b][:, 512:576], AF.Square, ["ck%d" % b], ["junk", "sr%d" % b], accum=st[:, 2 + b:3 + b])
                rstd_of((st[:, b:b + 1], "sq%d" % b), 512, None, 1, (st[:, 4 + b:5 + b], "t1%d" % b), (st[:, 6 + b:7 + b], "r1%d" % b))
                rstd_of((st[:, 2 + b:3 + b], "sr%d" % b), 64, None, 1, (st[:, 8 + b:9 + b], "t2%d" % b), (st[:, 10 + b:11 + b], "r2%d" % b))
                stt(ckn[b], ck[b][:, 0:512], st[:, 6 + b:7 + b], vec_t[:, V_KV:V_KV + 512], ALU.mult, ALU.mult,
                    ["ck%d" % b, "r1%d" % b, "vec"], ["ckn%d" % b])
                stt(krn[b], ck[b][:, 512:576], st[:, 10 + b:11 + b], vec_t[:, V_KR:V_KR + 64], ALU.mult, ALU.mult,
                    ["ck%d" % b, "r2%d" % b, "vec"], ["krn%d" % b])
                x1 = krn[b][:, 0:32]; x2 = krn[b][:, 32:64]
                Ct = cosT[:, t, :]; St = sinT[:, t, :]
                tt("pool", rt[0], x1, Ct, ALU.mult, ["krn%d" % b, "tab1"], ["rt0"])
                tt("pool", rt[1], x2, St, ALU.mult, ["krn%d" % b, "tab0"], ["rt1"])
                tt("pool", krr[b][:, 0:32], rt[0], rt[1], ALU.subtract, ["rt0", "rt1"], ["krr%d" % b])
                tt("pool", rt[2], x2, Ct, ALU.mult, ["krn%d" % b, "tab1"], ["rt2"])
                tt("pool", rt[3], x1, St, ALU.mult, ["krn%d" % b, "tab0"], ["rt3"])
                tt("pool", krr[b][:, 32:64], rt[2], rt[3], ALU.add, ["rt2", "rt3", "krr%d" % b], ["krr%d" % b])
                for j in range(4):
                    tr(psb(4)[:, j * 128:(j + 1) * 128], ckn[b][:, j * 128:(j + 1) * 128], ["ckn%d" % b, "ident"], ["ps4"])
                evac(ckvT[:, :, t * 128:(t + 1) * 128], psb(4)[:, 0:512].rearrange("p (a b) -> p a b", b=128), ["ps4"], ["ckvT.%d" % t])
                tr(psb(5)[0:64, 0:128], krr[b][:, 0:64], ["krr%d" % b, "ident"], ["ps5"])
                evac(krTs[0:64, t * 128:(t + 1) * 128], psb(5)[0:64, 0:128], ["ps5"], ["krTs"])
            dma("sp", gin_rows(KR_OFF, 64), krTs[0:64, :], ["krTs"], ["gin.kr"], "st2")
            cc_written("gin.kr")
            wuk = [ph.bf16(4 * 512).rearrange("p (a b) -> p a b", b=512) for _ in range(2)]
            kst = [ph.bf16(2 * TOK).rearrange("p (a b) -> p a b", b=TOK) for _ in range(2)]
            vst = [ph.bf16(2 * TOK).rearrange("p (a b) -> p a b", b=TOK) for _ in range(2)]
            knb = [ph.bf16(128) for _ in range(4)]
            kq = ph.f32(12)
            kn_i = [0]
            for g in range(4):
                sl = g % 2

                def epi(t, pap, pres, g=g, sl=sl):
                    cc_tick(10)
                    for hh in range(2):
                        i = kn_i[0]; kn_i[0] = (i + 1) % 4
                        kps = pap[:, hh * 256:hh * 256 + 128]
                        vps = pap[:, hh * 256 + 128:hh * 256 + 256]
                        act(junk[:, 0:128], kps, AF.Square, [pres], ["junk", "ks%d" % i], accum=kq[:, i:i + 1])
                        rstd_of((kq[:, i:i + 1], "ks%d" % i), 128, None, 1, (kq[:, 4 + i:5 + i], "kt%d" % i), (kq[:, 8 + i:9 + i], "kr%d" % i))
                        stt(knb[i], kps, kq[:, 8 + i:9 + i], vec_t[:, V_KN:V_KN + 128], ALU.mult, ALU.mult,
                            [pres, "kr%d" % i, "vec"], ["knb%d" % i])
                        cp("act", vst[sl][:, hh, t * 128:(t + 1) * 128], vps, [pres], ["vst%d" % sl])
                        bank = 4 + i
                        tr(psb(bank)[:, 0:128], knb[i], ["knb%d" % i, "ident"], ["ps%d" % bank])
                        evac(kst[sl][:, hh, t * 128:(t + 1) * 128], psb(bank)[:, 0:128], ["ps%d" % bank], ["kst%d" % sl])

                gemm(wukv_d[l, :, g * 512:(g + 1) * 512], 4, 512, wuk[sl], "wuk%d" % sl, ckvT, lambda t: ["ckvT.%d" % t], epi)
                for hh in range(2):
                    h = 2 * g + hh
                    dma("sp", gin_rows(K_OFF + h * 128, 128), kst[sl][:, hh, :], ["kst%d" % sl], ["gin.k%d" % h], "sk%d" % sl)
                    dma("sp", gin_rows(V_OFF + h * 128, 128), vst[sl][:, hh, :], ["vst%d" % sl], ["gin.v%d" % h], "sv%d" % sl)
                    cc_written("gin.k%d" % h); cc_written("gin.v%d" % h)
            if KSTOP <= 3:
                break
            cc_flush_ready()
            barrier()

            if KSTOP <= 4:
                break
            uvz = [ph.f32(1536) for _ in range(2)]
            vsq = [ph.f32(512) for _ in range(2)]
            sa_ = [ph.f32(16) for _ in range(2)]
            vn = [ph.bf16(512) for _ in range(2)]
            t1_ = [ph.f32(512) for _ in range(2)]
            ya_ = [ph.f32(512) for _ in range(2)]
            junk_ = [ph.bf16(512) for _ in range(2)]
            yan = [ph.bf16(512) for _ in range(2)]
            def a_s1(t):
                b = t % 2
                cc_tick(4)
                sa = sa_[b]; t1 = t1_[b]; ya = ya_[b]; junk = junk_[b]; B_ = str(b)
                U = uvz[b][:, 0:512]; Vv = uvz[b][:, 512:1024]; Z = uvz[b][:, 1024:1536]
                dma("sp", uvz[b], proj_d[t * 128:(t + 1) * 128, 0:1536], [], ["uvz%d" % b], "ldh%d" % b)
                tt("pool", vsq[b], Vv, Vv, ALU.mult, ["uvz%d" % b], ["vsq" + B_])
                red(sa[:, 0:4], vsq[b].rearrange("p (a b) -> p a b", b=128), ["vsq" + B_], ["sa0" + B_])
                rstd_of((sa[:, 0:4], "sa0" + B_), 128, None, 4, (sa[:, 4:8], "sa1" + B_), (sa[:, 8:12], "sa2" + B_))
                for h in range(4):
                    stt(vn[b][:, h * 128:(h + 1) * 128], Vv[:, h * 128:(h + 1) * 128], sa[:, 8 + h:9 + h],
                        vec_t[:, V_SGU + h * 128:V_SGU + (h + 1) * 128], ALU.mult, ALU.mult, ["uvz%d" % b, "sa2" + B_, "vec"], ["vn%d" % b])
                for h in range(4):
                    mm(ps[b][:, h * 128:(h + 1) * 128], wsp_b[:, h, :], vn[b][:, h * 128:(h + 1) * 128], True, True, ["wsp_b", "vn%d" % b], ["ps" + B_])

            def a_s2(t):
                b = t % 2
                sa = sa_[b]; t1 = t1_[b]; ya = ya_[b]; junk = junk_[b]; B_ = str(b)
                U = uvz[b][:, 0:512]; Vv = uvz[b][:, 512:1024]; Z = uvz[b][:, 1024:1536]
                for h in range(4):
                    stt(t1[:, h * 128:(h + 1) * 128], ps[b][:, h * 128:(h + 1) * 128], cols_t[:, 8 + h:9 + h], U[:, h * 128:(h + 1) * 128],
                        ALU.add, ALU.mult, ["ps" + B_, "cols", "uvz%d" % b], ["t1" + B_])
                tt("pool", ya, t1, Z, ALU.mult, ["t1" + B_, "uvz%d" % b], ["ya" + B_])
                act(junk, ya, AF.Square, ["ya" + B_], ["junk" + B_, "sa3" + B_], accum=sa[:, 12:13])
                rstd_of((sa[:, 12:13], "sa3" + B_), 512, None, 1, (sa[:, 13:14], "sa4" + B_), (sa[:, 14:15], "sa5" + B_))
                stt(yan[b], ya, sa[:, 14:15], vec_t[:, V_OA:V_OA + 512], ALU.mult, ALU.mult, ["ya" + B_, "sa5" + B_, "vec"], ["yan%d" % b])
                for j in range(4):
                    tr(psb(4 + b)[:, j * 128:(j + 1) * 128], yan[b][:, j * 128:(j + 1) * 128], ["yan%d" % b, "ident"], ["ps%d" % (4 + b)])
                evac(A[:, 0:4, t * 128:(t + 1) * 128], psb(4 + b)[:, 0:512].rearrange("p (a b) -> p a b", b=128), ["ps%d" % (4 + b)], ["A.%d.0" % t])

            for t in range(NT + 1):
                if t < NT:
                    a_s1(t)
                if t >= 1:
                    a_s2(t - 1)
            barrier()

            if KSTOP <= 5:
                break
            cc_flush_all()
            for rk in range(4):
                dma("sp", G_b[2 * rk:2 * rk + 2, :], gout_rows(rk, HALO, 2)[:, 0:512],
                    [gres(HALO)], ["G_b.%d" % rk], "lgb%d" % rk)
            b4 = [ph.f32(2048) for _ in range(3)]
            xf = [ph.f32(512) for _ in range(3)]
            xb = [ph.bf16(512) for _ in range(3)]
            cvs = [[ph.f32(512) for _ in range(4)] for _ in range(2)]
            junk_ = [ph.bf16(512) for _ in range(2)]
            sbs = [ph.f32(4) for _ in range(2)]
            ybn = [ph.bf16(512) for _ in range(2)]
            for t in range(NT + 1):
                if t < NT:
                    b = t % 3
                    dma("sp", b4[b], proj_d[t * 128:(t + 1) * 128, 1536:3584], [], ["b4%d" % b], "ldb%d" % b)
                    tt("pool", xf[b], b4[b][:, 512:1024], b4[b][:, 1024:1536], ALU.mult, ["b4%d" % b], ["xf%d" % b])
                    cp("act", xb[b], xf[b], ["xf%d" % b], ["xb%d" % b])
                j = t - 1
                if j < 0:
                    continue
                jb = j % 3; pb_ = (j - 1) % 3; nb_ = (j + 1) % 3
                o = j % 2; O_ = str(o)
                cv, ca, cb2, yb = cvs[o]; junk = junk_[o]; sb_ = sbs[o]
                b1, b2 = (1, 2) if o == 0 else (6, 7)
                p1 = "ps%d" % b1; p2 = "ps%d" % b2
                mm(ps[b1][:, :], shm[:, 0, :], xb[jb], True, False, ["shm", "xb%d" % jb], [p1])
                if j > 0:
                    mm(ps[b1][:, :], shm[:, 2, :], xb[pb_], False, True, ["shm", "xb%d" % pb_], [p1])
                else:
                    mm(ps[b1][:, :], sel_b[0:8, 0:128], G_b[0:8, :], False, True, ["sel_b", "G_b.0", "G_b.1", "G_b.2", "G_b.3"], [p1])
                mm(ps[b2][:, :], shm[:, 1, :], xb[jb], True, False, ["shm", "xb%d" % jb], [p2])
                if j < NT - 1:
                    mm(ps[b2][:, :], shm[:, 3, :], xb[nb_], False, True, ["shm", "xb%d" % nb_], [p2])
                else:
                    mm(ps[b2][:, :], sel_b[0:8, 128:256], G_b[0:8, :], False, True, ["sel_b", "G_b.0", "G_b.1", "G_b.2", "G_b.3"], [p2])
                tt("pool", cv, xf[jb], vec_t[:, V_CW1:V_CW1 + 512], ALU.mult, ["xf%d" % jb, "vec"], ["cv" + O_])
                tt("pool", cv, cv, vec_t[:, V_CB:V_CB + 512], ALU.add, ["cv" + O_, "vec"], ["cv" + O_])
                tt("dve", ca, ps[b1][:, :], vec_t[:, V_CW0:V_CW0 + 512], ALU.mult, [p1, "vec"], ["ca" + O_])
                tt("dve", cb2, ps[b2][:, :], vec_t[:, V_CW2:V_CW2 + 512], ALU.mult, [p2, "vec"], ["cb2" + O_])
                tt("dve", cv, cv, ca, ALU.add, ["cv" + O_, "ca" + O_], ["cv" + O_])
                tt("dve", cv, cv, cb2, ALU.add, ["cv" + O_, "cb2" + O_], ["cv" + O_])
                tt("dve", yb, cv, b4[jb][:, 0:512], ALU.mult, ["cv" + O_, "b4%d" % jb], ["yb" + O_])
                tt("pool", yb, yb, b4[jb][:, 1536:2048], ALU.mult, ["yb" + O_, "b4%d" % jb], ["yb" + O_])
                act(junk, yb, AF.Square, ["yb" + O_], ["junk" + O_, "sb0" + O_], accum=sb_[:, 0:1])
                rstd_of((sb_[:, 0:1], "sb0" + O_), 512, None, 1, (sb_[:, 1:2], "sb1" + O_), (sb_[:, 2:3], "sb2" + O_))
                stt(ybn[o], yb, sb_[:, 2:3], vec_t[:, V_OB:V_OB + 512], ALU.mult, ALU.mult, ["yb" + O_, "sb2" + O_, "vec"], ["ybn%d" % o])
                for q in range(4):
                    tr(psb(4 + o)[:, q * 128:(q + 1) * 128], ybn[o][:, q * 128:(q + 1) * 128], ["ybn%d" % o, "ident"], ["ps%d" % (4 + o)])
                evac(A[:, 4:8, j * 128:(j + 1) * 128], psb(4 + o)[:, 0:512].rearrange("p (a b) -> p a b", b=128), ["ps%d" % (4 + o)], ["A.%d.1" % j])
            barrier()

            if KSTOP <= 6:
                break
            krT_all = ph.bf16(SEQ)
            for rk in range(4):
                dma("sp", krT_all[0:64, rk * TOK:(rk + 1) * TOK], gout_rows(rk, KR_OFF, 64), [gres(KR_OFF)], ["krT.%d" % rk], "lkr%d" % rk)
            P.op("pool", lambda e: e.memset(krT_all[64:128, :], 0.0), w=["krT.pad"])
            kTr = [ph.bf16(TOK) for _ in range(3)]
            Vr = [ph.bf16(TOK).rearrange("p (a b) -> p a b", b=128) for _ in range(3)]
            qt = ph.f32(1536)
            qsq = ph.f32(1536)
            qs = ph.f32(48)
            qnb = ph.bf16(1024)
            qrf = ph.f32(512)
            qrb = ph.bf16(512)
            r4 = [qsq[:, i * 256:(i + 1) * 256] for i in range(4)]
            QnT = ph.bf16(8 * 512).rearrange("p (a b) -> p a b", b=512)
            QrT = ph.bf16(8 * 512).rearrange("p (a b) -> p a b", b=512)
            P.op("pool", lambda e: e.memset(QrT[64:128, :, :], 0.0), w=["QrT.pad"])
            zc = [ph.f32(512) for _ in range(2)]
            yc = ph.f32(8 * 512).rearrange("p (a b) -> p a b", b=512)
            pT = [ph.bf16(512) for _ in range(8)]
            d4 = ph.f32(512)
            rden = ph.f32(512)
            ysq = rden
            rc = qsq[:, 0:512]; rc2 = qsq[:, 512:1024]
            for qb in range(NB):
                for ti in range(4):
                    t = qb * 4 + ti
                    dma("sp", qt, proj_d[t * 128:(t + 1) * 128, 3584:5120], [], ["qt"], "ldq")
                    q3 = qt.rearrange("p (h d) -> p h d", d=192)
                    s3 = qsq.rearrange("p (h d) -> p h d", d=192)
                    tt("dve", qsq, qt, qt, ALU.mult, ["qt"], ["qsq", "r40", "r41", "r42", "r43"])
                    red(qs[:, 0:8], s3[:, :, 0:128], ["qsq"], ["qs0"])
                    red(qs[:, 8:16], s3[:, :, 128:192], ["qsq"], ["qs1"])
                    rstd_of((qs[:, 0:8], "qs0"), 128, None, 8, (qs[:, 16:24], "qs2"), (qs[:, 32:40], "qs4"))
                    rstd_of((qs[:, 8:16], "qs1"), 64, None, 8, (qs[:, 24:32], "qs3"), (qs[:, 40:48], "qs5"))
                    tt("dve", q3[:, :, 0:128], q3[:, :, 0:128], qs[:, 32:40].unsqueeze(2).to_broadcast([128, 8, 128]), ALU.mult,
                       ["qt", "qs4", "qsq"], ["qt"])
                    tt("dve", qnb.rearrange("p (h d) -> p h d", d=128), q3[:, :, 0:128],
                       vec_t[:, V_QN:V_QN + 128].unsqueeze(1).to_broadcast([128, 8, 128]), ALU.mult, ["qt", "vec"], ["qnb"])
                    tt("dve", q3[:, :, 128:192], q3[:, :, 128:192], qs[:, 40:48].unsqueeze(2).to_broadcast([128, 8, 64]), ALU.mult,
                       ["qt", "qs5", "qsq"], ["qt"])
                    tt("dve", qrf.rearrange("p (h d) -> p h d", d=64), q3[:, :, 128:192],
                       vec_t[:, V_QR:V_QR + 64].unsqueeze(1).to_broadcast([128, 8, 64]), ALU.mult, ["qt", "vec"], ["qrf"])
                    f3 = qrf.rearrange("p (h d) -> p h d", d=64)
                    o3 = qrb.rearrange("p (h d) -> p h d", d=64)
                    x1 = f3[:, :, 0:32]; x2 = f3[:, :, 32:64]
                    Ct = cosT[:, t, :].unsqueeze(1).to_broadcast([128, 8, 32])
                    St = sinT[:, t, :].unsqueeze(1).to_broadcast([128, 8, 32])
                    rv = [r.rearrange("p (h d) -> p h d", d=32) for r in r4]
                    tt("pool", rv[0], x1, Ct, ALU.mult, ["qrf", "tab1", "qs0", "qs1"], ["r40"])
                    tt("pool", rv[1], x2, St, ALU.mult, ["qrf", "tab0", "qs0", "qs1"], ["r41"])
                    tt("pool", o3[:, :, 0:32], rv[0], rv[1], ALU.subtract, ["r40", "r41"], ["qrb.a"])
                    tt("dve", rv[2], x2, Ct, ALU.mult, ["qrf", "tab1", "qs0", "qs1"], ["r42"])
                    tt("dve", rv[3], x1, St, ALU.mult, ["qrf", "tab0", "qs0", "qs1"], ["r43"])
                    tt("dve", o3[:, :, 32:64], rv[2], rv[3], ALU.add, ["r42", "r43"], ["qrb.b"])
                    for hg in range(2):
                        for j in range(4):
                            h = hg * 4 + j
                            tr(psb(4)[:, j * 128:(j + 1) * 128], qnb[:, h * 128:(h + 1) * 128], ["qnb", "ident"], ["ps4"])
                        evac(QnT[:, hg * 4:hg * 4 + 4, ti * 128:(ti + 1) * 128], psb(4)[:, 0:512].rearrange("p (a b) -> p a b", b=128),
                             ["ps4"], ["QnT"])
                    for h in range(8):
                        tr(psb(5)[0:64, h * 128:(h + 1) * 128], qrb[:, h * 64:(h + 1) * 64], ["qrb.a", "qrb.b", "ident"], ["ps5"])
                    evac(QrT[0:64, :, ti * 128:(ti + 1) * 128], psb(5)[0:64, 0:1024].rearrange("p (a b) -> p a b", b=128), ["ps5"], ["QrT"])
                its = [(h, rk, kc) for h in range(8) for rk in range(4) for kc in range(VC)]
                segs = [(h, rk) for h in range(8) for rk in range(4)]

                def load_seg(si):
                    h, rk = segs[si]
                    s = si % 3
                    dma("sp", kTr[s], gout_rows(rk, K_OFF + h * 128, 128), [gres(K_OFF + h * 128)], ["kTr%d" % s], "lk%d" % s)
                    dma("sp", Vr[s], gout_rows(rk, V_OFF + h * 128, 128).rearrange("p (a b) -> p a b", b=128),
                        [gres(V_OFF + h * 128)], ["Vr%d" % s], "lv%d" % s)

                SB = (0, 1, 2, 5)

                def s_mm(ii):
                    h, rk, kc = its[ii]
                    si = h * 4 + rk
                    s = si % 3
                    bank = SB[ii % 4]
                    mm(ps[bank][:, :], kTr[s][:, kc * 128:(kc + 1) * 128], QnT[:, h, :], True, False, ["kTr%d" % s, "QnT"], ["ps%d" % bank])
                    mm(ps[bank][:, :], krT_all[:, rk * TOK + kc * 128:rk * TOK + (kc + 1) * 128], QrT[:, h, :], False, True,
                       ["krT.%d" % rk, "krT.pad", "QrT", "QrT.pad"], ["ps%d" % bank])

                load_seg(0); load_seg(1)
                nit = len(its)
                s_mm(0); s_mm(1)
                for ii in range(nit):
                    h, rk, kc = its[ii]
                    si = h * 4 + rk
                    s = si % 3
                    if kc == 0:
                        if si + 2 < len(segs):
                            load_seg(si + 2)
                        if rk == 0:
                            z = h % 2
                            dma("sp", zc[z], zct_d[h * 128:(h + 1) * 128, qb * 512:(qb + 1) * 512], [], ["zc%d" % z], "lz%d" % z)
                    if ii + 2 < nit:
                        s_mm(ii + 2)
                    bank = SB[ii % 4]
                    pi_ = ii % 8
                    act(pT[pi_], ps[bank][:, :], AF.Exp, ["ps%d" % bank], ["pT%d" % pi_], scale=SC)
                    first = (rk == 0 and kc == 0); last = (rk == 3 and kc == VC - 1)
                    mm(ps[3][:, :], Vr[s][:, kc, :], pT[pi_], first, last, ["Vr%d" % s, "pT%d" % pi_], ["ps3"])
                    if ii % 4 == 3:
                        gfirst = (rk == 0 and kc == 3); glast = last
                        for j in range(4):
                            pj = (ii - 3 + j) % 8
                            P.op("pe", lambda e, j=j, pj=pj, gfirst=gfirst, glast=glast:
                                 e.matmul(ps[6][32 * j:32 * j + 32, :], lhsT=ones_b[:, 0:32], rhs=pT[pj], start=gfirst, stop=glast,
                                          tile_position=(0, 32 * j)),
                                 r=["ones_b", "pT%d" % pj], w=["ps6"])
                    if last:
                        z = h % 2
                        cp("dve", d4, ps[6][:, :], ["ps6"], ["d4"])
                        mm(ps[4][:, :], ones_f[:], d4, True, True, ["ones_f", "d4"], ["ps4"])
                        P.op("dve", lambda e: e.reciprocal(out=rden, in_=ps[4][:, :]), r=["ps4"], w=["rden"])
                        stt(yc[:, h, :], ps[3][:, :], 32.0, rden, ALU.mult, ALU.mult, ["ps3", "rden"], ["yc%d" % h])
                        tt("pool", yc[:, h, :], yc[:, h, :], zc[z], ALU.mult, ["yc%d" % h, "zc%d" % z], ["yc%d" % h])
                        tt("pool", ysq, yc[:, h, :], yc[:, h, :], ALU.mult, ["yc%d" % h], ["rden"])
                        mm(ps[7][:, :], ones_f[:], ysq, h == 0, h == 7, ["ones_f", "rden"], ["ps7"])
                ts("dve", rc, ps[7][:, :], 1.0 / 1024, EPS, ALU.mult, ALU.add, ["ps7"], ["qsq", "r40", "r41", "r42", "r43"])
                act(rc2, rc, AF.Sqrt, ["qsq"], ["qsq"])
                P.op("dve", lambda e: e.reciprocal(out=rc2, in_=rc2), r=["qsq"], w=["qsq"])
                for h in range(8):
                    stt(A[:, 8 + h, qb * 512:(qb + 1) * 512], yc[:, h, :], cols_t[:, h:h + 1], rc2, ALU.mult, ALU.mult,
                        ["yc%d" % h, "cols", "qsq"], ["A.%d.%d" % (4 * qb + i, 2 + h // 4) for i in range(4)])
            barrier()

            if KSTOP <= 7:
                break
            wslots = [ph.bf16(16 * 512).rearrange("p (a b) -> p a b", b=512) for _ in range(2)]
            hsl = [ph.f32(512) for _ in range(4)]
            hi = [0]
            for g in range(4):
                sl = g % 2

                def epi(t, pap, pres, g=g):
                    s = hi[0]; hi[0] = (s + 1) % 4
                    dma("act", hsl[s], hsrc[t * 128:(t + 1) * 128, g * 512:(g + 1) * 512], [], ["hsl%d" % s], "lh%d" % s)
                    tt("dve", hsl[s], pap, hsl[s], ALU.add, [pres, "hsl%d" % s], ["hsl%d" % s])
                    dma("sp", hbuf_d[t * 128:(t + 1) * 128, g * 512:(g + 1) * 512], hsl[s], ["hsl%d" % s], [], "sh%d" % s)

                gemm(wout_d[l, :, g * 512:(g + 1) * 512], 16, 512, wslots[sl], "wslot%d" % sl, A, a_res, epi)
            barrier()

            if KSTOP <= 8:
                break
            norm_transpose(l, hbuf_d, 1)
            pTb = ph.bf16(2 * TOK).rearrange("p (a b) -> p a b", b=TOK)
            pf = [ph.f32(256) for _ in range(2)]
            pbf = [ph.bf16(256) for _ in range(2)]
            for t in range(NT):
                b = t % 2
                dma("sp", pf[b], p_d[l, t * 128:(t + 1) * 128, :], [], ["pf%d" % b], "lp%d" % b)
                cp("pool", pbf[b], pf[b], ["pf%d" % b], ["pbf%d" % b])
                for j in range(2):
                    tr(psb(6)[:, j * 128:(j + 1) * 128], pbf[b][:, j * 128:(j + 1) * 128], ["pbf%d" % b, "ident"], ["ps6"])
                evac(pTb[:, :, t * 128:(t + 1) * 128], psb(6)[:, 0:256].rearrange("p (a b) -> p a b", b=128), ["ps6"], ["pTb.%d" % t])
            wslots = [ph.bf16(16 * 512).rearrange("p (a b) -> p a b", b=512) for _ in range(2)]
            wpp = ph.bf16(2 * D).rearrange("p (a b) -> p a b", b=D)
            P.op("pool", lambda e, l=l: e.dma_start(out=wpp, in_=wpp_d[l, :, :].rearrange("(kc p) n -> p kc n", p=128)), r=[], w=["wpp"], stream="w.wpp")
            hsl = [ph.f32(512) for _ in range(4)]
            gt = [ph.f32(512) for _ in range(2)]
            gi_ = [0]
            for g in range(4):
                sl = g % 2

                def epi(t, pap, pres, g=g):
                    s = hi[0]; hi[0] = (s + 1) % 4
                    k = gi_[0]; gi_[0] ^= 1
                    pbank = 6 + k
                    for kc in range(2):
                        mm(ps[pbank][:, :], pTb[:, kc, t * 128:(t + 1) * 128], wpp[:, kc, g * 512:(g + 1) * 512], kc == 0, kc == 1,
                           ["pTb.%d" % t, "wpp"], ["ps%d" % pbank])
                    dma("act", hsl[s], hbuf_d[t * 128:(t + 1) * 128, g * 512:(g + 1) * 512], [], ["hsl%d" % s], "lh%d" % s)
                    act(gt[k], pap, AF.Sigmoid, [pres], ["gt%d" % k])
                    tt("dve", gt[k], gt[k], ps[pbank][:, :], ALU.mult, ["gt%d" % k, "ps%d" % pbank], ["gt%d" % k])
                    tt("pool", hsl[s], hsl[s], gt[k], ALU.add, ["hsl%d" % s, "gt%d" % k], ["hsl%d" % s])
                    dma("sp", hdst[t * 128:(t + 1) * 128, g * 512:(g + 1) * 512], hsl[s], ["hsl%d" % s], [], "sh%d" % s)

                gemm(wgate_d[l, :, g * 512:(g + 1) * 512], 16, 512, wslots[sl], "wslot%d" % sl, A, a_res, epi)
            barrier()

        streams = sorted(P.stream_last.keys())
        sems = {}
        for e in Prog.ENGS:
            sems[e] = es.enter_context(nc.semaphore("sem_" + e))
        ssem = {}
        for i, s in enumerate(streams):
            ssem[s] = es.enter_context(nc.semaphore("ds%d" % i))
        P.emit(nc, sems, ssem)
    return nc


_CACHE = {}


def _host_inputs(L, TOK, x, p, positions, attn_norm, w_in, sgu_norm, w_spatial, b_spatial, conv_w, conv_b,
                 kv_norm, w_ukv, q_nope_norm, q_rope_norm, k_nope_norm, k_rope_norm,
                 out_norm, w_out, ple_norm, w_ple_gate, w_ple_proj):
    f = np.float32
    NT = TOK // 128
    a = lambda v: np.ascontiguousarray(np.asarray(v))
    x = a(x); p = a(p); positions = a(positions)
    w_in = a(w_in)[:L]; w_ukv = a(w_ukv)[:L]; w_out = a(w_out)[:L]; w_gate = a(w_ple_gate)[:L]; w_pp = a(w_ple_proj)[:L]
    w_spT = a(np.transpose(np.asarray(w_spatial)[:L], (0, 3, 1, 2)).reshape(L, 128, 512))
    out_norm = np.asarray(out_norm)[:L]
    vec = np.concatenate([out_norm[:, 0:1024], np.asarray(sgu_norm)[:L].reshape(L, 512), np.asarray(conv_w)[:L].reshape(L, 1536),
                          np.asarray(conv_b)[:L], np.asarray(kv_norm)[:L], np.asarray(q_nope_norm)[:L], np.asarray(q_rope_norm)[:L],
                          np.asarray(k_nope_norm)[:L], np.asarray(k_rope_norm)[:L]], axis=1).astype(f)
    assert vec.shape[1] == NV
    gn = a(np.stack([np.asarray(attn_norm)[:L], np.asarray(ple_norm)[:L]], axis=1).astype(f))
    gc = np.transpose(out_norm[:, 1024:2048].reshape(L, 8, 128), (0, 2, 1))
    bs = np.transpose(np.asarray(b_spatial)[:L], (0, 2, 1))
    cols = a(np.concatenate([gc, bs], axis=2).astype(f))
    cmat = np.zeros((128, 640), f)
    cmat[:, 0:128] = np.eye(128, dtype=f)
    for m in range(127):
        cmat[m, 128 + m + 1] = 1.0
        cmat[m + 1, 256 + m] = 1.0
    cmat[127, 384 + 0] = 1.0
    cmat[0, 512 + 127] = 1.0
    invf = (1.0 / (10000.0 ** (np.arange(0, 64, 2, dtype=f) / f(64)))).astype(f)
    invf = a(np.broadcast_to(invf[None, :], (128, 32)))
    shared = dict(w_in=w_in, w_ukv=w_ukv, w_out=w_out, w_gate=w_gate, w_pp=w_pp, w_spT=w_spT, vec=vec, gn=gn, cols=cols,
                  cmat=cmat, invf=invf)
    maps = []
    for c in range(8):
        b, r = c // 4, c % 4
        sel = np.zeros((8, 256), f)
        if r > 0:
            sel[2 * (r - 1) + 1, 0] = 1.0
        if r < 3:
            sel[2 * (r + 1), 128 + 127] = 1.0
        m = dict(shared)
        m["x"] = a(x[b, r * TOK:(r + 1) * TOK, :])
        m["p"] = a(p[:L, b, r * TOK:(r + 1) * TOK, :])
        m["pos"] = a(positions[b, r * TOK:(r + 1) * TOK].reshape(NT, 128).T.astype(np.int32))
        m["sel"] = sel
        maps.append(m)
    return maps


def run(L, TOK, inputs, trace=False):
    key = (L, TOK)
    if key not in _CACHE:
        _CACHE[key] = build(L, TOK)
    nc = _CACHE[key]
    maps = _host_inputs(L, TOK, **inputs)
    res = run_bass_kernel_spmd(nc, maps, core_ids=list(range(8)), trace=trace)
    S = 4 * TOK
    out = np.empty((2, S, D), np.float32)
    for c in range(8):
        b, r = c // 4, c % 4
        out[b, r * TOK:(r + 1) * TOK, :] = res.results[c]["out"]
    return out, res


def kernel(**inputs):
    out, _ = run(4, 2048, inputs)
    return out
```

```python
import contextlib
import numpy as np
import concourse.bass as bass
import concourse.mybir as mybir
from concourse.bass_utils import run_bass_kernel_spmd

F32 = mybir.dt.float32
BF16 = mybir.dt.bfloat16
I32 = mybir.dt.int32
AF = mybir.ActivationFunctionType
ALU = mybir.AluOpType
AX = mybir.AxisListType

D = 2048
INW = 6720
NPROJ = 5696
EPS = 1e-6
PI = float(np.pi)
SC = float(192 ** -0.5)
V_OA, V_OB, V_SGU, V_CW0, V_CW1, V_CW2, V_CB, V_KV, V_QN, V_QR, V_KN, V_KR, NV = (
    0, 512, 1024, 1536, 2048, 2560, 3072, 3584, 4096, 4224, 4288, 4416, 4480)


class Ins:
    __slots__ = ("idx", "eng", "fn", "waits", "stream", "inc", "signal", "sigval")


class Prog:
    ENGS = ("pe", "act", "dve", "pool", "sp")

    def __init__(self):
        self.ins = []
        self.last_w = {}
        self.readers = {}
        self.stream_last = {}
        self.stream_inc = {}
        self.eng_last = {e: None for e in self.ENGS}
        self.waited = {e: {} for e in self.ENGS}
        self.pending = {e: [] for e in self.ENGS}

    def _key(self, p):
        return ("s", p.stream) if p.stream is not None else ("e", p.eng)

    def op(self, eng, fn, r=(), w=(), stream=None, inc=16, extra=()):
        i = Ins()
        i.idx = len(self.ins); i.eng = eng; i.fn = fn; i.stream = stream; i.inc = inc
        i.signal = False; i.sigval = 0
        deps = set(extra)
        for x in r:
            if x in self.last_w:
                deps.add(self.last_w[x])
        for x in w:
            if x in self.last_w:
                deps.add(self.last_w[x])
            rd = self.readers.get(x)
            if rd:
                deps.update(rd.values())
        deps.update(self.pending[eng]); self.pending[eng] = []
        me_key = ("s", stream) if stream is not None else ("e", eng)
        for x in r:
            self.readers.setdefault(x, {})[me_key] = i.idx
        for x in w:
            self.last_w[x] = i.idx
            self.readers[x] = {}
        waits = []
        wd = self.waited[eng]
        best = {}
        for d in deps:
            k = self._key(self.ins[d])
            if k == ("e", "pe") and eng == "pe" and stream is None:
                continue
            if best.get(k, -1) < d:
                best[k] = d
        for k, d in sorted(best.items(), key=lambda kv: kv[1]):
            if wd.get(k, -1) >= d:
                continue
            wd[k] = d
            self.ins[d].signal = True
            waits.append(d)
        i.waits = waits
        self.ins.append(i)
        if stream is not None:
            self.stream_last[stream] = i.idx
            self.stream_inc[stream] = inc
        self.eng_last[eng] = i.idx
        return i.idx

    def barrier(self, fn):
        extra = [v for v in self.eng_last.values() if v is not None]
        extra += [v for s, v in self.stream_last.items() if not s.startswith("cc")]
        b = self.op("dve", fn, extra=extra)
        for e in self.ENGS:
            if e != "dve":
                self.pending[e].append(b)
        return b

    def emit(self, nc, sem_of_eng, sem_of_stream):
        cnt = {}
        for i in self.ins:
            k = self._key(i)
            if i.stream is not None:
                cnt[k] = cnt.get(k, 0) + i.inc
                i.sigval = cnt[k]
            elif i.signal:
                cnt[k] = cnt.get(k, 0) + 1
                i.sigval = cnt[k]
        self.final_counts = cnt

        def semof(p):
            return sem_of_stream[p.stream] if p.stream is not None else sem_of_eng[p.eng]

        def run(engname, e):
            for i in self.ins:
                if i.eng != engname:
                    continue
                for d in i.waits:
                    p = self.ins[d]
                    e.wait_ge(semof(p), p.sigval)
                inst = i.fn(e)
                if i.stream is not None:
                    inst.then_inc(sem_of_stream[i.stream], i.inc)
                elif i.signal:
                    inst.then_inc(sem_of_eng[i.eng], 1)
            if engname == "sp":
                for s, last in self.stream_last.items():
                    e.wait_ge(sem_of_stream[s], self.ins[last].sigval)

        with nc.Block() as block:
            @block.tensor
            def _(e):
                run("pe", e)

            @block.scalar
            def _(e):
                run("act", e)

            @block.vector
            def _(e):
                run("dve", e)

            @block.gpsimd
            def _(e):
                run("pool", e)

            @block.sync
            def _(e):
                run("sp", e)


class Arena:
    def __init__(self, ap_f32, nf32):
        self.ap = ap_f32; self.n = nf32; self.off = 0

    def reset(self):
        self.off = 0

    def f32(self, n):
        assert self.off + n <= self.n, ("arena overflow", self.off, n, self.n)
        a = self.ap[:, self.off:self.off + n]
        self.off += n
        return a

    def bf16(self, n):
        m = (n + 1) // 2
        return self.f32(m).bitcast(BF16)[:, 0:n]


def build(L, TOK):
    NT = TOK // 128
    NB = TOK // 512
    SEQ = 4 * TOK
    ROWS = 8 * 128 + 8 * 128 + 64 + 2
    K_OFF, V_OFF, KR_OFF, HALO = 0, 1024, 2048, 2112
    CH = max(128, ((1 << 20) // (TOK * 2)) // 128 * 128)
    NCH = (ROWS + CH - 1) // CH
    GW = TOK
    VC = TOK // 128
    nc = bass.Bass("TRN2", target_bir_lowering=False)

    def din(name, shape, dt=F32):
        return nc.dram_tensor(name, list(shape), dt, kind="ExternalInput").ap()

    x_d = din("x", [TOK, D])
    p_d = din("p", [L, TOK, 256])
    pos_d = din("pos", [128, NT], I32)
    win_d = din("w_in", [L, D, INW])
    wukv_d = din("w_ukv", [L, 512, 2048])
    wout_d = din("w_out", [L, D, D])
    wgate_d = din("w_gate", [L, D, D])
    wpp_d = din("w_pp", [L, 256, D])
    wsp_d = din("w_spT", [L, 128, 512])
    vec_d = din("vec", [L, NV])
    gn_d = din("gn", [L, 2, D])
    cols_d = din("cols", [L, 128, 12])
    sel_d = din("sel", [8, 256])
    cmat_d = din("cmat", [128, 640])
    invf_d = din("invf", [128, 32])
    out_d = nc.dram_tensor("out", [TOK, D], F32, kind="ExternalOutput").ap()
    proj_d = nc.dram_tensor("proj", [TOK, NPROJ], F32).ap()
    zct_d = nc.dram_tensor("zct", [1024, TOK], F32).ap()
    hbuf_d = nc.dram_tensor("hbuf", [TOK, D], F32).ap()
    chn = [min(CH, ROWS - k * CH) for k in range(NCH)]
    gin_ts = [nc.dram_tensor("gin%d" % k, [chn[k], GW], BF16) for k in range(NCH)]
    gout_ts = [nc.dram_tensor("gout%d" % k, [4 * chn[k], GW], BF16) for k in range(NCH)]

    def gin_rows(r0, n):
        k = r0 // CH
        assert (r0 + n - 1) // CH == k
        return gin_ts[k].ap()[r0 - k * CH:r0 - k * CH + n, :]

    def gout_rows(rk, r0, n):
        k = r0 // CH
        assert (r0 + n - 1) // CH == k
        o = rk * chn[k] + r0 - k * CH
        return gout_ts[k].ap()[o:o + n, :]

    def gres(r0):
        return "gout.%d" % (r0 // CH)

    P = Prog()
    es = contextlib.ExitStack()
    with es:
        def sb(name, shape, dt):
            return es.enter_context(nc.sbuf_tensor("s_" + name, list(shape), dt))

        A_t = sb("A", [128, 16, TOK], BF16)
        PHN = 27136
        PH_t = sb("PH", [128, PHN], F32)
        vec_t = sb("vec", [128, NV], F32)
        cols_t = sb("cols", [128, 12], F32)
        ident = sb("ident", [128, 128], BF16)
        ones_b = sb("ones_b", [128, 128], BF16)
        ones_f = sb("ones_f", [128, 128], F32)
        shm = sb("shm", [128, 4, 128], BF16)
        sel_b = sb("sel_b", [8, 256], BF16)
        cosT = sb("cosT", [128, NT, 32], F32)
        sinT = sb("sinT", [128, NT, 32], F32)
        invf = sb("invf", [128, 32], F32)
        posi = sb("posi", [128, NT], I32)
        neghalf = sb("neghalf", [128, 1], F32)
        dummy = sb("dummy", [128, 8], F32)
        wsp_b = sb("wsp_b", [128, 4, 128], BF16)
        G_b = sb("G_b", [8, 512], BF16)
        ps = [es.enter_context(nc.psum_tensor("ps%d" % b, [128, 512], F32)) for b in range(8)]
        A = A_t[:]
        ph = Arena(PH_t[:], PHN)

        def psb(b):
            return ps[b][:].bitcast(BF16)

        def dma(q, out, in_, r, w, stream):
            P.op(q, lambda e, out=out, in_=in_: e.dma_start(out=out, in_=in_), r=r, w=w, stream=stream)

        def act(out, in_, func, r, w, scale=None, accum=None, bias=None):
            kw = {}
            if scale is not None:
                kw["scale"] = scale
            if accum is not None:
                kw["accum_out"] = accum
            if bias is not None:
                kw["bias"] = bias
            P.op("act", lambda e, out=out, in_=in_, func=func, kw=kw: e.activation(out=out, in_=in_, func=func, **kw), r=r, w=w)

        def tt(eng, out, in0, in1, op, r, w):
            P.op(eng, lambda e, out=out, in0=in0, in1=in1, op=op: e.tensor_tensor(out=out, in0=in0, in1=in1, op=op), r=r, w=w)

        def ts(eng, out, in0, s1, s2, op0, op1, r, w):
            if op1 is None:
                P.op(eng, lambda e, out=out, in0=in0, s1=s1, op0=op0: e.tensor_scalar(out=out, in0=in0, scalar1=s1, scalar2=None, op0=op0), r=r, w=w)
            else:
                P.op(eng, lambda e, out=out, in0=in0, s1=s1, s2=s2, op0=op0, op1=op1: e.tensor_scalar(out=out, in0=in0, scalar1=s1, scalar2=s2, op0=op0, op1=op1), r=r, w=w)

        def stt(out, in0, scalar, in1, op0, op1, r, w):
            P.op("dve", lambda e, out=out, in0=in0, scalar=scalar, in1=in1, op0=op0, op1=op1:
                 e.scalar_tensor_tensor(out=out, in0=in0, scalar=scalar, in1=in1, op0=op0, op1=op1), r=r, w=w)

        def cp(eng, out, in_, r, w):
            if eng == "act":
                P.op("act", lambda e, out=out, in_=in_: e.copy(out=out, in_=in_), r=r, w=w)
            else:
                P.op(eng, lambda e, out=out, in_=in_: e.tensor_copy(out=out, in_=in_), r=r, w=w)

        def mm(out, lhsT, rhs, start, stop, r, w):
            P.op("pe", lambda e, out=out, lhsT=lhsT, rhs=rhs, start=start, stop=stop:
                 e.matmul(out, lhsT=lhsT, rhs=rhs, start=start, stop=stop), r=r, w=w)

        def tr(out, in_, r, w, idn=None):
            idn = ident[:] if idn is None else idn
            P.op("pe", lambda e, out=out, in_=in_, idn=idn: e.transpose(out=out, in_=in_, identity=idn), r=r, w=w)

        def red(out, in_, r, w):
            P.op("dve", lambda e, out=out, in_=in_: e.tensor_reduce(out=out, in_=in_, axis=AX.X, op=ALU.add), r=r, w=w)

        def rstd_of(ssq, n, rname, width, tmp, out):
            ts("dve", tmp[0], ssq[0], 1.0 / n, EPS, ALU.mult, ALU.add, r=[ssq[1]], w=[tmp[1]])
            tt("pool", out[0], tmp[0], neghalf[:, 0:1].to_broadcast([128, width]), ALU.pow, r=[tmp[1]], w=[out[1]])

        def barrier():
            P.barrier(lambda e: e.memset(dummy[:, 0:1], 0.0))
            ph.reset()

        evac_flip = [0]

        def evac(out, in_, r, w):
            evac_flip[0] ^= 1
            cp("act" if evac_flip[0] else "dve", out, in_, r, w)

        cm_f = ph.f32(640)
        dma("sp", cm_f, cmat_d[:, :], [], ["cm_f"], "i0")
        cp("dve", ident[:], cm_f[:, 0:128], ["cm_f"], ["ident"])
        cp("dve", shm[:].rearrange("p a b -> p (a b)"), cm_f[:, 128:640], ["cm_f"], ["shm"])
        P.op("pool", lambda e: e.memset(ones_b[:], 1.0), w=["ones_b"])
        P.op("pool", lambda e: e.memset(ones_f[:], 1.0), w=["ones_f"])
        P.op("pool", lambda e: e.memset(neghalf[:], -0.5), w=["neghalf"])
        sel_f = ph.f32(256)
        dma("sp", sel_f[0:8, :], sel_d[:, :], [], ["sel_f"], "i1")
        cp("dve", sel_b[:], sel_f[0:8, :], ["sel_f"], ["sel_b"])
        dma("sp", invf[:], invf_d[:, :], [], ["invf"], "i2")
        dma("sp", posi[:], pos_d[:, :], [], ["posi"], "i3")
        posf = ph.f32(NT)
        cp("dve", posf, posi[:], ["posi"], ["posf"])
        ang = ph.f32(NT * 32).rearrange("p (a b) -> p a b", b=32)
        tt("dve", ang, posf.unsqueeze(2).to_broadcast([128, NT, 32]),
           invf[:].unsqueeze(1).to_broadcast([128, NT, 32]), ALU.mult, ["posf", "invf"], ["ang"])
        uu = ph.f32(NT * 32).rearrange("p (a b) -> p a b", b=32)
        ki = ph.f32(NT * 32).bitcast(I32).rearrange("p (a b) -> p a b", b=32)
        kf = ph.f32(NT * 32).rearrange("p (a b) -> p a b", b=32)
        rr = ph.f32(NT * 32).rearrange("p (a b) -> p a b", b=32)
        mk = ph.f32(NT * 32).rearrange("p (a b) -> p a b", b=32)
        C1 = 6.28125
        C2 = float(2 * np.pi - 6.28125)
        for which, dst in ((0, sinT), (1, cosT)):
            if which == 1:
                ts("dve", ang, ang, PI / 2, None, ALU.add, None, ["ang"], ["ang"])
            ts("dve", uu, ang, 1.0 / (2 * PI), None, ALU.mult, None, ["ang"], ["uu"])
            cp("dve", ki, uu, ["uu"], ["ki"])
            cp("dve", kf, ki, ["ki"], ["kf"])
            stt(rr, kf, -C1, ang, ALU.mult, ALU.add, ["kf", "ang"], ["rr"])
            stt(rr, kf, -C2, rr, ALU.mult, ALU.add, ["kf", "rr"], ["rr"])
            ts("dve", mk, rr, -PI, None, ALU.is_lt, None, ["rr"], ["mk"])
            stt(rr, mk, 2 * PI, rr, ALU.mult, ALU.add, ["mk", "rr"], ["rr"])
            ts("dve", mk, rr, PI, None, ALU.is_gt, None, ["rr"], ["mk"])
            stt(rr, mk, -2 * PI, rr, ALU.mult, ALU.add, ["mk", "rr"], ["rr"])
            ts("dve", rr, rr, PI, -PI, ALU.min, ALU.max, ["rr"], ["rr"])
            act(dst[:], rr, AF.Sin, ["rr"], ["tab%d" % which])
        barrier()

        def norm_transpose(l, src_d, gidx):
            gnorm = ph.f32(D)
            dma("sp", gnorm, gn_d[l, gidx, :].partition_broadcast(128), [], ["gnorm"], "lgn")
            hts = [ph.f32(D) for _ in range(2)]
            hns = [ph.bf16(D) for _ in range(2)]
            junk = ph.bf16(D)
            st = ph.f32(8)
            def nt_s1(t):
                b = t % 2
                dma("sp", hts[b], src_d[t * 128:(t + 1) * 128, :], [], ["ht%d" % b], "ldh%d" % b)
                act(junk, hts[b], AF.Square, ["ht%d" % b], ["junk", "ssq%d" % b], accum=st[:, b:b + 1])
                rstd_of((st[:, b:b + 1], "ssq%d" % b), D, None, 1, (st[:, 2 + b:3 + b], "tq%d" % b), (st[:, 4 + b:5 + b], "rs%d" % b))
                stt(hns[b], hts[b], st[:, 4 + b:5 + b], gnorm, ALU.mult, ALU.mult, ["ht%d" % b, "rs%d" % b, "gnorm"], ["hn%d" % b])

            def nt_s2(t):
                b = t % 2
                for kg in range(4):
                    bank = 4 + (kg % 2)
                    for j in range(4):
                        kc = kg * 4 + j
                        tr(psb(bank)[:, j * 128:(j + 1) * 128], hns[b][:, kc * 128:(kc + 1) * 128], ["hn%d" % b, "ident"], ["ps%d" % bank])
                    evac(A[:, kg * 4:kg * 4 + 4, t * 128:(t + 1) * 128],
                         psb(bank)[:, 0:512].rearrange("p (a b) -> p a b", b=128), ["ps%d" % bank], ["A.%d.%d" % (t, kg)])

            for t in range(NT + 1):
                if t < NT:
                    nt_s1(t)
                if t >= 1:
                    nt_s2(t - 1)

        gemm_bank = [0]

        def gemm(w_src, KC, ncols, wslot, wres, actT, act_res, epilogue, yform=False):
            P.op("pool", lambda e: e.dma_start(out=wslot[:, 0:KC, 0:ncols], in_=w_src.rearrange("(kc p) n -> p kc n", p=128)),
                 r=[], w=[wres], stream="w." + wres)
            if not yform:
                for t in range(NT):
                    bank = gemm_bank[0]; gemm_bank[0] = (gemm_bank[0] + 1) % 4
                    for kc in range(KC):
                        mm(ps[bank][:, 0:ncols], actT[:, kc, t * 128:(t + 1) * 128], wslot[:, kc, 0:ncols],
                           kc == 0, kc == KC - 1, [wres] + act_res(t), ["ps%d" % bank])
                    epilogue(t, ps[bank][:, 0:ncols], "ps%d" % bank)
            else:
                for fc in range(ncols // 128):
                    for qb in range(NB):
                        bank = gemm_bank[0]; gemm_bank[0] = (gemm_bank[0] + 1) % 4
                        rs = []
                        for t in range(4 * qb, 4 * qb + 4):
                            rs += act_res(t)
                        for kc in range(KC):
                            mm(ps[bank][:, :], wslot[:, kc, fc * 128:(fc + 1) * 128], actT[:, kc, qb * 512:(qb + 1) * 512],
                               kc == 0, kc == KC - 1, [wres] + rs, ["ps%d" % bank])
                        epilogue((fc, qb), ps[bank][:, :], "ps%d" % bank)

        def a_res(t):
            return ["A.%d.%d" % (t, kg) for kg in range(4)]

        KSTOP = 99
        for l in range(L):
            if KSTOP <= 0:
                break
            hsrc = x_d if l == 0 else hbuf_d
            hdst = out_d if l == L - 1 else hbuf_d
            dma("sp", vec_t[:], vec_d[l, :].partition_broadcast(128), [], ["vec"], "lvec")
            dma("sp", cols_t[:], cols_d[l, :, :], [], ["cols"], "lcols")
            P.op("pool", lambda e, l=l: e.dma_start(out=wsp_b[:].rearrange("p a b -> p (a b)"), in_=wsp_d[l, :, :]), r=[], w=["wsp_b"], stream="w.wsp")

            norm_transpose(l, hsrc, 0)
            barrier()

            if KSTOP <= 1:
                break
            wslots = [ph.bf16(16 * 512).rearrange("p (a b) -> p a b", b=512) for _ in range(3)]
            stg = [ph.f32(512) for _ in range(4)]
            stg_i = [0]
            groups = [(5120, 512, "plain"), (5632, 64, "plain")]
            groups += [(c0, 512, "plain") for c0 in (2048, 2560)]
            groups += [(0, 512, "plain"), (512, 512, "plain"), (1024, 512, "silu"), (1536, 512, "plain"),
                       (3072, 512, "silu"), (3584, 512, "plain"), (4096, 512, "plain"), (4608, 512, "plain"),
                       (5696, 512, "zc"), (6208, 512, "zc")]
            for gi, (c0, ncols, kind) in enumerate(groups):
                slot = gi % 3
                wres = "wslot%d" % slot

                def epi(t, pap, pres, c0=c0, ncols=ncols, kind=kind):
                    s = stg_i[0]; stg_i[0] = (s + 1) % 4
                    sres = "stg%d" % s
                    if kind == "plain":
                        evac(stg[s][:, 0:ncols], pap, [pres], [sres])
                        dma("sp", proj_d[t * 128:(t + 1) * 128, c0:c0 + ncols], stg[s][:, 0:ncols], [sres], [], "st%d" % s)
                    elif kind == "silu":
                        act(stg[s][:, 0:ncols], pap, AF.Silu, [pres], [sres])
                        dma("sp", proj_d[t * 128:(t + 1) * 128, c0:c0 + ncols], stg[s][:, 0:ncols], [sres], [], "st%d" % s)
                    else:
                        fc, qb = t
                        act(stg[s][:, :], pap, AF.Silu, [pres], [sres])
                        r0 = (c0 - 5696) + fc * 128
                        dma("sp", zct_d[r0:r0 + 128, qb * 512:(qb + 1) * 512], stg[s][:, :], [sres], [], "st%d" % s)

                gemm(win_d[l, :, c0:c0 + ncols], 16, ncols, wslots[slot], wres, A, a_res, epi, yform=(kind == "zc"))
            barrier()

            if KSTOP <= 2:
                break
            rowres = {}
            for h in range(8):
                rowres.setdefault((K_OFF + h * 128) // CH, []).append("gin.k%d" % h)
                rowres.setdefault((V_OFF + h * 128) // CH, []).append("gin.v%d" % h)
            rowres.setdefault(KR_OFF // CH, []).append("gin.kr")
            rowres.setdefault(HALO // CH, []).append("gin.h0")
            rowres.setdefault((HALO + 1) // CH, []).append("gin.h1")
            cc_state = {"written": set(), "issued": set(), "pending": [], "tick": 0, "last": -100}

            def cc_written(res):
                cc_state["written"].add(res)
                for k in range(NCH):
                    if k not in cc_state["issued"] and k not in cc_state["pending"] and all(x in cc_state["written"] for x in rowres[k]):
                        cc_state["pending"].append(k)

            def cc_issue(k):
                cc_state["issued"].add(k)
                P.op("pool", lambda e, k=k: e.collective_compute("AllGather", ALU.bypass, replica_groups=[[0, 1, 2, 3], [4, 5, 6, 7]],
                                                                ins=[gin_ts[k].ap()], outs=[gout_ts[k].ap()]),
                     r=rowres[k], w=["gout.%d" % k, "ccserial"], stream="cc%d" % k, inc=1)

            def cc_tick(spacing):
                cc_state["tick"] += 1
                if cc_state["pending"] and cc_state["tick"] - cc_state["last"] >= spacing:
                    cc_state["last"] = cc_state["tick"]
                    cc_issue(cc_state["pending"].pop(0))

            def cc_flush_ready():
                pass

            def cc_flush_all():
                while cc_state["pending"]:
                    cc_issue(cc_state["pending"].pop(0))
                assert len(cc_state["issued"]) == NCH

            ckvT = ph.bf16(4 * TOK).rearrange("p (a b) -> p a b", b=TOK)
            krTs = ph.bf16(TOK)
            ck = [ph.f32(576) for _ in range(2)]
            ckn = [ph.bf16(512) for _ in range(2)]
            junk = ph.bf16(512)
            st = ph.f32(16)
            krn = [ph.f32(64) for _ in range(2)]
            rt = [ph.f32(32) for _ in range(4)]
            krr = [ph.bf16(64) for _ in range(2)]
            hb = [ph.f32(1024) for _ in range(2)]
            hx = [ph.bf16(512) for _ in range(2)]
            for i, t in enumerate((0, NT - 1)):
                dma("sp", hb[i], proj_d[t * 128:(t + 1) * 128, 2048:3072], [], ["hb%d" % i], "lhb%d" % i)
                tt("pool", hx[i], hb[i][:, 0:512], hb[i][:, 512:1024], ALU.mult, ["hb%d" % i], ["hx%d" % i])
            dma("sp", gin_rows(HALO, 1)[:, 0:512], hx[0][0:1, :], ["hx0"], ["gin.h0"], "st0")
            dma("sp", gin_rows(HALO + 1, 1)[:, 0:512], hx[1][127:128, :], ["hx1"], ["gin.h1"], "st1")
            if GW > 512:
                zrow = ph.bf16(GW)
                P.op("pool", lambda e: e.memset(zrow[0:2, 0:GW - 512], 0.0), w=["zrow"])
                dma("sp", gin_rows(HALO, 2)[:, 512:GW], zrow[0:2, 0:GW - 512], ["zrow"], ["gin.hz"], "st3")
                rowres[HALO // CH].append("gin.hz")
            cc_written("gin.h0"); cc_written("gin.h1"); cc_written("gin.hz")
            for t in range(NT):
                b = t % 2
                dma("sp", ck[b], proj_d[t * 128:(t + 1) * 128, 5120:5696], [], ["ck%d" % b], "ldh%d" % b)
                act(junk, ck[b][:, 0:512], AF.Square, ["ck%d" % b], ["junk", "sq%d" % b], accum=st[:, b:b + 1])
                act(junk[:, 0:64], ck[b][:, 512:576], AF.Square, ["ck%d" % b], ["junk", "sr%d" % b], accum=st[:, 2 + b:3 + b])
                rstd_of((st[:, b:b + 1], "sq%d" % b), 512, None, 1, (st[:, 4 + b:5 + b], "t1%d" % b), (st[:, 6 + b:7 + b], "r1%d" % b))
                rstd_of((st[:, 2 + b:3 + b], "sr%d" % b), 64, None, 1, (st[:, 8 + b:9 + b], "t2%d" % b), (st[:, 10 + b:11 + b], "r2%d" % b))
                stt(ckn[b], ck[b][:, 0:512], st[:, 6 + b:7 + b], vec_t[:, V_KV:V_KV + 512], ALU.mult, ALU.mult,
                    ["ck%d" % b, "r1%d" % b, "vec"], ["ckn%d" % b])
                stt(krn[b], ck[b][:, 512:576], st[:, 10 + b:11 + b], vec_t[:, V_KR:V_KR + 64], ALU.mult, ALU.mult,
                    ["ck%d" % b, "r2%d" % b, "vec"], ["krn%d" % b])
                x1 = krn[b][:, 0:32]; x2 = krn[b][:, 32:64]
                Ct = cosT[:, t, :]; St = sinT[:, t, :]
                tt("pool", rt[0], x1, Ct, ALU.mult, ["krn%d" % b, "tab1"], ["rt0"])
                tt("pool", rt[1], x2, St, ALU.mult, ["krn%d" % b, "tab0"], ["rt1"])
                tt("pool", krr[b][:, 0:32], rt[0], rt[1], ALU.subtract, ["rt0", "rt1"], ["krr%d" % b])
                tt("pool", rt[2], x2, Ct, ALU.mult, ["krn%d" % b, "tab1"], ["rt2"])
                tt("pool", rt[3], x1, St, ALU.mult, ["krn%d" % b, "tab0"], ["rt3"])
                tt("pool", krr[b][:, 32:64], rt[2], rt[3], ALU.add, ["rt2", "rt3", "krr%d" % b], ["krr%d" % b])
                for j in range(4):
                    tr(psb(4)[:, j * 128:(j + 1) * 128], ckn[b][:, j * 128:(j + 1) * 128], ["ckn%d" % b, "ident"], ["ps4"])
                evac(ckvT[:, :, t * 128:(t + 1) * 128], psb(4)[:, 0:512].rearrange("p (a b) -> p a b", b=128), ["ps4"], ["ckvT.%d" % t])
                tr(psb(5)[0:64, 0:128], krr[b][:, 0:64], ["krr%d" % b, "ident"], ["ps5"])
                evac(krTs[0:64, t * 128:(t + 1) * 128], psb(5)[0:64, 0:128], ["ps5"], ["krTs"])
            dma("sp", gin_rows(KR_OFF, 64), krTs[0:64, :], ["krTs"], ["gin.kr"], "st2")
            cc_written("gin.kr")
            wuk = [ph.bf16(4 * 512).rearrange("p (a b) -> p a b", b=512) for _ in range(2)]
            kst = [ph.bf16(2 * TOK).rearrange("p (a b) -> p a b", b=TOK) for _ in range(2)]
            vst = [ph.bf16(2 * TOK).rearrange("p (a b) -> p a b", b=TOK) for _ in range(2)]
            knb = [ph.bf16(128) for _ in range(4)]
            kq = ph.f32(12)
            kn_i = [0]
            for g in range(4):
                sl = g % 2

                def epi(t, pap, pres, g=g, sl=sl):
                    cc_tick(10)
                    for hh in range(2):
                        i = kn_i[0]; kn_i[0] = (i + 1) % 4
                        kps = pap[:, hh * 256:hh * 256 + 128]
                        vps = pap[:, hh * 256 + 128:hh * 256 + 256]
                        act(junk[:, 0:128], kps, AF.Square, [pres], ["junk", "ks%d" % i], accum=kq[:, i:i + 1])
                        rstd_of((kq[:, i:i + 1], "ks%d" % i), 128, None, 1, (kq[:, 4 + i:5 + i], "kt%d" % i), (kq[:, 8 + i:9 + i], "kr%d" % i))
                        stt(knb[i], kps, kq[:, 8 + i:9 + i], vec_t[:, V_KN:V_KN + 128], ALU.mult, ALU.mult,
                            [pres, "kr%d" % i, "vec"], ["knb%d" % i])
                        cp("act", vst[sl][:, hh, t * 128:(t + 1) * 128], vps, [pres], ["vst%d" % sl])
                        bank = 4 + i
                        tr(psb(bank)[:, 0:128], knb[i], ["knb%d" % i, "ident"], ["ps%d" % bank])
                        evac(kst[sl][:, hh, t * 128:(t + 1) * 128], psb(bank)[:, 0:128], ["ps%d" % bank], ["kst%d" % sl])

                gemm(wukv_d[l, :, g * 512:(g + 1) * 512], 4, 512, wuk[sl], "wuk%d" % sl, ckvT, lambda t: ["ckvT.%d" % t], epi)
                for hh in range(2):
                    h = 2 * g + hh
                    dma("sp", gin_rows(K_OFF + h * 128, 128), kst[sl][:, hh, :], ["kst%d" % sl], ["gin.k%d" % h], "sk%d" % sl)
                    dma("sp", gin_rows(V_OFF + h * 128, 128), vst[sl][:, hh, :], ["vst%d" % sl], ["gin.v%d" % h], "sv%d" % sl)
                    cc_written("gin.k%d" % h); cc_written("gin.v%d" % h)
            if KSTOP <= 3:
                break
            cc_flush_ready()
            barrier()

            if KSTOP <= 4:
                break
            uvz = [ph.f32(1536) for _ in range(2)]
            vsq = [ph.f32(512) for _ in range(2)]
            sa_ = [ph.f32(16) for _ in range(2)]
            vn = [ph.bf16(512) for _ in range(2)]
            t1_ = [ph.f32(512) for _ in range(2)]
            ya_ = [ph.f32(512) for _ in range(2)]
            junk_ = [ph.bf16(512) for _ in range(2)]
            yan = [ph.bf16(512) for _ in range(2)]
            def a_s1(t):
                b = t % 2
                cc_tick(4)
                sa = sa_[b]; t1 = t1_[b]; ya = ya_[b]; junk = junk_[b]; B_ = str(b)
                U = uvz[b][:, 0:512]; Vv = uvz[b][:, 512:1024]; Z = uvz[b][:, 1024:1536]
                dma("sp", uvz[b], proj_d[t * 128:(t + 1) * 128, 0:1536], [], ["uvz%d" % b], "ldh%d" % b)
                tt("pool", vsq[b], Vv, Vv, ALU.mult, ["uvz%d" % b], ["vsq" + B_])
                red(sa[:, 0:4], vsq[b].rearrange("p (a b) -> p a b", b=128), ["vsq" + B_], ["sa0" + B_])
                rstd_of((sa[:, 0:4], "sa0" + B_), 128, None, 4, (sa[:, 4:8], "sa1" + B_), (sa[:, 8:12], "sa2" + B_))
                for h in range(4):
                    stt(vn[b][:, h * 128:(h + 1) * 128], Vv[:, h * 128:(h + 1) * 128], sa[:, 8 + h:9 + h],
                        vec_t[:, V_SGU + h * 128:V_SGU + (h + 1) * 128], ALU.mult, ALU.mult, ["uvz%d" % b, "sa2" + B_, "vec"], ["vn%d" % b])
                for h in range(4):
                    mm(ps[b][:, h * 128:(h + 1) * 128], wsp_b[:, h, :], vn[b][:, h * 128:(h + 1) * 128], True, True, ["wsp_b", "vn%d" % b], ["ps" + B_])

            def a_s2(t):
                b = t % 2
                sa = sa_[b]; t1 = t1_[b]; ya = ya_[b]; junk = junk_[b]; B_ = str(b)
                U = uvz[b][:, 0:512]; Vv = uvz[b][:, 512:1024]; Z = uvz[b][:, 1024:1536]
                for h in range(4):
                    stt(t1[:, h * 128:(h + 1) * 128], ps[b][:, h * 128:(h + 1) * 128], cols_t[:, 8 + h:9 + h], U[:, h * 128:(h + 1) * 128],
                        ALU.add, ALU.mult, ["ps" + B_, "cols", "uvz%d" % b], ["t1" + B_])
                tt("pool", ya, t1, Z, ALU.mult, ["t1" + B_, "uvz%d" % b], ["ya" + B_])
                act(junk, ya, AF.Square, ["ya" + B_], ["junk" + B_, "sa3" + B_], accum=sa[:, 12:13])
                rstd_of((sa[:, 12:13], "sa3" + B_), 512, None, 1, (sa[:, 13:14], "sa4" + B_), (sa[:, 14:15], "sa5" + B_))
                stt(yan[b], ya, sa[:, 14:15], vec_t[:, V_OA:V_OA + 512], ALU.mult, ALU.mult, ["ya" + B_, "sa5" + B_, "vec"], ["yan%d" % b])
                for j in range(4):
                    tr(psb(4 + b)[:, j * 128:(j + 1) * 128], yan[b][:, j * 128:(j + 1) * 128], ["yan%d" % b, "ident"], ["ps%d" % (4 + b)])
                evac(A[:, 0:4, t * 128:(t + 1) * 128], psb(4 + b)[:, 0:512].rearrange("p (a b) -> p a b", b=128), ["ps%d" % (4 + b)], ["A.%d.0" % t])

            for t in range(NT + 1):
                if t < NT:
                    a_s1(t)
                if t >= 1:
                    a_s2(t - 1)
            barrier()

            if KSTOP <= 5:
                break
            cc_flush_all()
            for rk in range(4):
                dma("sp", G_b[2 * rk:2 * rk + 2, :], gout_rows(rk, HALO, 2)[:, 0:512],
                    [gres(HALO)], ["G_b.%d" % rk], "lgb%d" % rk)
            b4 = [ph.f32(2048) for _ in range(3)]
            xf = [ph.f32(512) for _ in range(3)]
            xb = [ph.bf16(512) for _ in range(3)]
            cvs = [[ph.f32(512) for _ in range(4)] for _ in range(2)]
            junk_ = [ph.bf16(512) for _ in range(2)]
            sbs = [ph.f32(4) for _ in range(2)]
            ybn = [ph.bf16(512) for _ in range(2)]
            for t in range(NT + 1):
                if t < NT:
                    b = t % 3
                    dma("sp", b4[b], proj_d[t * 128:(t + 1) * 128, 1536:3584], [], ["b4%d" % b], "ldb%d" % b)
                    tt("pool", xf[b], b4[b][:, 512:1024], b4[b][:, 1024:1536], ALU.mult, ["b4%d" % b], ["xf%d" % b])
                    cp("act", xb[b], xf[b], ["xf%d" % b], ["xb%d" % b])
                j = t - 1
                if j < 0:
                    continue
                jb = j % 3; pb_ = (j - 1) % 3; nb_ = (j + 1) % 3
                o = j % 2; O_ = str(o)
                cv, ca, cb2, yb = cvs[o]; junk = junk_[o]; sb_ = sbs[o]
                b1, b2 = (1, 2) if o == 0 else (6, 7)
                p1 = "ps%d" % b1; p2 = "ps%d" % b2
                mm(ps[b1][:, :], shm[:, 0, :], xb[jb], True, False, ["shm", "xb%d" % jb], [p1])
                if j > 0:
                    mm(ps[b1][:, :], shm[:, 2, :], xb[pb_], False, True, ["shm", "xb%d" % pb_], [p1])
                else:
                    mm(ps[b1][:, :], sel_b[0:8, 0:128], G_b[0:8, :], False, True, ["sel_b", "G_b.0", "G_b.1", "G_b.2", "G_b.3"], [p1])
                mm(ps[b2][:, :], shm[:, 1, :], xb[jb], True, False, ["shm", "xb%d" % jb], [p2])
                if j < NT - 1:
                    mm(ps[b2][:, :], shm[:, 3, :], xb[nb_], False, True, ["shm", "xb%d" % nb_], [p2])
                else:
                    mm(ps[b2][:, :], sel_b[0:8, 128:256], G_b[0:8, :], False, True, ["sel_b", "G_b.0", "G_b.1", "G_b.2", "G_b.3"], [p2])
                tt("pool", cv, xf[jb], vec_t[:, V_CW1:V_CW1 + 512], ALU.mult, ["xf%d" % jb, "vec"], ["cv" + O_])
                tt("pool", cv, cv, vec_t[:, V_CB:V_CB + 512], ALU.add, ["cv" + O_, "vec"], ["cv" + O_])
                tt("dve", ca, ps[b1][:, :], vec_t[:, V_CW0:V_CW0 + 512], ALU.mult, [p1, "vec"], ["ca" + O_])
                tt("dve", cb2, ps[b2][:, :], vec_t[:, V_CW2:V_CW2 + 512], ALU.mult, [p2, "vec"], ["cb2" + O_])
                tt("dve", cv, cv, ca, ALU.add, ["cv" + O_, "ca" + O_], ["cv" + O_])
                tt("dve", cv, cv, cb2, ALU.add, ["cv" + O_, "cb2" + O_], ["cv" + O_])
                tt("dve", yb, cv, b4[jb][:, 0:512], ALU.mult, ["cv" + O_, "b4%d" % jb], ["yb" + O_])
                tt("pool", yb, yb, b4[jb][:, 1536:2048], ALU.mult, ["yb" + O_, "b4%d" % jb], ["yb" + O_])
                act(junk, yb, AF.Square, ["yb" + O_], ["junk" + O_, "sb0" + O_], accum=sb_[:, 0:1])
                rstd_of((sb_[:, 0:1], "sb0" + O_), 512, None, 1, (sb_[:, 1:2], "sb1" + O_), (sb_[:, 2:3], "sb2" + O_))
                stt(ybn[o], yb, sb_[:, 2:3], vec_t[:, V_OB:V_OB + 512], ALU.mult, ALU.mult, ["yb" + O_, "sb2" + O_, "vec"], ["ybn%d" % o])
                for q in range(4):
                    tr(psb(4 + o)[:, q * 128:(q + 1) * 128], ybn[o][:, q * 128:(q + 1) * 128], ["ybn%d" % o, "ident"], ["ps%d" % (4 + o)])
                evac(A[:, 4:8, j * 128:(j + 1) * 128], psb(4 + o)[:, 0:512].rearrange("p (a b) -> p a b", b=128), ["ps%d" % (4 + o)], ["A.%d.1" % j])
            barrier()

            if KSTOP <= 6:
                break
            krT_all = ph.bf16(SEQ)
            for rk in range(4):
                dma("sp", krT_all[0:64, rk * TOK:(rk + 1) * TOK], gout_rows(rk, KR_OFF, 64), [gres(KR_OFF)], ["krT.%d" % rk], "lkr%d" % rk)
            P.op("pool", lambda e: e.memset(krT_all[64:128, :], 0.0), w=["krT.pad"])
            kTr = [ph.bf16(TOK) for _ in range(3)]
            Vr = [ph.bf16(TOK).rearrange("p (a b) -> p a b", b=128) for _ in range(3)]
            qt = ph.f32(1536)
            qsq = ph.f32(1536)
            qs = ph.f32(48)
            qnb = ph.bf16(1024)
            qrf = ph.f32(512)
            qrb = ph.bf16(512)
            r4 = [qsq[:, i * 256:(i + 1) * 256] for i in range(4)]
            QnT = ph.bf16(8 * 512).rearrange("p (a b) -> p a b", b=512)
            QrT = ph.bf16(8 * 512).rearrange("p (a b) -> p a b", b=512)
            P.op("pool", lambda e: e.memset(QrT[64:128, :, :], 0.0), w=["QrT.pad"])
            zc = [ph.f32(512) for _ in range(2)]
            yc = ph.f32(8 * 512).rearrange("p (a b) -> p a b", b=512)
            pT = [ph.bf16(512) for _ in range(8)]
            d4 = ph.f32(512)
            rden = ph.f32(512)
            ysq = rden
            rc = qsq[:, 0:512]; rc2 = qsq[:, 512:1024]
            for qb in range(NB):
                for ti in range(4):
                    t = qb * 4 + ti
                    dma("sp", qt, proj_d[t * 128:(t + 1) * 128, 3584:5120], [], ["qt"], "ldq")
                    q3 = qt.rearrange("p (h d) -> p h d", d=192)
                    s3 = qsq.rearrange("p (h d) -> p h d", d=192)
                    tt("dve", qsq, qt, qt, ALU.mult, ["qt"], ["qsq", "r40", "r41", "r42", "r43"])
                    red(qs[:, 0:8], s3[:, :, 0:128], ["qsq"], ["qs0"])
                    red(qs[:, 8:16], s3[:, :, 128:192], ["qsq"], ["qs1"])
                    rstd_of((qs[:, 0:8], "qs0"), 128, None, 8, (qs[:, 16:24], "qs2"), (qs[:, 32:40], "qs4"))
                    rstd_of((qs[:, 8:16], "qs1"), 64, None, 8, (qs[:, 24:32], "qs3"), (qs[:, 40:48], "qs5"))
                    tt("dve", q3[:, :, 0:128], q3[:, :, 0:128], qs[:, 32:40].unsqueeze(2).to_broadcast([128, 8, 128]), ALU.mult,
                       ["qt", "qs4", "qsq"], ["qt"])
                    tt("dve", qnb.rearrange("p (h d) -> p h d", d=128), q3[:, :, 0:128],
                       vec_t[:, V_QN:V_QN + 128].unsqueeze(1).to_broadcast([128, 8, 128]), ALU.mult, ["qt", "vec"], ["qnb"])
                    tt("dve", q3[:, :, 128:192], q3[:, :, 128:192], qs[:, 40:48].unsqueeze(2).to_broadcast([128, 8, 64]), ALU.mult,
                       ["qt", "qs5", "qsq"], ["qt"])
                    tt("dve", qrf.rearrange("p (h d) -> p h d", d=64), q3[:, :, 128:192],
                       vec_t[:, V_QR:V_QR + 64].unsqueeze(1).to_broadcast([128, 8, 64]), ALU.mult, ["qt", "vec"], ["qrf"])
                    f3 = qrf.rearrange("p (h d) -> p h d", d=64)
                    o3 = qrb.rearrange("p (h d) -> p h d", d=64)
                    x1 = f3[:, :, 0:32]; x2 = f3[:, :, 32:64]
                    Ct = cosT[:, t, :].unsqueeze(1).to_broadcast([128, 8, 32])
                    St = sinT[:, t, :].unsqueeze(1).to_broadcast([128, 8, 32])
                    rv = [r.rearrange("p (h d) -> p h d", d=32) for r in r4]
                    tt("pool", rv[0], x1, Ct, ALU.mult, ["qrf", "tab1", "qs0", "qs1"], ["r40"])
                    tt("pool", rv[1], x2, St, ALU.mult, ["qrf", "tab0", "qs0", "qs1"], ["r41"])
                    tt("pool", o3[:, :, 0:32], rv[0], rv[1], ALU.subtract, ["r40", "r41"], ["qrb.a"])
                    tt("dve", rv[2], x2, Ct, ALU.mult, ["qrf", "tab1", "qs0", "qs1"], ["r42"])
                    tt("dve", rv[3], x1, St, ALU.mult, ["qrf", "tab0", "qs0", "qs1"], ["r43"])
                    tt("dve", o3[:, :, 32:64], rv[2], rv[3], ALU.add, ["r42", "r43"], ["qrb.b"])
                    for hg in range(2):
                        for j in range(4):
                            h = hg * 4 + j
                            tr(psb(4)[:, j * 128:(j + 1) * 128], qnb[:, h * 128:(h + 1) * 128], ["qnb", "ident"], ["ps4"])
                        evac(QnT[:, hg * 4:hg * 4 + 4, ti * 128:(ti + 1) * 128], psb(4)[:, 0:512].rearrange("p (a b) -> p a b", b=128),
                             ["ps4"], ["QnT"])
                    for h in range(8):
                        tr(psb(5)[0:64, h * 128:(h + 1) * 128], qrb[:, h * 64:(h + 1) * 64], ["qrb.a", "qrb.b", "ident"], ["ps5"])
                    evac(QrT[0:64, :, ti * 128:(ti + 1) * 128], psb(5)[0:64, 0:1024].rearrange("p (a b) -> p a b", b=128), ["ps5"], ["QrT"])
                its = [(h, rk, kc) for h in range(8) for rk in range(4) for kc in range(VC)]
                segs = [(h, rk) for h in range(8) for rk in range(4)]

                def load_seg(si):
                    h, rk = segs[si]
                    s = si % 3
                    dma("sp", kTr[s], gout_rows(rk, K_OFF + h * 128, 128), [gres(K_OFF + h * 128)], ["kTr%d" % s], "lk%d" % s)
                    dma("sp", Vr[s], gout_rows(rk, V_OFF + h * 128, 128).rearrange("p (a b) -> p a b", b=128),
                        [gres(V_OFF + h * 128)], ["Vr%d" % s], "lv%d" % s)

                SB = (0, 1, 2, 5)

                def s_mm(ii):
                    h, rk, kc = its[ii]
                    si = h * 4 + rk
                    s = si % 3
                    bank = SB[ii % 4]
                    mm(ps[bank][:, :], kTr[s][:, kc * 128:(kc + 1) * 128], QnT[:, h, :], True, False, ["kTr%d" % s, "QnT"], ["ps%d" % bank])
                    mm(ps[bank][:, :], krT_all[:, rk * TOK + kc * 128:rk * TOK + (kc + 1) * 128], QrT[:, h, :], False, True,
                       ["krT.%d" % rk, "krT.pad", "QrT", "QrT.pad"], ["ps%d" % bank])

                load_seg(0); load_seg(1)
                nit = len(its)
                s_mm(0); s_mm(1)
                for ii in range(nit):
                    h, rk, kc = its[ii]
                    si = h * 4 + rk
                    s = si % 3
                    if kc == 0:
                        if si + 2 < len(segs):
                            load_seg(si + 2)
                        if rk == 0:
                            z = h % 2
                            dma("sp", zc[z], zct_d[h * 128:(h + 1) * 128, qb * 512:(qb + 1) * 512], [], ["zc%d" % z], "lz%d" % z)
                    if ii + 2 < nit:
                        s_mm(ii + 2)
                    bank = SB[ii % 4]
                    pi_ = ii % 8
                    act(pT[pi_], ps[bank][:, :], AF.Exp, ["ps%d" % bank], ["pT%d" % pi_], scale=SC)
                    first = (rk == 0 and kc == 0); last = (rk == 3 and kc == VC - 1)
                    mm(ps[3][:, :], Vr[s][:, kc, :], pT[pi_], first, last, ["Vr%d" % s, "pT%d" % pi_], ["ps3"])
                    if ii % 4 == 3:
                        gfirst = (rk == 0 and kc == 3); glast = last
                        for j in range(4):
                            pj = (ii - 3 + j) % 8
                            P.op("pe", lambda e, j=j, pj=pj, gfirst=gfirst, glast=glast:
                                 e.matmul(ps[6][32 * j:32 * j + 32, :], lhsT=ones_b[:, 0:32], rhs=pT[pj], start=gfirst, stop=glast,
                                          tile_position=(0, 32 * j)),
                                 r=["ones_b", "pT%d" % pj], w=["ps6"])
                    if last:
                        z = h % 2
                        cp("dve", d4, ps[6][:, :], ["ps6"], ["d4"])
                        mm(ps[4][:, :], ones_f[:], d4, True, True, ["ones_f", "d4"], ["ps4"])
                        P.op("dve", lambda e: e.reciprocal(out=rden, in_=ps[4][:, :]), r=["ps4"], w=["rden"])
                        stt(yc[:, h, :], ps[3][:, :], 32.0, rden, ALU.mult, ALU.mult, ["ps3", "rden"], ["yc%d" % h])
                        tt("pool", yc[:, h, :], yc[:, h, :], zc[z], ALU.mult, ["yc%d" % h, "zc%d" % z], ["yc%d" % h])
                        tt("pool", ysq, yc[:, h, :], yc[:, h, :], ALU.mult, ["yc%d" % h], ["rden"])
                        mm(ps[7][:, :], ones_f[:], ysq, h == 0, h == 7, ["ones_f", "rden"], ["ps7"])
                ts("dve", rc, ps[7][:, :], 1.0 / 1024, EPS, ALU.mult, ALU.add, ["ps7"], ["qsq", "r40", "r41", "r42", "r43"])
                act(rc2, rc, AF.Sqrt, ["qsq"], ["qsq"])
                P.op("dve", lambda e: e.reciprocal(out=rc2, in_=rc2), r=["qsq"], w=["qsq"])
                for h in range(8):
                    stt(A[:, 8 + h, qb * 512:(qb + 1) * 512], yc[:, h, :], cols_t[:, h:h + 1], rc2, ALU.mult, ALU.mult,
                        ["yc%d" % h, "cols", "qsq"], ["A.%d.%d" % (4 * qb + i, 2 + h // 4) for i in range(4)])
            barrier()

            if KSTOP <= 7:
                break
            wslots = [ph.bf16(16 * 512).rearrange("p (a b) -> p a b", b=512) for _ in range(2)]
            hsl = [ph.f32(512) for _ in range(4)]
            hi = [0]
            for g in range(4):
                sl = g % 2

                def epi(t, pap, pres, g=g):
                    s = hi[0]; hi[0] = (s + 1) % 4
                    dma("act", hsl[s], hsrc[t * 128:(t + 1) * 128, g * 512:(g + 1) * 512], [], ["hsl%d" % s], "lh%d" % s)
                    tt("dve", hsl[s], pap, hsl[s], ALU.add, [pres, "hsl%d" % s], ["hsl%d" % s])
                    dma("sp", hbuf_d[t * 128:(t + 1) * 128, g * 512:(g + 1) * 512], hsl[s], ["hsl%d" % s], [], "sh%d" % s)

                gemm(wout_d[l, :, g * 512:(g + 1) * 512], 16, 512, wslots[sl], "wslot%d" % sl, A, a_res, epi)
            barrier()

            if KSTOP <= 8:
                break
            norm_transpose(l, hbuf_d, 1)
            pTb = ph.bf16(2 * TOK).rearrange("p (a b) -> p a b", b=TOK)
            pf = [ph.f32(256) for _ in range(2)]
            pbf = [ph.bf16(256) for _ in range(2)]
            for t in range(NT):
                b = t % 2
                dma("sp", pf[b], p_d[l, t * 128:(t + 1) * 128, :], [], ["pf%d" % b], "lp%d" % b)
                cp("pool", pbf[b], pf[b], ["pf%d" % b], ["pbf%d" % b])
                for j in range(2):
                    tr(psb(6)[:, j * 128:(j + 1) * 128], pbf[b][:, j * 128:(j + 1) * 128], ["pbf%d" % b, "ident"], ["ps6"])
                evac(pTb[:, :, t * 128:(t + 1) * 128], psb(6)[:, 0:256].rearrange("p (a b) -> p a b", b=128), ["ps6"], ["pTb.%d" % t])
            wslots = [ph.bf16(16 * 512).rearrange("p (a b) -> p a b", b=512) for _ in range(2)]
            wpp = ph.bf16(2 * D).rearrange("p (a b) -> p a b", b=D)
            P.op("pool", lambda e, l=l: e.dma_start(out=wpp, in_=wpp_d[l, :, :].rearrange("(kc p) n -> p kc n", p=128)), r=[], w=["wpp"], stream="w.wpp")
            hsl = [ph.f32(512) for _ in range(4)]
            gt = [ph.f32(512) for _ in range(2)]
            gi_ = [0]
            for g in range(4):
                sl = g % 2

                def epi(t, pap, pres, g=g):
                    s = hi[0]; hi[0] = (s + 1) % 4
                    k = gi_[0]; gi_[0] ^= 1
                    pbank = 6 + k
                    for kc in range(2):
                        mm(ps[pbank][:, :], pTb[:, kc, t * 128:(t + 1) * 128], wpp[:, kc, g * 512:(g + 1) * 512], kc == 0, kc == 1,
                           ["pTb.%d" % t, "wpp"], ["ps%d" % pbank])
                    dma("act", hsl[s], hbuf_d[t * 128:(t + 1) * 128, g * 512:(g + 1) * 512], [], ["hsl%d" % s], "lh%d" % s)
                    act(gt[k], pap, AF.Sigmoid, [pres], ["gt%d" % k])
                    tt("dve", gt[k], gt[k], ps[pbank][:, :], ALU.mult, ["gt%d" % k, "ps%d" % pbank], ["gt%d" % k])
                    tt("dve", hsl[s], hsl[s], gt[k], ALU.add, ["hsl%d" % s, "gt%d" % k], ["hsl%d" % s])
                    dma("sp", hdst[t * 128:(t + 1) * 128, g * 512:(g + 1) * 512], hsl[s], ["hsl%d" % s], [], "sh%d" % s)

                gemm(wgate_d[l, :, g * 512:(g + 1) * 512], 16, 512, wslots[sl], "wslot%d" % sl, A, a_res, epi)
            barrier()

        streams = sorted(P.stream_last.keys())
        sems = {}
        for e in Prog.ENGS:
            sems[e] = es.enter_context(nc.semaphore("sem_" + e))
        ssem = {}
        for i, s in enumerate(streams):
            ssem[s] = es.enter_context(nc.semaphore("ds%d" % i))
        P.emit(nc, sems, ssem)
    return nc


_CACHE = {}


def _host_inputs(L, TOK, x, p, positions, attn_norm, w_in, sgu_norm, w_spatial, b_spatial, conv_w, conv_b,
                 kv_norm, w_ukv, q_nope_norm, q_rope_norm, k_nope_norm, k_rope_norm,
                 out_norm, w_out, ple_norm, w_ple_gate, w_ple_proj):
    f = np.float32
    NT = TOK // 128
    a = lambda v: np.ascontiguousarray(np.asarray(v))
    x = a(x); p = a(p); positions = a(positions)
    w_in = a(w_in)[:L]; w_ukv = a(w_ukv)[:L]; w_out = a(w_out)[:L]; w_gate = a(w_ple_gate)[:L]; w_pp = a(w_ple_proj)[:L]
    w_spT = a(np.transpose(np.asarray(w_spatial)[:L], (0, 3, 1, 2)).reshape(L, 128, 512))
    out_norm = np.asarray(out_norm)[:L]
    vec = np.concatenate([out_norm[:, 0:1024], np.asarray(sgu_norm)[:L].reshape(L, 512), np.asarray(conv_w)[:L].reshape(L, 1536),
                          np.asarray(conv_b)[:L], np.asarray(kv_norm)[:L], np.asarray(q_nope_norm)[:L], np.asarray(q_rope_norm)[:L],
                          np.asarray(k_nope_norm)[:L], np.asarray(k_rope_norm)[:L]], axis=1).astype(f)
    assert vec.shape[1] == NV
    gn = a(np.stack([np.asarray(attn_norm)[:L], np.asarray(ple_norm)[:L]], axis=1).astype(f))
    gc = np.transpose(out_norm[:, 1024:2048].reshape(L, 8, 128), (0, 2, 1))
    bs = np.transpose(np.asarray(b_spatial)[:L], (0, 2, 1))
    cols = a(np.concatenate([gc, bs], axis=2).astype(f))
    cmat = np.zeros((128, 640), f)
    cmat[:, 0:128] = np.eye(128, dtype=f)
    for m in range(127):
        cmat[m, 128 + m + 1] = 1.0
        cmat[m + 1, 256 + m] = 1.0
    cmat[127, 384 + 0] = 1.0
    cmat[0, 512 + 127] = 1.0
    invf = (1.0 / (10000.0 ** (np.arange(0, 64, 2, dtype=f) / f(64)))).astype(f)
    invf = a(np.broadcast_to(invf[None, :], (128, 32)))
    shared = dict(w_in=w_in, w_ukv=w_ukv, w_out=w_out, w_gate=w_gate, w_pp=w_pp, w_spT=w_spT, vec=vec, gn=gn, cols=cols,
                  cmat=cmat, invf=invf)
    maps = []
    for c in range(8):
        b, r = c // 4, c % 4
        sel = np.zeros((8, 256), f)
        if r > 0:
            sel[2 * (r - 1) + 1, 0] = 1.0
        if r < 3:
            sel[2 * (r + 1), 128 + 127] = 1.0
        m = dict(shared)
        m["x"] = a(x[b, r * TOK:(r + 1) * TOK, :])
        m["p"] = a(p[:L, b, r * TOK:(r + 1) * TOK, :])
        m["pos"] = a(positions[b, r * TOK:(r + 1) * TOK].reshape(NT, 128).T.astype(np.int32))
        m["sel"] = sel
        maps.append(m)
    return maps


def run(L, TOK, inputs, trace=False):
    key = (L, TOK)
    if key not in _CACHE:
        _CACHE[key] = build(L, TOK)
    nc = _CACHE[key]
    maps = _host_inputs(L, TOK, **inputs)
    res = run_bass_kernel_spmd(nc, maps, core_ids=list(range(8)), trace=trace)
    S = 4 * TOK
    out = np.empty((2, S, D), np.float32)
    for c in range(8):
        b, r = c // 4, c % 4
        out[b, r * TOK:(r + 1) * TOK, :] = res.results[c]["out"]
    return out, res


def kernel(**inputs):
    out, _ = run(4, 2048, inputs)
    return out
```
